# Optimizing a Trainium2 kernel written in Bass

```python
import math
import jax, jax.numpy as jnp
from jax import lax
import numpy as np

D_MODEL = 1024
BATCH = 16
SEQ = 4096
DEPTH = 1

CHUNK = 64
Q_BLOCK = 128
D_MIX = D_MODEL
W_ATTN = D_MIX // 2
W_RWKV = D_MIX - W_ATTN
ATTN_HEAD_DIM = 64
ATTN_HEADS = W_ATTN // (2 * ATTN_HEAD_DIM)
RWKV_HEAD_DIM = 64
RWKV_HEADS = W_RWKV // RWKV_HEAD_DIM
DECAY_LORA = max(32, int(round(1.8 * D_MODEL ** 0.5 / 32)) * 32)
AAA_LORA = max(32, int(round(1.8 * D_MODEL ** 0.5 / 32)) * 32)
RMS_EPS = 1e-6
SUBLN_EPS = 1e-5
GN_EPS = 1e-5 * RWKV_HEAD_DIM
SPLITS = (W_ATTN, W_ATTN, W_ATTN, W_ATTN,
          W_RWKV, W_RWKV, W_RWKV, DECAY_LORA, AAA_LORA, W_RWKV)
D_IN = sum(SPLITS)
SHIFT_WIDTH = 3 * W_RWKV + DECAY_LORA + AAA_LORA

kernel_name = "hymba_diffattn_rwkv7_chunk_causal"


def lambda_init(layer_idx):
    return 0.8 - 0.6 * math.exp(-0.3 * layer_idx)


def rms_norm(x, g, eps):
    x32 = x.astype(jnp.float32)
    y = x32 * lax.rsqrt(jnp.mean(x32 * x32, axis=-1, keepdims=True) + eps)
    return (y * g.astype(jnp.float32)).astype(x.dtype)


def token_shift(z, mu):
    prev = jnp.pad(z, ((0, 0), (1, 0), (0, 0)))[:, :-1]
    return z + (prev - z) * mu


def diff_attention(q1, q2, k1, k2, v, lam):
    seq = q1.shape[2]
    scale = ATTN_HEAD_DIM ** -0.5
    outs = []
    for blk in range(seq // Q_BLOCK):
        lo = blk * Q_BLOCK
        hi = lo + Q_BLOCK
        q_chunk = (lo + jnp.arange(Q_BLOCK)) // CHUNK
        k_chunk = jnp.arange(hi) // CHUNK
        mask = k_chunk[None, :] <= q_chunk[:, None]

        def probs(q, k):
            s = jnp.einsum('bhqd,bhkd->bhqk', q[:, :, lo:hi], k[:, :, :hi]).astype(jnp.float32) * scale
            return jax.nn.softmax(jnp.where(mask, s, -jnp.inf), axis=-1)

        p = probs(q1, k1) - lam * probs(q2, k2)
        outs.append(jnp.einsum('bhqk,bhkd->bhqd', p.astype(v.dtype), v[:, :, :hi]))
    return jnp.concatenate(outs, axis=2)


def rwkv7_recurrence(r, w, k, v, kk, a):
    bsz, _, heads, n = r.shape

    def step(state, inp):
        r_t, w_t, k_t, v_t, kk_t, a_t = inp
        sa = jnp.einsum('bhvk,bhk->bhv', state, -kk_t)
        state = (state * w_t[:, :, None, :]
                 + sa[..., None] * (kk_t * a_t)[:, :, None, :]
                 + v_t[..., None] * k_t[:, :, None, :])
        y_t = jnp.einsum('bhvk,bhk->bhv', state, r_t)
        return state, y_t

    xs = tuple(t.transpose(1, 0, 2, 3) for t in (r, w, k, v, kk, a))
    init = jnp.zeros((bsz, heads, n, n), jnp.float32)
    _, ys = lax.scan(step, init, xs)
    return ys.transpose(1, 0, 2, 3)


def setup_inputs(seed: int = 0) -> dict:
    key = jax.random.key(seed)
    ks = jax.random.split(key, 24)
    f32 = jnp.float32
    n = lambda k, shape: jax.random.normal(k, shape, f32)
    return {
        "x": n(ks[0], (BATCH, SEQ, D_MODEL)),
        "norm_g": 1.0 + 0.02 * n(ks[1], (DEPTH, D_MODEL)),
        "w_in": n(ks[2], (DEPTH, D_MODEL, D_IN)) * D_MODEL ** -0.5,
        "lambda_q1": 0.1 * n(ks[3], (DEPTH, ATTN_HEAD_DIM)),
        "lambda_k1": 0.1 * n(ks[4], (DEPTH, ATTN_HEAD_DIM)),
        "lambda_q2": 0.1 * n(ks[5], (DEPTH, ATTN_HEAD_DIM)),
        "lambda_k2": 0.1 * n(ks[6], (DEPTH, ATTN_HEAD_DIM)),
        "subln_g": 1.0 + 0.02 * n(ks[7], (DEPTH, 2 * ATTN_HEAD_DIM)),
        "shift_mu": jax.random.uniform(ks[8], (DEPTH, SHIFT_WIDTH), f32),
        "w0": jax.random.uniform(ks[9], (DEPTH, W_RWKV), f32, -3.0, 0.0),
        "w2": 0.5 * n(ks[10], (DEPTH, DECAY_LORA, W_RWKV)) * DECAY_LORA ** -0.5,
        "a0": 0.1 * n(ks[11], (DEPTH, W_RWKV)),
        "a2": n(ks[12], (DEPTH, AAA_LORA, W_RWKV)) * AAA_LORA ** -0.5,
        "k_k": 0.85 + 0.05 * n(ks[13], (DEPTH, W_RWKV)),
        "k_a": 1.0 + 0.05 * n(ks[14], (DEPTH, W_RWKV)),
        "r_k": 0.1 * n(ks[15], (DEPTH, RWKV_HEADS, RWKV_HEAD_DIM)),
        "ln_x_g": 1.0 + 0.02 * n(ks[16], (DEPTH, W_RWKV)),
        "ln_x_b": 0.02 * n(ks[17], (DEPTH, W_RWKV)),
        "w_out": n(ks[18], (DEPTH, D_MIX, D_MODEL)) * D_MIX ** -0.5,
        "final_g": 1.0 + 0.02 * n(ks[19], (D_MODEL,)),
    }


def reference(x, norm_g, w_in, lambda_q1, lambda_k1, lambda_q2, lambda_k2, subln_g,
              shift_mu, w0, w2, a0, a2, k_k, k_a, r_k, ln_x_g, ln_x_b, w_out, final_g):
    f32 = jnp.float32
    bsz, seq, _ = x.shape
    a_idx = [W_ATTN, 2 * W_ATTN, 3 * W_ATTN]
    b_idx = [W_RWKV, 2 * W_RWKV, 3 * W_RWKV, 3 * W_RWKV + DECAY_LORA]
    off_b = 4 * W_ATTN
    hd_b = lambda t: t.reshape(bsz, seq, RWKV_HEADS, RWKV_HEAD_DIM)
    h = x
    for l in range(DEPTH):
        xn = rms_norm(h, norm_g[l], RMS_EPS)
        proj = xn @ w_in[l]

        q, k_a_in, v_a, g_a = jnp.split(proj[..., :off_b], a_idx, axis=-1)
        q = q.reshape(bsz, seq, ATTN_HEADS, 2, ATTN_HEAD_DIM).transpose(3, 0, 2, 1, 4)
        k_att = k_a_in.reshape(bsz, seq, ATTN_HEADS, 2, ATTN_HEAD_DIM).transpose(3, 0, 2, 1, 4)
        v_a = v_a.reshape(bsz, seq, ATTN_HEADS, 2 * ATTN_HEAD_DIM).transpose(0, 2, 1, 3)
        lam_init = lambda_init(l)
        lam = (jnp.exp(jnp.sum(lambda_q1[l].astype(f32) * lambda_k1[l].astype(f32)))
               - jnp.exp(jnp.sum(lambda_q2[l].astype(f32) * lambda_k2[l].astype(f32)))
               + lam_init)
        o_a = diff_attention(q[0], q[1], k_att[0], k_att[1], v_a, lam)
        o_a = rms_norm(o_a, subln_g[l], SUBLN_EPS) * (1.0 - lam_init)
        o_a = o_a.transpose(0, 2, 1, 3).reshape(bsz, seq, W_ATTN) * jax.nn.silu(g_a)

        rwkv_in = token_shift(proj[..., off_b:off_b + SHIFT_WIDTH], shift_mu[l])
        r, k_b, v_b, wd, ad = jnp.split(rwkv_in, b_idx, axis=-1)
        g_b = proj[..., off_b + SHIFT_WIDTH:]
        w_log = -jax.nn.softplus(-(w0[l] + jnp.tanh(wd) @ w2[l]).astype(f32)) - 0.5
        decay = jnp.exp(-jnp.exp(w_log))
        a = jax.nn.sigmoid((a0[l] + ad @ a2[l]).astype(f32))
        kk = hd_b((k_b * k_k[l]).astype(f32))
        kk = kk / jnp.maximum(jnp.linalg.norm(kk, axis=-1, keepdims=True), 1e-12)
        k_mod = k_b.astype(f32) * (1.0 + (a - 1.0) * k_a[l].astype(f32))
        r_h, k_h, v_h = hd_b(r.astype(f32)), hd_b(k_mod), hd_b(v_b.astype(f32))
        y = rwkv7_recurrence(r_h, hd_b(decay), k_h, v_h, kk, hd_b(a))
        mu = jnp.mean(y, axis=-1, keepdims=True)
        var = jnp.mean(jnp.square(y - mu), axis=-1, keepdims=True)
        y = ((y - mu) * lax.rsqrt(var + GN_EPS)).reshape(bsz, seq, W_RWKV)
        y = y * ln_x_g[l].astype(f32) + ln_x_b[l].astype(f32)
        bonus = jnp.sum(r_h * k_h * r_k[l].astype(f32), axis=-1, keepdims=True) * v_h
        o_b = (y + bonus.reshape(bsz, seq, W_RWKV)).astype(x.dtype) * jax.nn.silu(g_b)

        mix = jnp.concatenate([o_a, o_b], axis=-1)
        h = h + mix @ w_out[l]
    return rms_norm(h, final_g, RMS_EPS)
```

```python
import os
import numpy as np
import concourse.bass as bass
import concourse.mybir as mybir
from concourse.bass_utils import run_bass_kernel_spmd

F32 = mybir.dt.float32
BF16 = mybir.dt.bfloat16
AF = mybir.ActivationFunctionType
ALU = mybir.AluOpType
AX = mybir.AxisListType

D = 1024
DIN = 4224
ST = 256
CH = 64
C0 = 0.6065306597126334
RMS_EPS = 1e-6
SUBLN_EPS = 1e-5
GN_EPS = 1e-5 * 64
LAM_INIT = 0.2
BIG = 1 << 40

OFF_Q, OFF_K, OFF_VA, OFF_GA, OFF_WDAD, OFF_RKV, OFF_GB = 0, 512, 1024, 1536, 2048, 2176, 3712

PP_MU, PP_W0, PP_A0, PP_KK, PP_KA, PP_RK, PP_LNG, PP_LNB, PP_SUB, PP_NG, PP_N = 0, 13, 17, 21, 25, 29, 33, 37, 41, 42, 50


class Prog:
    def __init__(self):
        self.engs = {}
        self.acc = {}

    def add(self, name, sem, real=True):
        self.engs[name] = dict(sem=sem, count=0, thunks=[], waited={}, real=real)

    @staticmethod
    def _norm(r):
        if isinstance(r, str):
            return (r, 0, BIG)
        return r

    def op(self, eng, fn, reads=(), writes=(), dma=None):
        self.nops = getattr(self, 'nops', 0) + 1
        if self.nops > int(os.environ.get('KSTOP', '100000000')):
            return
        if os.environ.get('KLOG'):
            import inspect
            fr = inspect.stack()
            print('OP', self.nops, eng, [f.lineno for f in fr[1:4]], flush=True)
        E = self.engs[eng]
        deps = {}
        reads = [self._norm(r) for r in reads]
        writes = [self._norm(r) for r in writes]
        for (n, lo, hi) in reads:
            for (l2, h2, k, e, i) in self.acc.get(n, ()):
                if k == 'w' and l2 < hi and lo < h2:
                    deps[e] = max(deps.get(e, 0), i)
        for (n, lo, hi) in writes:
            for (l2, h2, k, e, i) in self.acc.get(n, ()):
                if l2 < hi and lo < h2:
                    deps[e] = max(deps.get(e, 0), i)
        for e, i in deps.items():
            if e == eng and eng == 'pe':
                continue
            if E['waited'].get(e, 0) >= i:
                continue
            sem = self.engs[e]['sem']
            if self.engs[e].get('unordered'):
                E['waited'][e] = BIG
                E['thunks'].append(lambda en, s=sem, d=self.engs[e]: en.wait_ge(s, d['count']))
            else:
                E['waited'][e] = i
                E['thunks'].append(lambda en, s=sem, v=i: en.wait_ge(s, v))
        ce = dma if dma else eng
        CE = self.engs[ce]
        inc = 16 if dma else 1
        CE['count'] += inc
        idx = CE['count']
        E['thunks'].append(lambda en, f=fn, s=CE['sem'], i=inc: f(en).then_inc(s, i))
        for (n, lo, hi) in writes:
            lst = self.acc.setdefault(n, [])
            lst[:] = [a for a in lst if not (lo <= a[0] and a[1] <= hi)]
            lst.append((lo, hi, 'w', ce, idx))
        for (n, lo, hi) in reads:
            lst = self.acc.setdefault(n, [])
            lst[:] = [a for a in lst if not (a[2] == 'r' and a[3] == ce and lo <= a[0] and a[1] <= hi)]
            lst.append((lo, hi, 'r', ce, idx))

    def final_wait(self, eng, names):
        E = self.engs[eng]
        for n in names:
            c = self.engs[n]['count']
            if c > 0:
                E['thunks'].append(lambda en, s=self.engs[n]['sem'], v=c: en.wait_ge(s, v))


def build(NB, S, dbg=False):
    NT = S // 128
    NST = S // ST
    nc = bass.Bass("TRN2", target_bir_lowering=False)
    dt = nc.dram_tensor
    x_d = dt("x", [NB * S, D], F32, kind="ExternalInput").ap()
    win_d = dt("w_in", [D, DIN], F32, kind="ExternalInput").ap()
    wout_d = dt("w_out", [D, D], F32, kind="ExternalInput").ap()
    pp_d = dt("pp", [128, PP_N], F32, kind="ExternalInput").ap()
    lam_d = dt("lam4", [128, 256], F32, kind="ExternalInput").ap()
    lw_d = dt("lw", [128, 512], F32, kind="ExternalInput").ap()
    fg_d = dt("fg", [128, D], F32, kind="ExternalInput").ap()
    cst_d = dt("cst", [128, 832], F32, kind="ExternalInput").ap()
    out_d = dt("out", [NB * S, D], F32, kind="ExternalOutput").ap()
    wbf_d = dt("wbf", [D, DIN], BF16, kind="ExternalOutput").ap()
    wobf_d = dt("wobf", [D, D], BF16, kind="ExternalOutput").ap()

    sb = nc.alloc_sbuf_tensor
    kTc = sb("kTc", [128, 4, 4096], BF16)
    Vc = sb("Vc", [128, 32, 512], BF16)
    wslot = [sb(f"wslot{i}", [128, 8, 384], BF16) for i in range(2)]
    pp = sb("pp_sb", [128, PP_N], F32)
    omu = sb("omu", [128, 13], F32)
    omka = sb("omka", [128, 4], F32)
    g08 = sb("g08", [128, 1], F32)
    neglam = sb("neglam", [128, 1], F32)
    lamt = sb("lamt", [128, 256], F32)
    lamw = sb("lamw", [128, 8], F32)
    lw32 = sb("lw32", [128, 512], F32)
    lwbf = sb("lwbf", [128, 512], BF16)
    fg = sb("fg_sb", [128, D], F32)
    cst = sb("cst_sb", [128, 832], F32)
    ident = cst[:, 0:128]
    blockones = cst[:, 128:256]
    maskA = cst[0:64, 256:384]
    maskC = cst[0:64, 384:448]
    scanmask = cst[:, 576:832]
    identbf = sb("identbf", [128, 128], BF16)
    onesbf = sb("onesbf", [128, 128], BF16)
    ones32 = sb("ones32", [128, 128], F32)
    xbuf = [sb(f"xbuf{i}", [128, D], F32) for i in range(2)]
    hres = [sb(f"hres{i}", [128, D], F32) for i in range(2)]
    stat = sb("stat", [128, 16], F32)
    xnT = sb("xnT", [128, 8, ST], BF16)
    qT = sb("qT", [128, 4, 2, ST], BF16)
    sga = sb("sga", [128, 4, ST], BF16)
    sgb = sb("sgb", [128, 4, ST], BF16)
    mixT = sb("mixT", [128, 8, ST], BF16)
    rkv = [sb(f"rkv{i}", [128, 3, ST], F32) for i in range(2)]
    wdad = sb("wdad", [128, ST], F32)
    linbf = sb("linbf", [128, 2, ST], BF16)
    tmpm = [sb(f"tmpm{i}", [128, ST + 2], F32) for i in range(2)]
    carry = sb("carry", [128, 16], F32)
    NTMP = 12
    tmp = [sb(f"tmp{i}", [128, ST], F32) for i in range(NTMP)]
    arT = sb("arT", [128, 4, 2, 4, 2, CH], BF16)
    bT = sb("bT", [128, 4, ST], BF16)
    ktT = sb("ktT", [128, 4, ST], BF16)
    bonus = sb("bonus", [128, 4, ST], F32)
    gam = sb("gam", [128, 4, 4], F32)
    yT = sb("yT", [128, 4, ST], F32)
    MNb = sb("MNb", [128, 8, 128], BF16)
    MNk = sb("MNk", [128, 8, 128], BF16)
    NQ = 6
    qbuf = [sb(f"qbuf{i}", [128, 8, 64], BF16) for i in range(NQ)]
    Tb = [sb(f"Tb{i}", [128, 8, 64], BF16) for i in range(2)]
    bkTF = sb("bkTF", [128, 1024], BF16)
    VTF = sb("VTF", [128, 512], BF16)
    Wsb = sb("Wsb", [128, 512], BF16)
    Usb = sb("Usb", [128, 512], BF16)
    S32 = sb("S32", [128, 512], F32)
    Sbf = [sb(f"Sbf{i}", [128, 512], BF16) for i in range(2)]
    tmpS = sb("tmpS", [128, 512], F32)
    Ytf = sb("Ytf", [128, 512], F32)
    Pt = [sb(f"Pt{i}", [128, 2, ST], BF16) for i in range(2)]
    att1 = sb("att1", [128, 2, ST], F32)
    att2 = sb("att2", [128, 2, ST], F32)

    ps = nc.alloc_psum_tensor
    B = [ps(f"B{i}", [128, 512], F32) for i in range(7)]
    PTB = ps("PTB", [128, 1024], BF16)

    P = Prog()
    import contextlib
    es = contextlib.ExitStack()
    with es:
        def sem(n):
            return es.enter_context(nc.semaphore(n))
        for e in ('pe', 'act', 'dve', 'pool', 'sync'):
            P.add(e, sem("s_" + e))
        slots = ['ld_misc', 'ld_x0', 'ld_x1', 'ld_w0', 'ld_w1', 'ld_h0', 'ld_h1', 'st_0', 'st_1', 'st_w0', 'st_w1', 'ld_s0', 'ld_s1']
        for s_ in slots:
            P.add(s_, sem(s_), real=False)
        P.engs['ld_misc']['unordered'] = True

        def ACT(out, in_, func, r, w, scale=1.0, bias=0.0, accum=None):
            if accum is None:
                P.op('act', lambda e: e.activation(out=out, in_=in_, func=func, scale=scale, bias=bias), r, w)
            else:
                P.op('act', lambda e: e.activation(out=out, in_=in_, func=func, scale=scale, bias=bias, accum_out=accum), r, w)

        def CP(eng, out, in_, r, w):
            if eng == 'act':
                P.op('act', lambda e: e.activation(out=out, in_=in_, func=AF.Copy), r, w)
            else:
                P.op(eng, lambda e: e.tensor_copy(out=out, in_=in_), r, w)

        def TT(eng, out, in0, in1, op, r, w):
            P.op(eng, lambda e: e.tensor_tensor(out=out, in0=in0, in1=in1, op=op), r, w)

        def TS(eng, out, in0, s1, s2, op0, op1, r, w):
            if s2 is None:
                P.op(eng, lambda e: e.tensor_scalar(out=out, in0=in0, scalar1=s1, scalar2=None, op0=op0), r, w)
            else:
                P.op(eng, lambda e: e.tensor_scalar(out=out, in0=in0, scalar1=s1, scalar2=s2, op0=op0, op1=op1), r, w)

        def STT(out, in0, scalar, in1, op0, op1, r, w):
            P.op('dve', lambda e: e.scalar_tensor_tensor(out=out, in0=in0, scalar=scalar, in1=in1, op0=op0, op1=op1), r, w)

        def RECIP(out, in_, r, w):
            P.op('dve', lambda e: e.reciprocal(out=out, in_=in_), r, w)

        pemode = [None]

        def _rnd(v):
            return 32 if v <= 32 else (64 if v <= 64 else 128)

        def pe_mode(lhsT):
            K_ = lhsT.shape[0]
            M_ = 1
            for d_ in lhsT.shape[1:]:
                M_ *= d_
            md = (_rnd(K_), _rnd(M_))
            if pemode[0] is not None and pemode[0] != md:
                E = P.engs['pe']
                if E['count'] > 0:
                    E['thunks'].append(lambda en, s=E['sem'], v=E['count']: en.wait_ge(s, v))
            pemode[0] = md

        def MM(out, lhsT, rhs, start, stop, r, w):
            pe_mode(lhsT)
            P.op('pe', lambda e: e.matmul(out, lhsT=lhsT, rhs=rhs, start=start, stop=stop), r, w)

        def TR(out, in_, idn, r, w):
            pe_mode(in_)
            P.op('pe', lambda e: e.transpose(out, in_, idn), r, w)

        def DMA(queue, slot, out, in_, r, w):
            P.op(queue, lambda e: e.dma_start(out=out, in_=in_), r, w, dma=slot)

        def MEMSET(eng, ap, val, w):
            P.op(eng, lambda e: e.memset(ap, val), (), w)

        DMA('sync', 'ld_misc', pp[:, :], pp_d[:, :], (), ['pp'])
        DMA('sync', 'ld_misc', lamt[:, :], lam_d[:, :], (), ['lamt'])
        DMA('sync', 'ld_misc', lw32[:, :], lw_d[:, :], (), ['lw32'])
        DMA('sync', 'ld_misc', fg[:, :], fg_d[:, :], (), ['fg'])
        DMA('sync', 'ld_misc', cst[:, :], cst_d[:, :], (), ['cst'])
        TS('dve', omu[:, :], pp[:, PP_MU:PP_MU + 13], -1.0, 1.0, ALU.mult, ALU.add, ['pp'], ['omu'])
        TS('dve', omka[:, :], pp[:, PP_KA:PP_KA + 4], -1.0, 1.0, ALU.mult, ALU.add, ['pp'], ['omka'])
        TS('dve', g08[:, :], pp[:, PP_SUB:PP_SUB + 1], 1.0 - LAM_INIT, None, ALU.mult, None, ['pp'], ['g08'])
        CP('dve', lwbf[:, :], lw32[:, :], ['lw32'], ['lwbf'])
        CP('dve', identbf[:, :], ident, ['cst'], ['identbf'])
        MEMSET('pool', onesbf[:, :], 1.0, ['onesbf'])
        MEMSET('pool', ones32[:, :], 1.0, ['ones32'])
        for t_, n_ in [(qT, 'qT'), (linbf, 'linbf'), (arT, 'arT'), (MNb, 'MNb'), (MNk, 'MNk'), (bkTF, 'bkTF'), (VTF, 'VTF'),
                       (Wsb, 'Wsb'), (Usb, 'Usb'), (Ytf, 'Ytf'), (Tb[0], 'Tb0'), (Tb[1], 'Tb1')] + [(qbuf[i], f'qbuf{i}') for i in range(NQ)]:
            nd = len(t_.shape)
            MEMSET('pool', t_[(slice(None),) * nd], 0.0, [n_])
        TT('dve', lamt[:, 0:64], lamt[:, 0:64], lamt[:, 64:128], ALU.mult, ['lamt'], ['lamt'])
        TT('dve', lamt[:, 128:192], lamt[:, 128:192], lamt[:, 192:256], ALU.mult, ['lamt'], ['lamt'])
        P.op('dve', lambda e: e.tensor_reduce(out=lamw[:, 0:1], in_=lamt[:, 0:64], axis=AX.X, op=ALU.add), ['lamt'], ['lamw'])
        P.op('dve', lambda e: e.tensor_reduce(out=lamw[:, 1:2], in_=lamt[:, 128:192], axis=AX.X, op=ALU.add), ['lamt'], ['lamw'])
        ACT(lamw[:, 2:4], lamw[:, 0:2], AF.Exp, ['lamw'], ['lamw2'])
        TT('dve', lamw[:, 4:5], lamw[:, 3:4], lamw[:, 2:3], ALU.subtract, ['lamw2'], ['lamw3'])
        TS('dve', neglam[:, :], lamw[:, 4:5], -LAM_INIT, None, ALU.add, None, ['lamw3'], ['neglam'])

        stg32 = [kTc[:, :, :].rearrange("p a b -> p (a b)").bitcast(F32), Vc[:, :, :].rearrange("p a b -> p (a b)").bitcast(F32)]
        stgbf = [kTc[:, :, :].rearrange("p a b -> p (a b)"), Vc[:, :, :].rearrange("p a b -> p (a b)")]
        stn = ['kTc', 'Vc']
        BO = 9216
        for c in range(8):
            s_ = c % 2
            src = stg32[s_]
            dst = stgbf[s_]
            DMA('sync', f'ld_s{s_}', src[:, 0:DIN], win_d[c * 128:(c + 1) * 128, :], (), [stn[s_]])
            ng = pp[:, PP_NG + c:PP_NG + c + 1]
            eng = 'dve' if c % 2 == 0 else 'pool'
            TS(eng, dst[:, BO:BO + 2048], src[:, 0:2048], ng, None, ALU.mult, None, [stn[s_], 'pp'], [stn[s_]])
            TS(eng, dst[:, BO + OFF_WDAD:BO + OFF_WDAD + 128], src[:, 3584:3712], ng, None, ALU.mult, None, [stn[s_], 'pp'], [stn[s_]])
            for j in range(3):
                outap = dst[:, BO + OFF_RKV:BO + OFF_RKV + 1536].rearrange("p (a b) -> p a b", b=384)[:, :, j * 128:(j + 1) * 128]
                inap = src[:, 2048 + j * 512:2048 + (j + 1) * 512].rearrange("p (a b) -> p a b", b=128)
                TS(eng, outap, inap, ng, None, ALU.mult, None, [stn[s_], 'pp'], [stn[s_]])
            TS(eng, dst[:, BO + OFF_GB:BO + OFF_GB + 512], src[:, 3712:4224], ng, None, ALU.mult, None, [stn[s_], 'pp'], [stn[s_]])
            DMA('sync', f'st_w{s_}', wbf_d[c * 128:(c + 1) * 128, :], dst[:, BO:BO + DIN], [stn[s_]], [('wbf', c, c + 1)])
        for c in range(8):
            s_ = c % 2
            src = stg32[s_]
            dst = stgbf[s_]
            DMA('sync', f'ld_s{s_}', src[:, 0:D], wout_d[c * 128:(c + 1) * 128, :], (), [stn[s_]])
            CP('dve' if c % 2 == 0 else 'pool', dst[:, BO:BO + D], src[:, 0:D], [stn[s_]], [stn[s_]])
            DMA('sync', f'st_w{s_}', wobf_d[c * 128:(c + 1) * 128, :], dst[:, BO:BO + D], [stn[s_]], [('wobf', c, c + 1)])

        wbf_v = wbf_d.rearrange("(c p) n -> p c n", p=128)
        wobf_v = wobf_d.rearrange("(c p) n -> p c n", p=128)

        wctr = [0]

        def load_w(view, off, ncols, rname):
            i = wctr[0] % 2
            wctr[0] += 1
            DMA('sync', f'ld_w{i}', wslot[i][:, :, 0:ncols], view[:, :, off:off + ncols], [rname], [f'wslot{i}'])
            return wslot[i], f'wslot{i}'

        tctr = [0]

        def T_():
            i = tctr[0] % NTMP
            tctr[0] += 1
            return tmp[i][:, :], f'tmp{i}'

        qctr = [0]

        def Q_():
            i = qctr[0] % NQ
            qctr[0] += 1
            return qbuf[i], f'qbuf{i}'

        stctr = [0]
        xctr = [0]
        pjctr = [0]

        def pj_slot():
            i = pjctr[0] % 4
            pjctr[0] += 1
            return B[2 + i][:, 0:256], f'B{2 + i}'

        def proj_ftile(ws, wsn, col):
            o, on = pj_slot()
            for c in range(8):
                MM(o, ws[:, c, col:col + 128], xnT[:, c, :], c == 0, c == 7, [wsn, 'xnT'], [on])
            return o, on

        for b in range(NB):
            MEMSET('pool', S32[:, :], 0.0, ['S32'])
            MEMSET('pool', Sbf[0][:, :], 0.0, ['Sbf0'])
            MEMSET('pool', carry[:, :], 0.0, ['carry'])
            sbi = 0
            for st in range(NST):
                row0 = b * S + st * ST
                t0 = st * ST
                for tt in range(2):
                    xi = xctr[0] % 2
                    xctr[0] += 1
                    xb = xbuf[xi]
                    xn_ = f'xbuf{xi}'
                    DMA('sync', f'ld_x{xi}', xb[:, :], x_d[row0 + tt * 128:row0 + (tt + 1) * 128, :], (), [xn_])
                    ACT(xbuf[1 - xi][:, :], xb[:, :], AF.Square, [xn_], [f'xbuf{1 - xi}', 'stat0'], accum=stat[:, 0:1])
                    ACT(stat[:, 1:2], stat[:, 0:1], AF.Sqrt, ['stat0'], ['stat1'], scale=1.0 / D, bias=RMS_EPS)
                    RECIP(stat[:, 2:3], stat[:, 1:2], ['stat1'], ['stat2'])
                    TS('dve', xb[:, :], xb[:, :], stat[:, 2:3], None, ALU.mult, None, [xn_, 'stat2'], [xn_])
                    for c in range(8):
                        bk = B[c // 4]
                        TR(bk[:, (c % 4) * 128:(c % 4) * 128 + 128], xb[:, c * 128:(c + 1) * 128], ident, [xn_, 'cst'], [f'B{c // 4}'])
                    for hb in range(2):
                        CP('act' if hb == 0 else 'dve', xnT[:, hb * 4:hb * 4 + 4, tt * 128:(tt + 1) * 128],
                           B[hb][:, :].rearrange("p (c t) -> p c t", t=128), [f'B{hb}'], ['xnT'])
                for half in range(2):
                    ws, wsn = load_w(wbf_v, OFF_K + half * 256, 256, 'wbf')
                    for f in range(2):
                        h = half * 2 + f
                        o, on = proj_ftile(ws, wsn, f * 128)
                        CP('act', kTc[:, h, t0:t0 + ST], o, [on], [('kTc', h * 4096 + t0, h * 4096 + t0 + ST)])
                for half in range(2):
                    ws, wsn = load_w(wbf_v, OFF_VA + half * 256, 256, 'wbf')
                    for tt in range(2):
                        bk = B[tt]
                        for c in range(8):
                            MM(bk[:, 0:256], xnT[:, c, tt * 128:(tt + 1) * 128], ws[:, c, 0:256], c == 0, c == 7, [wsn, 'xnT'], [f'B{tt}'])
                        kt = st * 2 + tt
                        CP('dve', Vc[:, kt, half * 256:half * 256 + 256], bk[:, 0:256], [f'B{tt}'], [('Vc', kt * 512, kt * 512 + 512)])
                for half in range(2):
                    ws, wsn = load_w(wbf_v, OFF_Q + half * 256, 256, 'wbf')
                    for f in range(2):
                        h = half * 2 + f
                        o, on = proj_ftile(ws, wsn, f * 128)
                        CP('act', qT[0:64, h, 0, :], o[0:64, :], [on], ['qT'])
                        CP('dve', qT[64:128, h, 1, :], o[64:128, :], [on], ['qT'])
                for half in range(2):
                    ws, wsn = load_w(wbf_v, OFF_GA + half * 256, 256, 'wbf')
                    for f in range(2):
                        h = half * 2 + f
                        o, on = proj_ftile(ws, wsn, f * 128)
                        ACT(sga[:, h, :], o, AF.Silu, [on], ['sga'])

                def shift_evac(o, on, mucol, dst, dstn):
                    tm = tmpm[stctr[0] % 2]
                    tmn = f'tmpm{stctr[0] % 2}'
                    stctr[0] += 1
                    ACT(tm[:, 1:ST + 1], o, AF.Copy, [on, 'pp'], [tmn], scale=pp[:, PP_MU + mucol:PP_MU + mucol + 1])
                    CP('pool', tm[:, 0:1], carry[:, mucol:mucol + 1], ['carry'], [tmn])
                    STT(dst, o, omu[:, mucol:mucol + 1], tm[:, 0:ST], ALU.mult, ALU.add, [on, tmn, 'omu'], [dstn])
                    CP('pool', carry[:, mucol:mucol + 1], tm[:, ST:ST + 1], [tmn], ['carry'])

                ws, wsn = load_w(wbf_v, OFF_WDAD, 128, 'wbf')
                o, on = proj_ftile(ws, wsn, 0)
                shift_evac(o, on, 12, wdad[:, :], 'wdad')
                ACT(linbf[0:64, 0, :], wdad[0:64, :], AF.Tanh, ['wdad'], ['linbf'])
                CP('dve', linbf[64:128, 1, :], wdad[64:128, :], ['wdad'], ['linbf'])

                for p in range(4):
                    ws, wsn = load_w(wbf_v, OFF_RKV + p * 384, 384, 'wbf')
                    rk = rkv[p % 2]
                    rkn = f'rkv{p % 2}'
                    for j in range(3):
                        o, on = proj_ftile(ws, wsn, j * 128)
                        shift_evac(o, on, j * 4 + p, rk[:, j, :], rkn)
                    rs, ks, vs = rk[:, 0, :], rk[:, 1, :], rk[:, 2, :]
                    MM(B[0][:, 0:256], lwbf[:, p * 128:(p + 1) * 128], linbf[:, 0, :], True, True, ['lwbf', 'linbf'], ['B0'])
                    MM(B[0][:, 256:512], lwbf[:, p * 128:(p + 1) * 128], linbf[:, 1, :], True, True, ['lwbf', 'linbf'], ['B0'])
                    sg, sgn = T_()
                    ACT(sg, B[0][:, 0:256], AF.Sigmoid, ['B0', 'pp'], [sgn], bias=pp[:, PP_W0 + p:PP_W0 + p + 1])
                    aT, aTn = T_()
                    ACT(aT, B[0][:, 256:512], AF.Sigmoid, ['B0', 'pp'], [aTn], bias=pp[:, PP_A0 + p:PP_A0 + p + 1])
                    cs, csn = T_()
                    P.op('dve', lambda e, cs=cs, sg=sg: e.tensor_tensor_scan(out=cs, data0=scanmask, data1=sg, initial=0.0, op0=ALU.mult, op1=ALU.add), [sgn, 'cst'], [csn])
                    csp, cspn = T_()
                    TT('pool', csp, cs, sg, ALU.subtract, [csn, sgn], [cspn])
                    Eg, Egn = T_()
                    ACT(Eg, cs, AF.Exp, [csn], [Egn], scale=-C0)
                    Egi, Egin = T_()
                    ACT(Egi, cs, AF.Exp, [csn], [Egin], scale=C0)
                    Egp, Egpn = T_()
                    ACT(Egp, csp, AF.Exp, [cspn], [Egpn], scale=-C0)
                    CP('pool', gam[:, p, :], Eg.rearrange("p (c t) -> p c t", t=CH)[:, :, CH - 1], [Egn], ['gam'])
                    kraw, krawn = T_()
                    TS('dve', kraw, ks, pp[:, PP_KK + p:PP_KK + p + 1], None, ALU.mult, None, [rkn, 'pp'], [krawn])
                    ksq, ksqn = T_()
                    TT('pool', ksq, kraw, kraw, ALU.mult, [krawn], [ksqn])
                    MM(B[1][:, 0:256], blockones, ksq, True, True, ['cst', ksqn], ['B1'])
                    TS('dve', ksq, B[1][:, 0:256], 1e-24, None, ALU.max, None, ['B1'], [ksqn])
                    ACT(ksq, ksq, AF.Sqrt, [ksqn], [ksqn])
                    RECIP(ksq, ksq, [ksqn], [ksqn])
                    kk, kkn = T_()
                    TT('dve', kk, kraw, ksq, ALU.mult, [krawn, ksqn], [kkn])
                    ka, kan = T_()
                    TS('dve', ka, aT, pp[:, PP_KA + p:PP_KA + p + 1], omka[:, p:p + 1], ALU.mult, ALU.add, [aTn, 'pp', 'omka'], [kan])
                    TT('dve', ka, ks, ka, ALU.mult, [rkn, kan], [kan])
                    for e_ in range(2):
                        PR_ = slice(64 * e_, 64 * e_ + 64)
                        TT('dve' if e_ == 0 else 'pool', arT[PR_, p, e_, :, 1, :], rs[PR_, :].rearrange("p (c t) -> p c t", t=CH),
                           Eg[PR_, :].rearrange("p (c t) -> p c t", t=CH), ALU.mult, [rkn, Egn], ['arT'])
                        STT(arT[PR_, p, e_, :, 0, :], kk[PR_, :].rearrange("p (c t) -> p c t", t=CH), -1.0,
                            Egp[PR_, :].rearrange("p (c t) -> p c t", t=CH), ALU.mult, ALU.mult, [kkn, Egpn], ['arT'])
                    TT('pool', kk, kk, aT, ALU.mult, [kkn, aTn], [kkn])
                    TT('dve', bT[:, p, :], kk, Egi, ALU.mult, [kkn, Egin], ['bT'])
                    TT('dve', ktT[:, p, :], ka, Egi, ALU.mult, [kan, Egin], ['ktT'])
                    STT(kraw, rs, pp[:, PP_RK + p:PP_RK + p + 1], ka, ALU.mult, ALU.mult, [rkn, kan, 'pp'], [krawn])
                    MM(B[1][:, 256:512], blockones, kraw, True, True, ['cst', krawn], ['B1'])
                    TT('dve', bonus[:, p, :], B[1][:, 256:512], vs, ALU.mult, ['B1', rkn], ['bonus'])
                    CP('pool', yT[:, p, :], vs, [rkn], ['yT'])
                for half in range(2):
                    ws, wsn = load_w(wbf_v, OFF_GB + half * 256, 256, 'wbf')
                    for f in range(2):
                        h = half * 2 + f
                        o, on = proj_ftile(ws, wsn, f * 128)
                        ACT(sgb[:, h, :], o, AF.Silu, [on], ['sgb'])

                for c in range(4):
                    cc = slice(c * CH, (c + 1) * CH)
                    for p in range(4):
                        TR(PTB[0:64, p * 128:(p + 1) * 128], bT[:, p, cc], identbf[:, :], ['bT', 'identbf'], ['PTB'])
                        TR(PTB[0:64, 512 + p * 128:512 + (p + 1) * 128], ktT[:, p, cc], identbf[:, :], ['ktT', 'identbf'], ['PTB'])
                    CP('act', bkTF[0:64, :], PTB[0:64, :], ['PTB'], ['bkTF'])
                    for p in range(4):
                        TR(B[6][0:64, p * 128:(p + 1) * 128], yT[:, p, cc], ident, [('yT', c * CH, (c + 1) * CH), 'cst'], ['B6'])
                    CP('dve', VTF[0:64, :], B[6][0:64, :], ['B6'], ['VTF'])
                    for h in range(8):
                        p, e = h // 2, h % 2
                        PR = slice(64 * e, 64 * e + 64)
                        ar2 = arT[:, p, e, c, :, :].rearrange("p a t -> p (a t)")
                        MM(B[h // 4][0:64, (h % 4) * 128:(h % 4) * 128 + 128], bT[:, p, cc], ar2, True, True, ['bT', 'arT'], [f'B{h // 4}'])
                    for h in range(8):
                        p, e = h // 2, h % 2
                        PR = slice(64 * e, 64 * e + 64)
                        ar2 = arT[:, p, e, c, :, :].rearrange("p a t -> p (a t)")
                        MM(B[2 + h // 4][0:64, (h % 4) * 128:(h % 4) * 128 + 128], ktT[:, p, cc], ar2, True, True, ['ktT', 'arT'], [f'B{2 + h // 4}'])
                    for h in range(8):
                        p, e = h // 2, h % 2
                        PR = slice(64 * e, 64 * e + 64)
                        MM(B[4][0:64, h * 64:(h + 1) * 64], arT[:, p, e, c, 0, :], bT[:, p, cc], True, True, ['bT', 'arT'], ['B4'])
                    mA = maskA.unsqueeze(1).broadcast_to([64, 4, 128])
                    for hf in range(2):
                        TT('dve', MNb[0:64, hf * 4:hf * 4 + 4, :], B[hf][0:64, :].rearrange("p (h t) -> p h t", t=128), mA, ALU.mult, [f'B{hf}', 'cst'], ['MNb'])
                        TT('dve', MNk[0:64, hf * 4:hf * 4 + 4, :], B[2 + hf][0:64, :].rearrange("p (h t) -> p h t", t=128), mA, ALU.mult, [f'B{2 + hf}', 'cst'], ['MNk'])
                    QT, QTn = Q_()
                    mC = maskC.unsqueeze(1).broadcast_to([64, 8, 64])
                    TT('dve', QT[0:64, :, :], B[4][0:64, :].rearrange("p (h t) -> p h t", t=64), mC, ALU.mult, ['B4', 'cst'], [QTn])
                    Qv = MNb[:, :, 0:64]
                    Qn = 'MNb'
                    Tc = Tb[0]
                    Tn = 'Tb0'
                    ti = 0
                    TT('pool', Tc[0:64, :, :], MNb[0:64, :, 0:64], ident[0:64, 0:64].unsqueeze(1).broadcast_to([64, 8, 64]), ALU.add, ['MNb', 'cst'], [Tn])
                    for lvl in range(1, 6):
                        need_q = lvl < 5
                        if need_q:
                            for h in range(8):
                                MM(B[5][0:64, h * 64:(h + 1) * 64], QT[:, h, :], Qv[:, h, :], True, True, [QTn, Qn], ['B5'])
                        for h in range(8):
                            MM(B[6][0:64, h * 64:(h + 1) * 64], Qv[:, h, :], QT[:, h, :], True, True, [QTn, Qn], ['B6'])
                        if need_q:
                            Q2, Q2n = Q_()
                            CP('act', Q2[0:64, :, :], B[5][0:64, :].rearrange("p (h t) -> p h t", t=64), ['B5'], [Q2n])
                        QT2, QT2n = Q_()
                        CP('act', QT2[0:64, :, :], B[6][0:64, :].rearrange("p (h t) -> p h t", t=64), ['B6'], [QT2n])
                        for h in range(8):
                            MM(B[4][0:64, h * 64:(h + 1) * 64], QT2[:, h, :], Tc[:, h, :], True, True, [QT2n, Tn], ['B4'])
                        ti ^= 1
                        Tnew, Tnn = Tb[ti], f'Tb{ti}'
                        TT('dve', Tnew[0:64, :, :], B[4][0:64, :].rearrange("p (h t) -> p h t", t=64), Tc[0:64, :, :], ALU.add, ['B4', Tn], [Tnn])
                        Tc, Tn = Tnew, Tnn
                        if need_q:
                            Qv, Qn = Q2[:, :, :], Q2n
                        QT, QTn = QT2, QT2n
                    So, Son = Sbf[sbi], f'Sbf{sbi}'
                    for h in range(8):
                        p, e = h // 2, h % 2
                        PR = slice(64 * e, 64 * e + 64)
                        MM(B[0][0:64, h * 64:(h + 1) * 64], arT[:, p, e, c, 0, :], So[:, p * 128 + e * 64:p * 128 + e * 64 + 64], True, False, ['arT', Son], ['B0'])
                        MM(B[0][0:64, h * 64:(h + 1) * 64], MNk[:, h, 0:64], VTF[:, h * 64:(h + 1) * 64], False, True, ['MNk', 'VTF'], ['B0'])
                    CP('act', Wsb[0:64, :], B[0][0:64, :], ['B0'], ['Wsb'])
                    for h in range(8):
                        MM(B[1][0:64, h * 64:(h + 1) * 64], Tc[:, h, :], Wsb[:, h * 64:(h + 1) * 64], True, True, [Tn, 'Wsb'], ['B1'])
                    CP('act', Usb[0:64, :], B[1][0:64, :], ['B1'], ['Usb'])
                    for h in range(8):
                        p, e = h // 2, h % 2
                        PR = slice(64 * e, 64 * e + 64)
                        o = B[3][0:64, h * 64:(h + 1) * 64]
                        MM(o, arT[:, p, e, c, 1, :], So[:, p * 128 + e * 64:p * 128 + e * 64 + 64], True, False, ['arT', Son], ['B3'])
                        MM(o, MNb[:, h, 64:128], Usb[:, h * 64:(h + 1) * 64], False, False, ['MNb', 'Usb'], ['B3'])
                        MM(o, MNk[:, h, 64:128], VTF[:, h * 64:(h + 1) * 64], False, True, ['MNk', 'VTF'], ['B3'])
                    CP('act', Ytf[0:64, :], B[3][0:64, :], ['B3'], ['Ytf'])
                    for p in range(4):
                        o = B[2][:, p * 128:(p + 1) * 128]
                        MM(o, bkTF[:, p * 128:(p + 1) * 128], Usb[:, p * 128:(p + 1) * 128], True, False, ['bkTF', 'Usb'], ['B2'])
                        MM(o, bkTF[:, 512 + p * 128:512 + (p + 1) * 128], VTF[:, p * 128:(p + 1) * 128], False, True, ['bkTF', 'VTF'], ['B2'])
                    TT('dve', tmpS[:, :], B[2][:, :], S32[:, :], ALU.add, ['B2', 'S32'], ['tmpS'])
                    gbc = gam[:, :, c:c + 1].broadcast_to([128, 4, 128])
                    tS3 = tmpS[:, :].rearrange("p (a b) -> p a b", b=128)
                    TT('dve', S32[:, :].rearrange("p (a b) -> p a b", b=128), tS3, gbc, ALU.mult, ['tmpS', 'gam'], ['S32'])
                    sbi ^= 1
                    TT('pool', Sbf[sbi][:, :].rearrange("p (a b) -> p a b", b=128), tS3, gbc, ALU.mult, ['tmpS', 'gam'], [f'Sbf{sbi}'])
                    for p in range(4):
                        TR(B[5][:, p * 128:(p + 1) * 128], Ytf[:, p * 128:(p + 1) * 128], ident, ['Ytf', 'cst'], ['B5'])
                    CP('dve', yT[:, :, cc], B[5][:, :].rearrange("p (a t) -> p a t", t=128)[:, :, 0:CH], ['B5'], [('yT', c * CH, (c + 1) * CH)])

                for p in range(4):
                    y = yT[:, p, :]
                    bk, bn = B[p % 2], f'B{p % 2}'
                    MM(bk[:, 0:256], blockones, y, True, True, ['cst', 'yT'], [bn])
                    sq, sqn = T_()
                    TT('pool', sq, y, y, ALU.mult, ['yT'], [sqn])
                    MM(bk[:, 256:512], blockones, sq, True, True, ['cst', sqn], [bn])
                    m, mn = T_()
                    TS('dve', m, bk[:, 0:256], 1.0 / 64, None, ALU.mult, None, [bn], [mn])
                    msq, msqn = T_()
                    TT('pool', msq, m, m, ALU.mult, [mn], [msqn])
                    STT(msq, bk[:, 256:512], 1.0 / 64, msq, ALU.mult, ALU.subtract, [bn, msqn], [msqn])
                    ACT(msq, msq, AF.Sqrt, [msqn], [msqn], bias=GN_EPS)
                    RECIP(msq, msq, [msqn], [msqn])
                    TT('dve', m, y, m, ALU.subtract, ['yT', mn], [mn])
                    TT('dve', m, m, msq, ALU.mult, [mn, msqn], [mn])
                    TS('dve', m, m, pp[:, PP_LNG + p:PP_LNG + p + 1], pp[:, PP_LNB + p:PP_LNB + p + 1], ALU.mult, ALU.add, [mn, 'pp'], [mn])
                    TT('pool', m, m, bonus[:, p, :], ALU.add, [mn, 'bonus'], [mn])
                    TT('dve', mixT[:, 4 + p, :], m, sgb[:, p, :], ALU.mult, [mn, 'sgb'], ['mixT'])

                nkt = 2 * st + 2
                pctr = 0
                for h in range(4):
                    for kt in range(nkt):
                        j = kt - 2 * st
                        col0 = 128 * max(j, 0)
                        bi = kt % 4
                        bk, bn = B[bi], f'B{bi}'
                        sview = bk[:, :].rearrange("p (m q) -> p m q", q=ST)[:, :, col0:ST]
                        kres = ('kTc', h * 4096 + kt * 128, h * 4096 + kt * 128 + 128)
                        if col0 == 0:
                            MM(bk[:, :], kTc[:, h, kt * 128:(kt + 1) * 128], qT[:, h, :, :].rearrange("p m q -> p (m q)"), True, True, [kres, 'qT'], [bn])
                        else:
                            for m_ in range(2):
                                MM(bk[:, m_ * ST + col0:(m_ + 1) * ST], kTc[:, h, kt * 128:(kt + 1) * 128], qT[:, h, m_, col0:ST], True, True, [kres, 'qT'], [bn])
                        pt = Pt[pctr % 2]
                        ptn = f'Pt{pctr % 2}'
                        pctr += 1
                        pview = pt[:, :, col0:ST]
                        ACT(pview, sview, AF.Exp, [bn], [ptn], scale=0.125)
                        if j >= 0:
                            MEMSET('pool', pt[64:128, :, col0:col0 + 64], 0.0, [ptn])
                        vv = Vc[:, kt, h * 128:(h + 1) * 128]
                        vres = ('Vc', kt * 512, kt * 512 + 512)
                        if col0 == 0:
                            p2 = pt[:, :, :].rearrange("p m q -> p (m q)")
                            MM(B[5][:, :], vv, p2, kt == 0, kt == nkt - 1, [vres, ptn], ['B5'])
                            MM(B[6][:, :], onesbf[:, :], p2, kt == 0, kt == nkt - 1, ['onesbf', ptn], ['B6'])
                        else:
                            for m_ in range(2):
                                MM(B[5][:, m_ * ST + col0:(m_ + 1) * ST], vv, pt[:, m_, col0:ST], False, kt == nkt - 1 and m_ == 1, [vres, ptn], ['B5'])
                            for m_ in range(2):
                                MM(B[6][:, m_ * ST + col0:(m_ + 1) * ST], onesbf[:, :], pt[:, m_, col0:ST], False, kt == nkt - 1 and m_ == 1, ['onesbf', ptn], ['B6'])
                    a1 = att1[:, :, :].rearrange("p m q -> p (m q)")
                    a2 = att2[:, :, :].rearrange("p m q -> p (m q)")
                    RECIP(a1, B[6][:, :], ['B6'], ['att1'])
                    TT('dve', a2, B[5][:, :], a1, ALU.mult, ['B5', 'att1'], ['att2'])
                    o_, on_ = T_()
                    STT(o_, att2[:, 1, :], neglam[:, 0:1], att2[:, 0, :], ALU.mult, ALU.add, ['att2', 'neglam'], [on_])
                    sq, sqn = T_()
                    TT('pool', sq, o_, o_, ALU.mult, [on_], [sqn])
                    MM(B[4][:, 0:256], ones32[:, :], sq, True, True, ['ones32', sqn], ['B4'])
                    ACT(sq, B[4][:, 0:256], AF.Sqrt, ['B4'], [sqn], scale=1.0 / 128, bias=SUBLN_EPS)
                    RECIP(sq, sq, [sqn], [sqn])
                    STT(o_, o_, g08[:, 0:1], sq, ALU.mult, ALU.mult, [on_, sqn, 'g08'], [on_])
                    TT('dve', mixT[:, h, :], o_, sga[:, h, :], ALU.mult, [on_, 'sga'], ['mixT'])

                for tt in range(2):
                    DMA('sync', f'ld_h{tt}', hres[tt][:, :], x_d[row0 + tt * 128:row0 + (tt + 1) * 128, :], (), [f'hres{tt}'])
                for n in range(4):
                    ws, wsn = load_w(wobf_v, n * 256, 256, 'wobf')
                    for tt in range(2):
                        bi = (n * 2 + tt) % 4
                        bk, bn = B[bi], f'B{bi}'
                        for f in range(8):
                            MM(bk[:, 0:256], mixT[:, f, tt * 128:(tt + 1) * 128], ws[:, f, 0:256], f == 0, f == 7, ['mixT', wsn], [bn])
                        TT('dve', hres[tt][:, n * 256:(n + 1) * 256], bk[:, 0:256], hres[tt][:, n * 256:(n + 1) * 256], ALU.add, [bn, f'hres{tt}'], [f'hres{tt}'])
                for tt in range(2):
                    xi = xctr[0] % 2
                    ACT(xbuf[xi][:, :], hres[tt][:, :], AF.Square, [f'hres{tt}'], [f'xbuf{xi}', 'stat4'], accum=stat[:, 4:5])
                    ACT(stat[:, 5:6], stat[:, 4:5], AF.Sqrt, ['stat4'], ['stat5'], scale=1.0 / D, bias=RMS_EPS)
                    RECIP(stat[:, 6:7], stat[:, 5:6], ['stat5'], ['stat6'])
                    STT(hres[tt][:, :], hres[tt][:, :], stat[:, 6:7], fg[:, :], ALU.mult, ALU.mult, [f'hres{tt}', 'stat6', 'fg'], [f'hres{tt}'])
                    DMA('pool', f'st_{tt}', out_d[row0 + tt * 128:row0 + (tt + 1) * 128, :], hres[tt][:, :], [f'hres{tt}'], ['out'])

        P.final_wait('pool', ['st_0', 'st_1'])

        with nc.Block() as block:
            @block.sync
            def _(e):
                for f in P.engs['sync']['thunks']:
                    f(e)

            @block.tensor
            def _(e):
                for f in P.engs['pe']['thunks']:
                    f(e)

            @block.scalar
            def _(e):
                for f in P.engs['act']['thunks']:
                    f(e)

            @block.vector
            def _(e):
                for f in P.engs['dve']['thunks']:
                    f(e)

            @block.gpsimd
            def _(e):
                for f in P.engs['pool']['thunks']:
                    f(e)
    return nc


def make_consts():
    cst = np.zeros((128, 832), np.float32)
    cst[:, 0:128] = np.eye(128, dtype=np.float32)
    cst[0:64, 128:192] = 1.0
    cst[64:128, 192:256] = 1.0
    j = np.arange(64)[:, None]
    t = np.arange(64)[None, :]
    cst[0:64, 256:320] = (j < t)
    cst[0:64, 320:384] = (j <= t)
    cst[0:64, 384:448] = (j > t)
    sm = np.ones(256, np.float32)
    sm[::64] = 0.0
    cst[:, 576:832] = sm[None, :]
    return cst


def prep_params(norm_g, lambda_q1, lambda_k1, lambda_q2, lambda_k2, subln_g, shift_mu, w0, w2, a0, a2,
                k_k, k_a, r_k, ln_x_g, ln_x_b, final_g):
    f = lambda v, n: np.ascontiguousarray(np.asarray(v, np.float32).reshape(n, 128).T)
    pp = np.zeros((128, PP_N), np.float32)
    pp[:, PP_MU:PP_MU + 13] = f(shift_mu[0], 13)
    pp[:, PP_W0:PP_W0 + 4] = f(w0[0], 4)
    pp[:, PP_A0:PP_A0 + 4] = f(a0[0], 4)
    pp[:, PP_KK:PP_KK + 4] = f(k_k[0], 4)
    pp[:, PP_KA:PP_KA + 4] = f(k_a[0], 4)
    pp[:, PP_RK:PP_RK + 4] = f(r_k[0].reshape(-1), 4)
    pp[:, PP_LNG:PP_LNG + 4] = f(ln_x_g[0], 4)
    pp[:, PP_LNB:PP_LNB + 4] = f(ln_x_b[0], 4)
    pp[:, PP_SUB] = np.asarray(subln_g[0], np.float32)
    pp[:, PP_NG:PP_NG + 8] = f(norm_g[0], 8)
    lam4 = np.concatenate([np.broadcast_to(np.asarray(v[0], np.float32)[None, :], (128, 64))
                           for v in (lambda_q1, lambda_k1, lambda_q2, lambda_k2)], axis=1)
    lw = np.concatenate([np.asarray(w2[0], np.float32), np.asarray(a2[0], np.float32)], axis=0)
    fg = np.broadcast_to(np.asarray(final_g, np.float32)[None, :], (128, D))
    return dict(pp=pp, lam4=np.ascontiguousarray(lam4), lw=np.ascontiguousarray(lw), fg=np.ascontiguousarray(fg),
                cst=make_consts())


_CACHE = {}


def run(x, w_in, w_out, small, n_cores):
    x = np.asarray(x, np.float32)
    Bt, S, _ = x.shape
    NB = Bt // n_cores
    key = (NB, S)
    if key not in _CACHE:
        _CACHE[key] = build(NB, S)
    nc = _CACHE[key]
    w_in2 = np.ascontiguousarray(np.asarray(w_in, np.float32)[0])
    w_out2 = np.ascontiguousarray(np.asarray(w_out, np.float32)[0])
    in_maps = []
    for c in range(n_cores):
        m = dict(small)
        m["x"] = np.ascontiguousarray(x[c * NB:(c + 1) * NB].reshape(NB * S, D))
        m["w_in"] = w_in2
        m["w_out"] = w_out2
        in_maps.append(m)
    res = run_bass_kernel_spmd(nc, in_maps, core_ids=list(range(n_cores)))
    outs = [np.asarray(r["out"]).reshape(NB, S, D) for r in res.results]
    return np.concatenate(outs, axis=0).astype(np.float32)


def kernel(x, norm_g, w_in, lambda_q1, lambda_k1, lambda_q2, lambda_k2, subln_g,
           shift_mu, w0, w2, a0, a2, k_k, k_a, r_k, ln_x_g, ln_x_b, w_out, final_g):
    small = prep_params(norm_g, lambda_q1, lambda_k1, lambda_q2, lambda_k2, subln_g, shift_mu, w0, w2, a0, a2,
                        k_k, k_a, r_k, ln_x_g, ln_x_b, final_g)
    return run(x, w_in, w_out, small, 8)
```

```python
import os
import numpy as np
import concourse.bass as bass
import concourse.mybir as mybir
from concourse.bass_utils import run_bass_kernel_spmd

F32 = mybir.dt.float32
BF16 = mybir.dt.bfloat16
AF = mybir.ActivationFunctionType
ALU = mybir.AluOpType
AX = mybir.AxisListType

D = 1024
DIN = 4224
ST = 256
CH = 64
C0 = 0.6065306597126334
RMS_EPS = 1e-6
SUBLN_EPS = 1e-5
GN_EPS = 1e-5 * 64
LAM_INIT = 0.2
BIG = 1 << 40

OFF_Q, OFF_K, OFF_VA, OFF_GA, OFF_WDAD, OFF_RKV, OFF_GB = 0, 512, 1024, 1536, 2048, 2176, 3712

PP_MU, PP_W0, PP_A0, PP_KK, PP_KA, PP_RK, PP_LNG, PP_LNB, PP_SUB, PP_NG, PP_N = 0, 13, 17, 21, 25, 29, 33, 37, 41, 42, 50


class Prog:
    def __init__(self):
        self.engs = {}
        self.acc = {}

    def add(self, name, sem, real=True):
        self.engs[name] = dict(sem=sem, count=0, thunks=[], waited={}, real=real)

    @staticmethod
    def _norm(r):
        if isinstance(r, str):
            return (r, 0, BIG)
        return r

    def op(self, eng, fn, reads=(), writes=(), dma=None):
        self.nops = getattr(self, 'nops', 0) + 1
        if self.nops > int(os.environ.get('KSTOP', '100000000')):
            return
        if os.environ.get('KLOG'):
            import inspect
            fr = inspect.stack()
            print('OP', self.nops, eng, [f.lineno for f in fr[1:4]], flush=True)
        E = self.engs[eng]
        deps = {}
        reads = [self._norm(r) for r in reads]
        writes = [self._norm(r) for r in writes]
        for (n, lo, hi) in reads:
            for (l2, h2, k, e, i) in self.acc.get(n, ()):
                if k == 'w' and l2 < hi and lo < h2:
                    deps[e] = max(deps.get(e, 0), i)
        for (n, lo, hi) in writes:
            for (l2, h2, k, e, i) in self.acc.get(n, ()):
                if l2 < hi and lo < h2:
                    deps[e] = max(deps.get(e, 0), i)
        for e, i in deps.items():
            if e == eng and eng == 'pe':
                continue
            if E['waited'].get(e, 0) >= i:
                continue
            sem = self.engs[e]['sem']
            if self.engs[e].get('unordered'):
                E['waited'][e] = BIG
                E['thunks'].append(lambda en, s=sem, d=self.engs[e]: en.wait_ge(s, d['count']))
            else:
                E['waited'][e] = i
                E['thunks'].append(lambda en, s=sem, v=i: en.wait_ge(s, v))
        ce = dma if dma else eng
        CE = self.engs[ce]
        inc = 16 if dma else 1
        CE['count'] += inc
        idx = CE['count']
        E['thunks'].append(lambda en, f=fn, s=CE['sem'], i=inc: f(en).then_inc(s, i))
        for (n, lo, hi) in writes:
            lst = self.acc.setdefault(n, [])
            lst[:] = [a for a in lst if not (lo <= a[0] and a[1] <= hi)]
            lst.append((lo, hi, 'w', ce, idx))
        for (n, lo, hi) in reads:
            lst = self.acc.setdefault(n, [])
            lst[:] = [a for a in lst if not (a[2] == 'r' and a[3] == ce and lo <= a[0] and a[1] <= hi)]
            lst.append((lo, hi, 'r', ce, idx))

    def final_wait(self, eng, names):
        E = self.engs[eng]
        for n in names:
            c = self.engs[n]['count']
            if c > 0:
                E['thunks'].append(lambda en, s=self.engs[n]['sem'], v=c: en.wait_ge(s, v))


def build(NB, S, dbg=False):
    NT = S // 128
    NST = S // ST
    nc = bass.Bass("TRN2", target_bir_lowering=False)
    dt = nc.dram_tensor
    x_d = dt("x", [NB * S, D], F32, kind="ExternalInput").ap()
    win_d = dt("w_in", [D, DIN], F32, kind="ExternalInput").ap()
    wout_d = dt("w_out", [D, D], F32, kind="ExternalInput").ap()
    pp_d = dt("pp", [128, PP_N], F32, kind="ExternalInput").ap()
    lam_d = dt("lam4", [128, 256], F32, kind="ExternalInput").ap()
    lw_d = dt("lw", [128, 512], F32, kind="ExternalInput").ap()
    fg_d = dt("fg", [128, D], F32, kind="ExternalInput").ap()
    cst_d = dt("cst", [128, 832], F32, kind="ExternalInput").ap()
    out_d = dt("out", [NB * S, D], F32, kind="ExternalOutput").ap()
    wbf_d = dt("wbf", [D, DIN], BF16, kind="ExternalOutput").ap()
    wobf_d = dt("wobf", [D, D], BF16, kind="ExternalOutput").ap()

    sb = nc.alloc_sbuf_tensor
    kTc = sb("kTc", [128, 4, 4096], BF16)
    Vc = sb("Vc", [128, 32, 512], BF16)
    wslot = [sb(f"wslot{i}", [128, 8, 256], BF16) for i in range(5)]
    pp = sb("pp_sb", [128, PP_N], F32)
    omu = sb("omu", [128, 13], F32)
    omka = sb("omka", [128, 4], F32)
    g08 = sb("g08", [128, 1], F32)
    neglam = sb("neglam", [128, 1], F32)
    lamt = sb("lamt", [128, 256], F32)
    lamw = sb("lamw", [128, 8], F32)
    lw32 = sb("lw32", [128, 512], F32)
    lwbf = sb("lwbf", [128, 512], BF16)
    fg = sb("fg_sb", [128, D], F32)
    cst = sb("cst_sb", [128, 832], F32)
    ident = cst[:, 0:128]
    blockones = cst[:, 128:256]
    maskA = cst[0:64, 256:384]
    maskC = cst[0:64, 384:448]
    scanmask = cst[:, 576:832]
    identbf = sb("identbf", [128, 128], BF16)
    onesbf = sb("onesbf", [128, 128], BF16)
    ones32 = sb("ones32", [128, 128], F32)
    xbuf = [sb(f"xbuf{i}", [128, D], F32) for i in range(2)]
    hres = [sb(f"hres{i}", [128, D], F32) for i in range(2)]
    stat = sb("stat", [128, 16], F32)
    xnT = sb("xnT", [128, 8, ST], BF16)
    qT = sb("qT", [128, 4, 2, ST], BF16)
    sga = sb("sga", [128, 4, ST], BF16)
    sgb = sb("sgb", [128, 4, ST], BF16)
    mixT = sb("mixT", [128, 8, ST], BF16)
    rkv = [sb(f"rkv{i}", [128, 3, ST], F32) for i in range(2)]
    wdad = sb("wdad", [128, ST], F32)
    linbf = sb("linbf", [128, 2, ST], BF16)
    tmpm = [sb(f"tmpm{i}", [128, ST + 2], F32) for i in range(2)]
    carry = sb("carry", [128, 16], F32)
    NTMP = 12
    tmp = [sb(f"tmp{i}", [128, ST], F32) for i in range(NTMP)]
    arT = sb("arT", [128, 4 * 2 * 4 * 2 * CH + CH], BF16)
    bT = sb("bT", [128, 4 * ST + CH], BF16)
    ktT = sb("ktT", [128, 4 * ST + CH], BF16)
    bonus = sb("bonus", [128, 4, ST], F32)
    gam = sb("gam", [128, 4, 4], F32)
    yT = sb("yT", [128, 4 * ST + CH], F32)
    MNb = sb("MNb", [128, 9, 128], BF16)
    MNk = sb("MNk", [128, 9, 128], BF16)
    NQ = 6
    qbuf = [sb(f"qbuf{i}", [128, 9, 64], BF16) for i in range(NQ)]
    Tb = [sb(f"Tb{i}", [128, 9, 64], BF16) for i in range(2)]
    bkTF = sb("bkTF", [128, 1024], BF16)
    VTF = sb("VTF", [128, 512], BF16)
    Wsb = sb("Wsb", [128, 512], BF16)
    Usb = sb("Usb", [128, 512], BF16)
    S32 = sb("S32", [128, 512], F32)
    Sbf = [sb(f"Sbf{i}", [128, 512], BF16) for i in range(2)]
    tmpS = sb("tmpS", [128, 512], F32)
    Ytf = sb("Ytf", [128, 512], F32)
    Pt = [sb(f"Pt{i}", [128, 2, ST], BF16) for i in range(2)]
    att1 = sb("att1", [128, 2, ST], F32)
    att2 = sb("att2", [128, 2, ST], F32)

    ps = nc.alloc_psum_tensor
    B = [ps(f"B{i}", [128, 512], F32) for i in range(7)]
    PTB = ps("PTB", [128, 1024], BF16)

    P = Prog()
    import contextlib
    es = contextlib.ExitStack()
    with es:
        def sem(n):
            return es.enter_context(nc.semaphore(n))
        for e in ('pe', 'act', 'dve', 'pool', 'sync'):
            P.add(e, sem("s_" + e))
        slots = ['ld_misc', 'ld_x0', 'ld_x1', 'ld_w0', 'ld_w1', 'ld_w2', 'ld_w3', 'ld_w4', 'ld_h0', 'ld_h1', 'st_0', 'st_1', 'st_w0', 'st_w1', 'ld_s0', 'ld_s1']
        for s_ in slots:
            P.add(s_, sem(s_), real=False)
        P.engs['ld_misc']['unordered'] = True

        def ACT(out, in_, func, r, w, scale=1.0, bias=0.0, accum=None):
            if accum is None:
                P.op('act', lambda e: e.activation(out=out, in_=in_, func=func, scale=scale, bias=bias), r, w)
            else:
                P.op('act', lambda e: e.activation(out=out, in_=in_, func=func, scale=scale, bias=bias, accum_out=accum), r, w)

        def CP(eng, out, in_, r, w):
            if eng == 'act':
                P.op('act', lambda e: e.activation(out=out, in_=in_, func=AF.Copy), r, w)
            else:
                P.op(eng, lambda e: e.tensor_copy(out=out, in_=in_), r, w)

        def TT(eng, out, in0, in1, op, r, w):
            P.op(eng, lambda e: e.tensor_tensor(out=out, in0=in0, in1=in1, op=op), r, w)

        def TS(eng, out, in0, s1, s2, op0, op1, r, w):
            if s2 is None:
                P.op(eng, lambda e: e.tensor_scalar(out=out, in0=in0, scalar1=s1, scalar2=None, op0=op0), r, w)
            else:
                P.op(eng, lambda e: e.tensor_scalar(out=out, in0=in0, scalar1=s1, scalar2=s2, op0=op0, op1=op1), r, w)

        def STT(out, in0, scalar, in1, op0, op1, r, w):
            P.op('dve', lambda e: e.scalar_tensor_tensor(out=out, in0=in0, scalar=scalar, in1=in1, op0=op0, op1=op1), r, w)

        def RECIP(out, in_, r, w):
            P.op('dve', lambda e: e.reciprocal(out=out, in_=in_), r, w)

        pemode = [None]

        def _rnd(v):
            return 32 if v <= 32 else (64 if v <= 64 else 128)

        def pe_mode(lhsT):
            K_ = lhsT.shape[0]
            M_ = 1
            for d_ in lhsT.shape[1:]:
                M_ *= d_
            md = (_rnd(K_), _rnd(M_))
            if pemode[0] is not None and pemode[0] != md:
                E = P.engs['pe']
                if E['count'] > 0:
                    E['thunks'].append(lambda en, s=E['sem'], v=E['count']: en.wait_ge(s, v))
            pemode[0] = md

        def MM(out, lhsT, rhs, start, stop, r, w):
            pe_mode(lhsT)
            P.op('pe', lambda e: e.matmul(out, lhsT=lhsT, rhs=rhs, start=start, stop=stop), r, w)

        def TR(out, in_, idn, r, w):
            pe_mode(in_)
            P.op('pe', lambda e: e.transpose(out, in_, idn), r, w)

        def DMA(queue, slot, out, in_, r, w):
            P.op(queue, lambda e: e.dma_start(out=out, in_=in_), r, w, dma=slot)

        def MEMSET(eng, ap, val, w):
            P.op(eng, lambda e: e.memset(ap, val), (), w)

        DMA('sync', 'ld_misc', pp[:, :], pp_d[:, :], (), ['pp'])
        DMA('sync', 'ld_misc', lamt[:, :], lam_d[:, :], (), ['lamt'])
        DMA('sync', 'ld_misc', lw32[:, :], lw_d[:, :], (), ['lw32'])
        DMA('sync', 'ld_misc', fg[:, :], fg_d[:, :], (), ['fg'])
        DMA('sync', 'ld_misc', cst[:, :], cst_d[:, :], (), ['cst'])
        TS('dve', omu[:, :], pp[:, PP_MU:PP_MU + 13], -1.0, 1.0, ALU.mult, ALU.add, ['pp'], ['omu'])
        TS('dve', omka[:, :], pp[:, PP_KA:PP_KA + 4], -1.0, 1.0, ALU.mult, ALU.add, ['pp'], ['omka'])
        TS('dve', g08[:, :], pp[:, PP_SUB:PP_SUB + 1], 1.0 - LAM_INIT, None, ALU.mult, None, ['pp'], ['g08'])
        CP('dve', lwbf[:, :], lw32[:, :], ['lw32'], ['lwbf'])
        CP('dve', identbf[:, :], ident, ['cst'], ['identbf'])
        MEMSET('pool', onesbf[:, :], 1.0, ['onesbf'])
        MEMSET('pool', ones32[:, :], 1.0, ['ones32'])
        for t_, n_ in [(qT, 'qT'), (linbf, 'linbf'), (arT, 'arT'), (MNb, 'MNb'), (MNk, 'MNk'), (bkTF, 'bkTF'), (VTF, 'VTF'),
                       (Wsb, 'Wsb'), (Usb, 'Usb'), (Ytf, 'Ytf'), (Tb[0], 'Tb0'), (Tb[1], 'Tb1'), (bT, 'bT'), (ktT, 'ktT'), (yT, 'yT')] + [(qbuf[i], f'qbuf{i}') for i in range(NQ)]:
            nd = len(t_.shape)
            MEMSET('pool', t_[(slice(None),) * nd], 0.0, [n_])
        TT('dve', lamt[:, 0:64], lamt[:, 0:64], lamt[:, 64:128], ALU.mult, ['lamt'], ['lamt'])
        TT('dve', lamt[:, 128:192], lamt[:, 128:192], lamt[:, 192:256], ALU.mult, ['lamt'], ['lamt'])
        P.op('dve', lambda e: e.tensor_reduce(out=lamw[:, 0:1], in_=lamt[:, 0:64], axis=AX.X, op=ALU.add), ['lamt'], ['lamw'])
        P.op('dve', lambda e: e.tensor_reduce(out=lamw[:, 1:2], in_=lamt[:, 128:192], axis=AX.X, op=ALU.add), ['lamt'], ['lamw'])
        ACT(lamw[:, 2:4], lamw[:, 0:2], AF.Exp, ['lamw'], ['lamw2'])
        TT('dve', lamw[:, 4:5], lamw[:, 3:4], lamw[:, 2:3], ALU.subtract, ['lamw2'], ['lamw3'])
        TS('dve', neglam[:, :], lamw[:, 4:5], -LAM_INIT, None, ALU.add, None, ['lamw3'], ['neglam'])

        stg32 = [kTc[:, :, :].rearrange("p a b -> p (a b)").bitcast(F32), Vc[:, :, :].rearrange("p a b -> p (a b)").bitcast(F32)]
        stgbf = [kTc[:, :, :].rearrange("p a b -> p (a b)"), Vc[:, :, :].rearrange("p a b -> p (a b)")]
        stn = ['kTc', 'Vc']
        BO = 9216
        for c in range(8):
            s_ = c % 2
            src = stg32[s_]
            dst = stgbf[s_]
            DMA('sync', f'ld_s{s_}', src[:, 0:DIN], win_d[c * 128:(c + 1) * 128, :], (), [stn[s_]])
            ng = pp[:, PP_NG + c:PP_NG + c + 1]
            eng = 'dve' if c % 2 == 0 else 'pool'
            TS(eng, dst[:, BO:BO + 2048], src[:, 0:2048], ng, None, ALU.mult, None, [stn[s_], 'pp'], [stn[s_]])
            TS(eng, dst[:, BO + OFF_WDAD:BO + OFF_WDAD + 128], src[:, 3584:3712], ng, None, ALU.mult, None, [stn[s_], 'pp'], [stn[s_]])
            for j in range(3):
                outap = dst[:, BO + OFF_RKV:BO + OFF_RKV + 1536].rearrange("p (a b) -> p a b", b=384)[:, :, j * 128:(j + 1) * 128]
                inap = src[:, 2048 + j * 512:2048 + (j + 1) * 512].rearrange("p (a b) -> p a b", b=128)
                TS(eng, outap, inap, ng, None, ALU.mult, None, [stn[s_], 'pp'], [stn[s_]])
            TS(eng, dst[:, BO + OFF_GB:BO + OFF_GB + 512], src[:, 3712:4224], ng, None, ALU.mult, None, [stn[s_], 'pp'], [stn[s_]])
            DMA('sync', f'st_w{s_}', wbf_d[c * 128:(c + 1) * 128, :], dst[:, BO:BO + DIN], [stn[s_]], [('wbf', c, c + 1)])
        for c in range(8):
            s_ = c % 2
            src = stg32[s_]
            dst = stgbf[s_]
            DMA('sync', f'ld_s{s_}', src[:, 0:D], wout_d[c * 128:(c + 1) * 128, :], (), [stn[s_]])
            CP('dve' if c % 2 == 0 else 'pool', dst[:, BO:BO + D], src[:, 0:D], [stn[s_]], [stn[s_]])
            DMA('sync', f'st_w{s_}', wobf_d[c * 128:(c + 1) * 128, :], dst[:, BO:BO + D], [stn[s_]], [('wobf', c, c + 1)])

        wbf_v = wbf_d.rearrange("(c p) n -> p c n", p=128)
        wobf_v = wobf_d.rearrange("(c p) n -> p c n", p=128)

        NS = 5
        wctr = [0]

        def load_w(view, off, ncols, rname):
            i = wctr[0] % NS
            wctr[0] += 1
            DMA('sync', f'ld_w{i}', wslot[i][:, :, 0:ncols], view[:, :, off:off + ncols], [rname], [f'wslot{i}'])
            return wslot[i], f'wslot{i}'

        tctr = [0]

        def T_():
            i = tctr[0] % NTMP
            tctr[0] += 1
            return tmp[i][:, :], f'tmp{i}'

        qctr = [0]

        def Q_():
            i = qctr[0] % NQ
            qctr[0] += 1
            return qbuf[i], f'qbuf{i}'

        stctr = [0]
        xctr = [0]
        rctr = {'A': 0, 'R': 0}
        BANKS = {'A': [0, 1, 2, 3], 'R': [4, 5, 6]}

        def bank(stream):
            bl = BANKS[stream]
            i = bl[rctr[stream] % len(bl)]
            rctr[stream] += 1
            return B[i], f'B{i}'

        def proj_ftile(ws, wsn, col, stream):
            bk, bn = bank(stream)
            o = bk[:, 0:ST]
            for c in range(8):
                MM(o, ws[:, c, col:col + 128], xnT[:, c, :], c == 0, c == 7, [wsn, 'xnT'], [bn])
            return o, bn

        def shift_evac(o, on, mucol, dst, dstn):
            tm = tmpm[stctr[0] % 2]
            tmn = f'tmpm{stctr[0] % 2}'
            stctr[0] += 1
            ACT(tm[:, 1:ST + 1], o, AF.Copy, [on, 'pp'], [tmn], scale=pp[:, PP_MU + mucol:PP_MU + mucol + 1])
            CP('pool', tm[:, 0:1], carry[:, mucol:mucol + 1], ['carry'], [tmn])
            STT(dst, o, omu[:, mucol:mucol + 1], tm[:, 0:ST], ALU.mult, ALU.add, [on, tmn, 'omu'], [dstn])
            CP('pool', carry[:, mucol:mucol + 1], tm[:, ST:ST + 1], [tmn], ['carry'])

        def bcol(p, lo, n):
            return bT[:, p * ST + lo:p * ST + lo + n]

        def kcol(p, lo, n):
            return ktT[:, p * ST + lo:p * ST + lo + n]

        def ycol(p, lo, n):
            return yT[:, p * ST + lo:p * ST + lo + n]

        def ar(p, e, c, a, n):
            off = (((p * 2 + e) * 4 + c) * 2 + a) * CH
            return arT[:, off:off + n]

        MNbf = MNb[:, :, :].rearrange("p h t -> p (h t)")
        MNkf = MNk[:, :, :].rearrange("p h t -> p (h t)")

        def merge(gens):
            prog = [0] * len(gens)
            alive = [True] * len(gens)
            while any(alive):
                best = None
                for i, (g, n) in enumerate(gens):
                    if alive[i]:
                        f = prog[i] / max(n, 1)
                        if best is None or f < best[0]:
                            best = (f, i)
                i = best[1]
                try:
                    next(gens[i][0])
                    prog[i] += 1
                except StopIteration:
                    alive[i] = False

        sbi = [0]
        for b in range(NB):
            MEMSET('pool', S32[:, :], 0.0, ['S32'])
            MEMSET('pool', Sbf[sbi[0]][:, :], 0.0, [f'Sbf{sbi[0]}'])
            MEMSET('pool', carry[:, :], 0.0, ['carry'])
            for st in range(NST):
                row0 = b * S + st * ST
                t0 = st * ST
                for tt in range(2):
                    xi = xctr[0] % 2
                    xctr[0] += 1
                    xb = xbuf[xi]
                    xn_ = f'xbuf{xi}'
                    DMA('sync', f'ld_x{xi}', xb[:, :], x_d[row0 + tt * 128:row0 + (tt + 1) * 128, :], (), [xn_])
                    ACT(xbuf[1 - xi][:, :], xb[:, :], AF.Square, [xn_], [f'xbuf{1 - xi}', 'stat0'], accum=stat[:, 0:1])
                    ACT(stat[:, 1:2], stat[:, 0:1], AF.Sqrt, ['stat0'], ['stat1'], scale=1.0 / D, bias=RMS_EPS)
                    RECIP(stat[:, 2:3], stat[:, 1:2], ['stat1'], ['stat2'])
                    TS('dve', xb[:, :], xb[:, :], stat[:, 2:3], None, ALU.mult, None, [xn_, 'stat2'], [xn_])
                    for c in range(8):
                        bk = B[c // 4]
                        TR(bk[:, (c % 4) * 128:(c % 4) * 128 + 128], xb[:, c * 128:(c + 1) * 128], ident, [xn_, 'cst'], [f'B{c // 4}'])
                    for hb in range(2):
                        CP('act' if hb == 0 else 'dve', xnT[:, hb * 4:hb * 4 + 4, tt * 128:(tt + 1) * 128],
                           B[hb][:, :].rearrange("p (c t) -> p c t", t=128), [f'B{hb}'], ['xnT'])
                for half in range(2):
                    ws, wsn = load_w(wbf_v, OFF_K + half * 256, 256, 'wbf')
                    for f in range(2):
                        h = half * 2 + f
                        o, on = proj_ftile(ws, wsn, f * 128, 'A')
                        CP('act', kTc[:, h, t0:t0 + ST], o, [on], [('kTc', h * 4096 + t0, h * 4096 + t0 + ST)])
                for half in range(2):
                    ws, wsn = load_w(wbf_v, OFF_VA + half * 256, 256, 'wbf')
                    for tt in range(2):
                        bk, bn = bank('A')
                        for c in range(8):
                            MM(bk[:, 0:256], xnT[:, c, tt * 128:(tt + 1) * 128], ws[:, c, 0:256], c == 0, c == 7, [wsn, 'xnT'], [bn])
                        kt = st * 2 + tt
                        CP('dve', Vc[:, kt, half * 256:half * 256 + 256], bk[:, 0:256], [bn], [('Vc', kt * 512, kt * 512 + 512)])
                for half in range(2):
                    ws, wsn = load_w(wbf_v, OFF_Q + half * 256, 256, 'wbf')
                    for f in range(2):
                        h = half * 2 + f
                        o, on = proj_ftile(ws, wsn, f * 128, 'A')
                        CP('act', qT[0:64, h, 0, :], o[0:64, :], [on], ['qT'])
                        CP('dve', qT[64:128, h, 1, :], o[64:128, :], [on], ['qT'])
                for half in range(2):
                    ws, wsn = load_w(wbf_v, OFF_GA + half * 256, 256, 'wbf')
                    for f in range(2):
                        h = half * 2 + f
                        o, on = proj_ftile(ws, wsn, f * 128, 'A')
                        ACT(sga[:, h, :], o, AF.Silu, [on], ['sga'])

                def rwkv_stream():
                    ws, wsn = load_w(wbf_v, OFF_WDAD, 128, 'wbf')
                    o, on = proj_ftile(ws, wsn, 0, 'R')
                    shift_evac(o, on, 12, wdad[:, :], 'wdad')
                    ACT(linbf[0:64, 0, :], wdad[0:64, :], AF.Tanh, ['wdad'], ['linbf'])
                    CP('dve', linbf[64:128, 1, :], wdad[64:128, :], ['wdad'], ['linbf'])
                    yield
                    for p in range(4):
                        rk = rkv[p % 2]
                        rkn = f'rkv{p % 2}'
                        ws, wsn = load_w(wbf_v, OFF_RKV + p * 384, 256, 'wbf')
                        for j in range(2):
                            o, on = proj_ftile(ws, wsn, j * 128, 'R')
                            shift_evac(o, on, j * 4 + p, rk[:, j, :], rkn)
                            yield
                        ws, wsn = load_w(wbf_v, OFF_RKV + p * 384 + 256, 128, 'wbf')
                        o, on = proj_ftile(ws, wsn, 0, 'R')
                        shift_evac(o, on, 2 * 4 + p, rk[:, 2, :], rkn)
                        yield
                        rs, ks, vs = rk[:, 0, :], rk[:, 1, :], rk[:, 2, :]
                        bk, bn = bank('R')
                        MM(bk[:, 0:256], lwbf[:, p * 128:(p + 1) * 128], linbf[:, 0, :], True, True, ['lwbf', 'linbf'], [bn])
                        MM(bk[:, 256:512], lwbf[:, p * 128:(p + 1) * 128], linbf[:, 1, :], True, True, ['lwbf', 'linbf'], [bn])
                        sg, sgn = T_()
                        ACT(sg, bk[:, 0:256], AF.Sigmoid, [bn, 'pp'], [sgn], bias=pp[:, PP_W0 + p:PP_W0 + p + 1])
                        aT, aTn = T_()
                        ACT(aT, bk[:, 256:512], AF.Sigmoid, [bn, 'pp'], [aTn], bias=pp[:, PP_A0 + p:PP_A0 + p + 1])
                        cs, csn = T_()
                        P.op('dve', lambda e, cs=cs, sg=sg: e.tensor_tensor_scan(out=cs, data0=scanmask, data1=sg, initial=0.0, op0=ALU.mult, op1=ALU.add), [sgn, 'cst'], [csn])
                        csp, cspn = T_()
                        TT('pool', csp, cs, sg, ALU.subtract, [csn, sgn], [cspn])
                        Eg, Egn = T_()
                        ACT(Eg, cs, AF.Exp, [csn], [Egn], scale=-C0)
                        Egi, Egin = T_()
                        ACT(Egi, cs, AF.Exp, [csn], [Egin], scale=C0)
                        Egp, Egpn = T_()
                        ACT(Egp, csp, AF.Exp, [cspn], [Egpn], scale=-C0)
                        CP('pool', gam[:, p, :], Eg.rearrange("p (c t) -> p c t", t=CH)[:, :, CH - 1], [Egn], ['gam'])
                        yield
                        kraw, krawn = T_()
                        TS('dve', kraw, ks, pp[:, PP_KK + p:PP_KK + p + 1], None, ALU.mult, None, [rkn, 'pp'], [krawn])
                        ksq, ksqn = T_()
                        TT('pool', ksq, kraw, kraw, ALU.mult, [krawn], [ksqn])
                        bk2, bn2 = bank('R')
                        MM(bk2[:, 0:256], blockones, ksq, True, True, ['cst', ksqn], [bn2])
                        TS('dve', ksq, bk2[:, 0:256], 1e-24, None, ALU.max, None, [bn2], [ksqn])
                        ACT(ksq, ksq, AF.Sqrt, [ksqn], [ksqn])
                        RECIP(ksq, ksq, [ksqn], [ksqn])
                        kk, kkn = T_()
                        TT('dve', kk, kraw, ksq, ALU.mult, [krawn, ksqn], [kkn])
                        ka, kan = T_()
                        TS('dve', ka, aT, pp[:, PP_KA + p:PP_KA + p + 1], omka[:, p:p + 1], ALU.mult, ALU.add, [aTn, 'pp', 'omka'], [kan])
                        TT('dve', ka, ks, ka, ALU.mult, [rkn, kan], [kan])
                        yield
                        for e_ in range(2):
                            PR_ = slice(64 * e_, 64 * e_ + 64)
                            base = (p * 2 + e_) * 512
                            av = arT[PR_, base:base + 512].rearrange("p (c a t) -> p c a t", c=4, a=2)
                            TT('dve' if e_ == 0 else 'pool', av[:, :, 1, :], rs[PR_, :].rearrange("p (c t) -> p c t", t=CH),
                               Eg[PR_, :].rearrange("p (c t) -> p c t", t=CH), ALU.mult, [rkn, Egn], ['arT'])
                            STT(av[:, :, 0, :], kk[PR_, :].rearrange("p (c t) -> p c t", t=CH), -1.0,
                                Egp[PR_, :].rearrange("p (c t) -> p c t", t=CH), ALU.mult, ALU.mult, [kkn, Egpn], ['arT'])
                        TT('pool', kk, kk, aT, ALU.mult, [kkn, aTn], [kkn])
                        TT('dve', bcol(p, 0, ST), kk, Egi, ALU.mult, [kkn, Egin], ['bT'])
                        TT('dve', kcol(p, 0, ST), ka, Egi, ALU.mult, [kan, Egin], ['ktT'])
                        STT(kraw, rs, pp[:, PP_RK + p:PP_RK + p + 1], ka, ALU.mult, ALU.mult, [rkn, kan, 'pp'], [krawn])
                        bk3, bn3 = bank('R')
                        MM(bk3[:, 0:256], blockones, kraw, True, True, ['cst', krawn], [bn3])
                        TT('dve', bonus[:, p, :], bk3[:, 0:256], vs, ALU.mult, [bn3, rkn], ['bonus'])
                        CP('pool', ycol(p, 0, ST), vs, [rkn], ['yT'])
                        yield
                    for half in range(2):
                        ws, wsn = load_w(wbf_v, OFF_GB + half * 256, 256, 'wbf')
                        for f in range(2):
                            h = half * 2 + f
                            o, on = proj_ftile(ws, wsn, f * 128, 'R')
                            ACT(sgb[:, h, :], o, AF.Silu, [on], ['sgb'])
                            yield
                    for c in range(0 if os.environ.get('KNOCHUNK') else 4):
                        c0 = c * CH
                        cres = ('yT', c0, c0 + CH)
                        for p in range(4):
                            TR(PTB[:, p * 128:(p + 1) * 128], bcol(p, c0, 128), identbf[:, :], ['bT', 'identbf'], ['PTB'])
                            TR(PTB[:, 512 + p * 128:512 + (p + 1) * 128], kcol(p, c0, 128), identbf[:, :], ['ktT', 'identbf'], ['PTB'])
                        CP('act', bkTF[0:64, :], PTB[0:64, :], ['PTB'], ['bkTF'])
                        for p in range(4):
                            TR(B[4][:, p * 128:(p + 1) * 128], ycol(p, c0, 128), ident, [cres, 'cst'], ['B4'])
                        CP('dve', VTF[0:64, :], B[4][0:64, :], ['B4'], ['VTF'])
                        yield
                        mA = maskA.unsqueeze(1).broadcast_to([64, 4, 128])
                        for h in range(8):
                            p, e = h // 2, h % 2
                            MM(B[5 + h // 4][:, (h % 4) * 128:(h % 4) * 128 + 128], bcol(p, c0, 128), ar(p, e, c, 0, 128), True, True, ['bT', 'arT'], [f'B{5 + h // 4}'])
                        for hf in range(2):
                            TT('dve', MNb[0:64, hf * 4:hf * 4 + 4, :], B[5 + hf][0:64, :].rearrange("p (h t) -> p h t", t=128), mA, ALU.mult, [f'B{5 + hf}', 'cst'], ['MNb'])
                        for h in range(8):
                            p, e = h // 2, h % 2
                            MM(B[4][:, h * 64:(h + 1) * 64], ar(p, e, c, 0, 128), bcol(p, c0, 64), True, True, ['bT', 'arT'], ['B4'])
                        QT, QTn = Q_()
                        mC = maskC.unsqueeze(1).broadcast_to([64, 8, 64])
                        TT('dve', QT[0:64, 0:8, :], B[4][0:64, :].rearrange("p (h t) -> p h t", t=64), mC, ALU.mult, ['B4', 'cst'], [QTn])
                        yield
                        for h in range(8):
                            p, e = h // 2, h % 2
                            MM(B[5 + h // 4][:, (h % 4) * 128:(h % 4) * 128 + 128], kcol(p, c0, 128), ar(p, e, c, 0, 128), True, True, ['ktT', 'arT'], [f'B{5 + h // 4}'])
                        for hf in range(2):
                            TT('dve', MNk[0:64, hf * 4:hf * 4 + 4, :], B[5 + hf][0:64, :].rearrange("p (h t) -> p h t", t=128), mA, ALU.mult, [f'B{5 + hf}', 'cst'], ['MNk'])
                        ti = 0
                        Tc, Tn = Tb[0], 'Tb0'
                        TT('pool', Tc[0:64, 0:8, :], MNb[0:64, 0:8, 0:64], ident[0:64, 0:64].unsqueeze(1).broadcast_to([64, 8, 64]), ALU.add, ['MNb', 'cst'], [Tn])
                        yield
                        Qr = lambda h: MNb[:, h, 0:64]
                        Ql = lambda h: MNbf[:, h * 128:h * 128 + 128]
                        Qn = 'MNb'
                        QTf = QT[:, :, :].rearrange("p h t -> p (h t)")
                        QTr = lambda h, QT=QT: QT[:, h, :]
                        QTl = lambda h, QTf=QTf: QTf[:, h * 64:h * 64 + 128]
                        pend = None
                        for lvl in range(1, 7):
                            need_q = lvl <= 4
                            need_qt = lvl <= 5
                            if need_q:
                                for h in range(8):
                                    MM(B[4][:, h * 64:(h + 1) * 64], QTl(h), Qr(h), True, True, [QTn, Qn], ['B4'])
                            if need_qt:
                                for h in range(8):
                                    MM(B[5][:, h * 64:(h + 1) * 64], Ql(h), QTr(h), True, True, [QTn, Qn], ['B5'])
                            if pend is not None:
                                pl, pn = pend
                                Tf = Tc[:, :, :].rearrange("p h t -> p (h t)")
                                for h in range(8):
                                    MM(B[6][:, h * 64:(h + 1) * 64], pl(h), Tc[:, h, :], True, True, [pn, Tn], ['B6'])
                            if need_q:
                                Q2, Q2n = Q_()
                                CP('act', Q2[0:64, 0:8, :], B[4][0:64, :].rearrange("p (h t) -> p h t", t=64), ['B4'], [Q2n])
                            if need_qt:
                                QT2, QT2n = Q_()
                                CP('act', QT2[0:64, 0:8, :], B[5][0:64, :].rearrange("p (h t) -> p h t", t=64), ['B5'], [QT2n])
                            if pend is not None:
                                ti ^= 1
                                Tnew, Tnn = Tb[ti], f'Tb{ti}'
                                TT('dve', Tnew[0:64, 0:8, :], B[6][0:64, :].rearrange("p (h t) -> p h t", t=64), Tc[0:64, 0:8, :], ALU.add, ['B6', Tn], [Tnn])
                                Tc, Tn = Tnew, Tnn
                            if need_qt:
                                QT2f = QT2[:, :, :].rearrange("p h t -> p (h t)")
                                pend = ((lambda h, f_=QT2f: f_[:, h * 64:h * 64 + 128]), QT2n)
                                QTr = lambda h, QT2=QT2: QT2[:, h, :]
                                QTl = pend[0]
                                QTn = QT2n
                            else:
                                pend = None
                            if need_q:
                                Q2f = Q2[:, :, :].rearrange("p h t -> p (h t)")
                                Qr = lambda h, Q2=Q2: Q2[:, h, :]
                                Ql = lambda h, f_=Q2f: f_[:, h * 64:h * 64 + 128]
                                Qn = Q2n
                            yield
                        Tf = Tc[:, :, :].rearrange("p h t -> p (h t)")
                        So, Son = Sbf[sbi[0]], f'Sbf{sbi[0]}'
                        for h in range(8):
                            p, e = h // 2, h % 2
                            MM(B[4][:, h * 64:(h + 1) * 64], ar(p, e, c, 0, 128), So[:, p * 128 + e * 64:p * 128 + e * 64 + 64], True, False, ['arT', Son], ['B4'])
                            MM(B[4][:, h * 64:(h + 1) * 64], MNkf[:, h * 128:h * 128 + 128], VTF[:, h * 64:(h + 1) * 64], False, True, ['MNk', 'VTF'], ['B4'])
                        CP('act', Wsb[0:64, :], B[4][0:64, :], ['B4'], ['Wsb'])
                        yield
                        for h in range(8):
                            MM(B[5][:, h * 64:(h + 1) * 64], Tf[:, h * 64:h * 64 + 128], Wsb[:, h * 64:(h + 1) * 64], True, True, [Tn, 'Wsb'], ['B5'])
                        CP('act', Usb[0:64, :], B[5][0:64, :], ['B5'], ['Usb'])
                        yield
                        for h in range(8):
                            p, e = h // 2, h % 2
                            o = B[6][:, h * 64:(h + 1) * 64]
                            MM(o, ar(p, e, c, 1, 128), So[:, p * 128 + e * 64:p * 128 + e * 64 + 64], True, False, ['arT', Son], ['B6'])
                            MM(o, MNbf[:, h * 128 + 64:h * 128 + 192], Usb[:, h * 64:(h + 1) * 64], False, False, ['MNb', 'Usb'], ['B6'])
                            MM(o, MNkf[:, h * 128 + 64:h * 128 + 192], VTF[:, h * 64:(h + 1) * 64], False, True, ['MNk', 'VTF'], ['B6'])
                        CP('act', Ytf[0:64, :], B[6][0:64, :], ['B6'], ['Ytf'])
                        for p in range(4):
                            o = B[4][:, p * 128:(p + 1) * 128]
                            MM(o, bkTF[:, p * 128:(p + 1) * 128], Usb[:, p * 128:(p + 1) * 128], True, False, ['bkTF', 'Usb'], ['B4'])
                            MM(o, bkTF[:, 512 + p * 128:512 + (p + 1) * 128], VTF[:, p * 128:(p + 1) * 128], False, True, ['bkTF', 'VTF'], ['B4'])
                        TT('dve', tmpS[:, :], B[4][:, :], S32[:, :], ALU.add, ['B4', 'S32'], ['tmpS'])
                        gbc = gam[:, :, c:c + 1].broadcast_to([128, 4, 128])
                        tS3 = tmpS[:, :].rearrange("p (a b) -> p a b", b=128)
                        TT('dve', S32[:, :].rearrange("p (a b) -> p a b", b=128), tS3, gbc, ALU.mult, ['tmpS', 'gam'], ['S32'])
                        sbi[0] ^= 1
                        TT('pool', Sbf[sbi[0]][:, :].rearrange("p (a b) -> p a b", b=128), tS3, gbc, ALU.mult, ['tmpS', 'gam'], [f'Sbf{sbi[0]}'])
                        yield
                        for p in range(4):
                            TR(B[5][:, p * 128:(p + 1) * 128], Ytf[:, p * 128:(p + 1) * 128], ident, ['Ytf', 'cst'], ['B5'])
                        for p in range(4):
                            CP('dve' if p % 2 == 0 else 'act', ycol(p, c0, CH), B[5][:, p * 128:p * 128 + CH], ['B5'], [cres])
                        yield
                    for p in range(4):
                        y = ycol(p, 0, ST)
                        bk, bn = bank('R')
                        MM(bk[:, 0:256], blockones, y, True, True, ['cst', 'yT'], [bn])
                        sq, sqn = T_()
                        TT('pool', sq, y, y, ALU.mult, ['yT'], [sqn])
                        MM(bk[:, 256:512], blockones, sq, True, True, ['cst', sqn], [bn])
                        m, mn = T_()
                        TS('dve', m, bk[:, 0:256], 1.0 / 64, None, ALU.mult, None, [bn], [mn])
                        msq, msqn = T_()
                        TT('pool', msq, m, m, ALU.mult, [mn], [msqn])
                        STT(msq, bk[:, 256:512], 1.0 / 64, msq, ALU.mult, ALU.subtract, [bn, msqn], [msqn])
                        ACT(msq, msq, AF.Sqrt, [msqn], [msqn], bias=GN_EPS)
                        RECIP(msq, msq, [msqn], [msqn])
                        TT('dve', m, y, m, ALU.subtract, ['yT', mn], [mn])
                        TT('dve', m, m, msq, ALU.mult, [mn, msqn], [mn])
                        TS('dve', m, m, pp[:, PP_LNG + p:PP_LNG + p + 1], pp[:, PP_LNB + p:PP_LNB + p + 1], ALU.mult, ALU.add, [mn, 'pp'], [mn])
                        TT('pool', m, m, bonus[:, p, :], ALU.add, [mn, 'bonus'], [mn])
                        TT('dve', mixT[:, 4 + p, :], m, sgb[:, p, :], ALU.mult, [mn, 'sgb'], ['mixT'])
                        yield

                nkt = 2 * st + 2

                def att_stream():
                    items = [(h, kt) for h in range(0 if os.environ.get('KNOATT') else 4) for kt in range(nkt)]

                    def emit_scores(i):
                        h, kt = items[i]
                        j = kt - 2 * st
                        col0 = 128 * max(j, 0)
                        bk, bn = B[i % 2], f'B{i % 2}'
                        kres = ('kTc', h * 4096 + kt * 128, h * 4096 + kt * 128 + 128)
                        if col0 == 0:
                            MM(bk[:, :], kTc[:, h, kt * 128:(kt + 1) * 128], qT[:, h, :, :].rearrange("p m q -> p (m q)"), True, True, [kres, 'qT'], [bn])
                        else:
                            for m_ in range(2):
                                MM(bk[:, m_ * ST + col0:(m_ + 1) * ST], kTc[:, h, kt * 128:(kt + 1) * 128], qT[:, h, m_, col0:ST], True, True, [kres, 'qT'], [bn])

                    if items:
                        emit_scores(0)
                    for i, (h, kt) in enumerate(items):
                        if i + 1 < len(items):
                            emit_scores(i + 1)
                        j = kt - 2 * st
                        col0 = 128 * max(j, 0)
                        bk, bn = B[i % 2], f'B{i % 2}'
                        pt = Pt[i % 2]
                        ptn = f'Pt{i % 2}'
                        sview = bk[:, :].rearrange("p (m q) -> p m q", q=ST)[:, :, col0:ST]
                        pview = pt[:, :, col0:ST]
                        ACT(pview, sview, AF.Exp, [bn], [ptn], scale=0.125)
                        if j >= 0:
                            MEMSET('pool', pt[64:128, :, col0:col0 + 64], 0.0, [ptn])
                        vv = Vc[:, kt, h * 128:(h + 1) * 128]
                        vres = ('Vc', kt * 512, kt * 512 + 512)
                        last = kt == nkt - 1
                        if col0 == 0:
                            p2 = pt[:, :, :].rearrange("p m q -> p (m q)")
                            MM(B[2][:, :], vv, p2, kt == 0, last, [vres, ptn], ['B2'])
                            MM(B[3][:, :], onesbf[:, :], p2, kt == 0, last, ['onesbf', ptn], ['B3'])
                        else:
                            for m_ in range(2):
                                MM(B[2][:, m_ * ST + col0:(m_ + 1) * ST], vv, pt[:, m_, col0:ST], False, last and m_ == 1, [vres, ptn], ['B2'])
                            for m_ in range(2):
                                MM(B[3][:, m_ * ST + col0:(m_ + 1) * ST], onesbf[:, :], pt[:, m_, col0:ST], False, last and m_ == 1, ['onesbf', ptn], ['B3'])
                        if last:
                            a1 = att1[:, :, :].rearrange("p m q -> p (m q)")
                            a2 = att2[:, :, :].rearrange("p m q -> p (m q)")
                            RECIP(a1, B[3][:, :], ['B3'], ['att1'])
                            TT('dve', a2, B[2][:, :], a1, ALU.mult, ['B2', 'att1'], ['att2'])
                            o_, on_ = att1[:, 0, :], 'att1'
                            STT(o_, att2[:, 1, :], neglam[:, 0:1], att2[:, 0, :], ALU.mult, ALU.add, ['att2', 'neglam'], [on_])
                            sq = att1[:, 1, :]
                            TT('pool', sq, o_, o_, ALU.mult, [on_], [on_])
                            sbk, sbn = B[i % 2], f'B{i % 2}'
                            MM(sbk[:, 0:256], ones32[:, :], sq, True, True, ['ones32', on_], [sbn])
                            ACT(sq, sbk[:, 0:256], AF.Sqrt, [sbn], [on_], scale=1.0 / 128, bias=SUBLN_EPS)
                            RECIP(sq, sq, [on_], [on_])
                            STT(o_, o_, g08[:, 0:1], sq, ALU.mult, ALU.mult, [on_, 'g08'], [on_])
                            TT('dve', mixT[:, h, :], o_, sga[:, h, :], ALU.mult, [on_, 'sga'], ['mixT'])
                        yield

                merge([(rwkv_stream(), 75), (att_stream(), 4 * nkt)])

                for tt in range(2):
                    DMA('sync', f'ld_h{tt}', hres[tt][:, :], x_d[row0 + tt * 128:row0 + (tt + 1) * 128, :], (), [f'hres{tt}'])
                for n in range(4):
                    ws, wsn = load_w(wobf_v, n * 256, 256, 'wobf')
                    for tt in range(2):
                        bk, bn = bank('A')
                        for f in range(8):
                            MM(bk[:, 0:256], mixT[:, f, tt * 128:(tt + 1) * 128], ws[:, f, 0:256], f == 0, f == 7, ['mixT', wsn], [bn])
                        TT('dve', hres[tt][:, n * 256:(n + 1) * 256], bk[:, 0:256], hres[tt][:, n * 256:(n + 1) * 256], ALU.add, [bn, f'hres{tt}'], [f'hres{tt}'])
                for tt in range(2):
                    xi = xctr[0] % 2
                    ACT(xbuf[xi][:, :], hres[tt][:, :], AF.Square, [f'hres{tt}'], [f'xbuf{xi}', 'stat4'], accum=stat[:, 4:5])
                    ACT(stat[:, 5:6], stat[:, 4:5], AF.Sqrt, ['stat4'], ['stat5'], scale=1.0 / D, bias=RMS_EPS)
                    RECIP(stat[:, 6:7], stat[:, 5:6], ['stat5'], ['stat6'])
                    STT(hres[tt][:, :], hres[tt][:, :], stat[:, 6:7], fg[:, :], ALU.mult, ALU.mult, [f'hres{tt}', 'stat6', 'fg'], [f'hres{tt}'])
                    DMA('pool', f'st_{tt}', out_d[row0 + tt * 128:row0 + (tt + 1) * 128, :], hres[tt][:, :], [f'hres{tt}'], ['out'])

        P.final_wait('pool', ['st_0', 'st_1'])

        with nc.Block() as block:
            @block.sync
            def _(e):
                for f in P.engs['sync']['thunks']:
                    f(e)

            @block.tensor
            def _(e):
                for f in P.engs['pe']['thunks']:
                    f(e)

            @block.scalar
            def _(e):
                for f in P.engs['act']['thunks']:
                    f(e)

            @block.vector
            def _(e):
                for f in P.engs['dve']['thunks']:
                    f(e)

            @block.gpsimd
            def _(e):
                for f in P.engs['pool']['thunks']:
                    f(e)
    return nc


def make_consts():
    cst = np.zeros((128, 832), np.float32)
    cst[:, 0:128] = np.eye(128, dtype=np.float32)
    cst[0:64, 128:192] = 1.0
    cst[64:128, 192:256] = 1.0
    j = np.arange(64)[:, None]
    t = np.arange(64)[None, :]
    cst[0:64, 256:320] = (j < t)
    cst[0:64, 320:384] = (j <= t)
    cst[0:64, 384:448] = (j > t)
    sm = np.ones(256, np.float32)
    sm[::64] = 0.0
    cst[:, 576:832] = sm[None, :]
    return cst


def prep_params(norm_g, lambda_q1, lambda_k1, lambda_q2, lambda_k2, subln_g, shift_mu, w0, w2, a0, a2,
                k_k, k_a, r_k, ln_x_g, ln_x_b, final_g):
    f = lambda v, n: np.ascontiguousarray(np.asarray(v, np.float32).reshape(n, 128).T)
    pp = np.zeros((128, PP_N), np.float32)
    pp[:, PP_MU:PP_MU + 13] = f(shift_mu[0], 13)
    pp[:, PP_W0:PP_W0 + 4] = f(w0[0], 4)
    pp[:, PP_A0:PP_A0 + 4] = f(a0[0], 4)
    pp[:, PP_KK:PP_KK + 4] = f(k_k[0], 4)
    pp[:, PP_KA:PP_KA + 4] = f(k_a[0], 4)
    pp[:, PP_RK:PP_RK + 4] = f(r_k[0].reshape(-1), 4)
    pp[:, PP_LNG:PP_LNG + 4] = f(ln_x_g[0], 4)
    pp[:, PP_LNB:PP_LNB + 4] = f(ln_x_b[0], 4)
    pp[:, PP_SUB] = np.asarray(subln_g[0], np.float32)
    pp[:, PP_NG:PP_NG + 8] = f(norm_g[0], 8)
    lam4 = np.concatenate([np.broadcast_to(np.asarray(v[0], np.float32)[None, :], (128, 64))
                           for v in (lambda_q1, lambda_k1, lambda_q2, lambda_k2)], axis=1)
    lw = np.concatenate([np.asarray(w2[0], np.float32), np.asarray(a2[0], np.float32)], axis=0)
    fg = np.broadcast_to(np.asarray(final_g, np.float32)[None, :], (128, D))
    return dict(pp=pp, lam4=np.ascontiguousarray(lam4), lw=np.ascontiguousarray(lw), fg=np.ascontiguousarray(fg),
                cst=make_consts())


_CACHE = {}


def run(x, w_in, w_out, small, n_cores):
    x = np.asarray(x, np.float32)
    Bt, S, _ = x.shape
    NB = Bt // n_cores
    key = (NB, S)
    if key not in _CACHE:
        _CACHE[key] = build(NB, S)
    nc = _CACHE[key]
    w_in2 = np.ascontiguousarray(np.asarray(w_in, np.float32)[0])
    w_out2 = np.ascontiguousarray(np.asarray(w_out, np.float32)[0])
    in_maps = []
    for c in range(n_cores):
        m = dict(small)
        m["x"] = np.ascontiguousarray(x[c * NB:(c + 1) * NB].reshape(NB * S, D))
        m["w_in"] = w_in2
        m["w_out"] = w_out2
        in_maps.append(m)
    res = run_bass_kernel_spmd(nc, in_maps, core_ids=list(range(n_cores)))
    outs = [np.asarray(r["out"]).reshape(NB, S, D) for r in res.results]
    return np.concatenate(outs, axis=0).astype(np.float32)


def kernel(x, norm_g, w_in, lambda_q1, lambda_k1, lambda_q2, lambda_k2, subln_g,
           shift_mu, w0, w2, a0, a2, k_k, k_a, r_k, ln_x_g, ln_x_b, w_out, final_g):
    small = prep_params(norm_g, lambda_q1, lambda_k1, lambda_q2, lambda_k2, subln_g, shift_mu, w0, w2, a0, a2,
                        k_k, k_a, r_k, ln_x_g, ln_x_b, final_g)
    return run(x, w_in, w_out, small, 8)
```

```python
import os
import numpy as np
import concourse.bass as bass
import concourse.mybir as mybir
from concourse.bass_utils import run_bass_kernel_spmd

F32 = mybir.dt.float32
BF16 = mybir.dt.bfloat16
AF = mybir.ActivationFunctionType
ALU = mybir.AluOpType
AX = mybir.AxisListType

D = 1024
DIN = 4224
ST = 256
CH = 64
C0 = 0.6065306597126334
RMS_EPS = 1e-6
SUBLN_EPS = 1e-5
GN_EPS = 1e-5 * 64
LAM_INIT = 0.2
BIG = 1 << 40

OFF_Q, OFF_K, OFF_VA, OFF_GA, OFF_WDAD, OFF_RKV, OFF_GB = 0, 512, 1024, 1536, 2048, 2176, 3712

PP_MU, PP_W0, PP_A0, PP_KK, PP_KA, PP_RK, PP_LNG, PP_LNB, PP_SUB, PP_NG, PP_N = 0, 13, 17, 21, 25, 29, 33, 37, 41, 42, 50


class Prog:
    def __init__(self):
        self.engs = {}
        self.acc = {}

    def add(self, name, sem, real=True):
        self.engs[name] = dict(sem=sem, count=0, thunks=[], waited={}, real=real)

    @staticmethod
    def _norm(r):
        if isinstance(r, str):
            return (r, 0, BIG)
        return r

    def op(self, eng, fn, reads=(), writes=(), dma=None):
        self.nops = getattr(self, 'nops', 0) + 1
        if self.nops > int(os.environ.get('KSTOP', '100000000')):
            return
        if os.environ.get('KLOG'):
            import inspect
            fr = inspect.stack()
            print('OP', self.nops, eng, [f.lineno for f in fr[1:4]], flush=True)
        E = self.engs[eng]
        deps = {}
        reads = [self._norm(r) for r in reads]
        writes = [self._norm(r) for r in writes]
        for (n, lo, hi) in reads:
            for (l2, h2, k, e, i) in self.acc.get(n, ()):
                if k == 'w' and l2 < hi and lo < h2:
                    deps[e] = max(deps.get(e, 0), i)
        for (n, lo, hi) in writes:
            for (l2, h2, k, e, i) in self.acc.get(n, ()):
                if l2 < hi and lo < h2:
                    deps[e] = max(deps.get(e, 0), i)
        for e, i in deps.items():
            if e == eng and eng == 'pe':
                continue
            if E['waited'].get(e, 0) >= i:
                continue
            sem = self.engs[e]['sem']
            if self.engs[e].get('unordered'):
                E['waited'][e] = BIG
                E['thunks'].append(lambda en, s=sem, d=self.engs[e]: en.wait_ge(s, d['count']))
            else:
                E['waited'][e] = i
                E['thunks'].append(lambda en, s=sem, v=i: en.wait_ge(s, v))
        ce = dma if dma else eng
        CE = self.engs[ce]
        inc = 16 if dma else 1
        CE['count'] += inc
        idx = CE['count']
        E['thunks'].append(lambda en, f=fn, s=CE['sem'], i=inc: f(en).then_inc(s, i))
        for (n, lo, hi) in writes:
            lst = self.acc.setdefault(n, [])
            lst[:] = [a for a in lst if not (lo <= a[0] and a[1] <= hi)]
            lst.append((lo, hi, 'w', ce, idx))
        for (n, lo, hi) in reads:
            lst = self.acc.setdefault(n, [])
            lst[:] = [a for a in lst if not (a[2] == 'r' and a[3] == ce and lo <= a[0] and a[1] <= hi)]
            lst.append((lo, hi, 'r', ce, idx))

    def final_wait(self, eng, names):
        E = self.engs[eng]
        for n in names:
            c = self.engs[n]['count']
            if c > 0:
                E['thunks'].append(lambda en, s=self.engs[n]['sem'], v=c: en.wait_ge(s, v))


def build(NB, S, dbg=False):
    NT = S // 128
    NST = S // ST
    nc = bass.Bass("TRN2", target_bir_lowering=False)
    dt = nc.dram_tensor
    x_d = dt("x", [NB * S, D], F32, kind="ExternalInput").ap()
    win_d = dt("w_in", [D, DIN], F32, kind="ExternalInput").ap()
    wout_d = dt("w_out", [D, D], F32, kind="ExternalInput").ap()
    pp_d = dt("pp", [128, PP_N], F32, kind="ExternalInput").ap()
    lam_d = dt("lam4", [128, 256], F32, kind="ExternalInput").ap()
    lw_d = dt("lw", [128, 512], F32, kind="ExternalInput").ap()
    fg_d = dt("fg", [128, D], F32, kind="ExternalInput").ap()
    cst_d = dt("cst", [128, 832], F32, kind="ExternalInput").ap()
    out_d = dt("out", [NB * S, D], F32, kind="ExternalOutput").ap()
    wbf_d = dt("wbf", [D, DIN], BF16, kind="ExternalOutput").ap()
    wobf_d = dt("wobf", [D, D], BF16, kind="ExternalOutput").ap()

    sb = nc.alloc_sbuf_tensor
    kTc = sb("kTc", [128, 4, 4096], BF16)
    Vc = sb("Vc", [128, 32, 512], BF16)
    wslot = [sb(f"wslot{i}", [128, 8, 256], BF16) for i in range(5)]
    pp = sb("pp_sb", [128, PP_N], F32)
    omu = sb("omu", [128, 13], F32)
    omka = sb("omka", [128, 4], F32)
    g08 = sb("g08", [128, 1], F32)
    neglam = sb("neglam", [128, 1], F32)
    lamw = sb("lamw", [128, 8], F32)
    lw32 = sb("lw32", [128, 512], F32)
    lwbf = sb("lwbf", [128, 512], BF16)
    fg = sb("fg_sb", [128, D], F32)
    cst = sb("cst_sb", [128, 832], F32)
    ident = cst[:, 0:128]
    blockones = cst[:, 128:256]
    maskA = cst[0:64, 256:384]
    maskC = cst[0:64, 384:448]
    scanmask = cst[:, 576:832]
    identbf = sb("identbf", [128, 128], BF16)
    onesbf = sb("onesbf", [128, 128], BF16)
    ones32 = sb("ones32", [128, 128], F32)
    xbuf = [sb(f"xbuf{i}", [128, D], F32) for i in range(2)]
    hres = [sb(f"hres{i}", [128, D], F32) for i in range(2)]
    stat = sb("stat", [128, 16], F32)
    junk = sb("junk", [128, D], F32)
    xnT = sb("xnT", [128, 8, ST], BF16)
    qT = sb("qT", [128, 4, 2, ST], BF16)
    sga = sb("sga", [128, 4, ST], BF16)
    sgb = sb("sgb", [128, 4, ST], BF16)
    mixT = sb("mixT", [128, 8, ST], BF16)
    rkv = [sb(f"rkv{i}", [128, 3, ST], F32) for i in range(2)]
    wdad = sb("wdad", [128, ST], F32)
    linbf = sb("linbf", [128, 2, ST], BF16)
    tmpm = [sb(f"tmpm{i}", [128, ST + 2], F32) for i in range(2)]
    carry = sb("carry", [128, 16], F32)
    NTMP = 12
    tmp = [sb(f"tmp{i}", [128, ST], F32) for i in range(NTMP)]
    lamt = tmp[0]
    arT = sb("arT", [128, 4 * 2 * 4 * 2 * CH + CH], BF16)
    bT = sb("bT", [128, 4 * ST + CH], BF16)
    ktT = sb("ktT", [128, 4 * ST + CH], BF16)
    bonus = sb("bonus", [128, 4, ST], F32)
    gam = sb("gam", [128, 4, 4], F32)
    yT = sb("yT", [128, 4 * ST + CH], F32)
    MNb = sb("MNb", [128, 9, 128], BF16)
    MNk = sb("MNk", [128, 9, 128], BF16)
    NQ = 6
    qbuf = [sb(f"qbuf{i}", [128, 9, 64], BF16) for i in range(NQ)]
    Tb = [sb(f"Tb{i}", [128, 9, 64], BF16) for i in range(2)]
    bkTF = sb("bkTF", [128, 1024], BF16)
    VTF = sb("VTF", [128, 512], BF16)
    Wsb = sb("Wsb", [128, 512], BF16)
    Usb = sb("Usb", [128, 512], BF16)
    S32 = sb("S32", [128, 512], F32)
    Sbf = [sb(f"Sbf{i}", [128, 512], BF16) for i in range(2)]
    tmpS = sb("tmpS", [128, 512], F32)
    Ytf = sb("Ytf", [128, 512], F32)
    Pt = [sb(f"Pt{i}", [128, 2, ST], BF16) for i in range(2)]
    att1 = sb("att1", [128, 2, ST], F32)
    att2 = sb("att2", [128, 2, ST], F32)

    ps = nc.alloc_psum_tensor
    B = [ps(f"B{i}", [128, 512], F32) for i in range(7)]
    PTB = ps("PTB", [128, 1024], BF16)

    P = Prog()
    import contextlib
    es = contextlib.ExitStack()
    with es:
        def sem(n):
            return es.enter_context(nc.semaphore(n))
        for e in ('pe', 'act', 'dve', 'pool', 'sync'):
            P.add(e, sem("s_" + e))
        slots = ['ld_misc', 'ld_x0', 'ld_x1', 'ld_w0', 'ld_w1', 'ld_w2', 'ld_w3', 'ld_w4', 'ld_h0', 'ld_h1', 'st_0', 'st_1', 'st_w0', 'st_w1', 'ld_s0', 'ld_s1']
        for s_ in slots:
            P.add(s_, sem(s_), real=False)
        P.engs['ld_misc']['unordered'] = True

        def ACT(out, in_, func, r, w, scale=1.0, bias=0.0, accum=None):
            if accum is None:
                P.op('act', lambda e: e.activation(out=out, in_=in_, func=func, scale=scale, bias=bias), r, w)
            else:
                P.op('act', lambda e: e.activation(out=out, in_=in_, func=func, scale=scale, bias=bias, accum_out=accum), r, w)

        def CP(eng, out, in_, r, w):
            if eng == 'act':
                P.op('act', lambda e: e.activation(out=out, in_=in_, func=AF.Copy), r, w)
            else:
                P.op(eng, lambda e: e.tensor_copy(out=out, in_=in_), r, w)

        def TT(eng, out, in0, in1, op, r, w):
            P.op(eng, lambda e: e.tensor_tensor(out=out, in0=in0, in1=in1, op=op), r, w)

        def TS(eng, out, in0, s1, s2, op0, op1, r, w):
            if s2 is None:
                P.op(eng, lambda e: e.tensor_scalar(out=out, in0=in0, scalar1=s1, scalar2=None, op0=op0), r, w)
            else:
                P.op(eng, lambda e: e.tensor_scalar(out=out, in0=in0, scalar1=s1, scalar2=s2, op0=op0, op1=op1), r, w)

        def STT(out, in0, scalar, in1, op0, op1, r, w):
            P.op('dve', lambda e: e.scalar_tensor_tensor(out=out, in0=in0, scalar=scalar, in1=in1, op0=op0, op1=op1), r, w)

        def RECIP(out, in_, r, w):
            P.op('dve', lambda e: e.reciprocal(out=out, in_=in_), r, w)

        pemode = [None]

        def _rnd(v):
            return 32 if v <= 32 else (64 if v <= 64 else 128)

        def pe_mode(lhsT):
            K_ = lhsT.shape[0]
            M_ = 1
            for d_ in lhsT.shape[1:]:
                M_ *= d_
            md = (_rnd(K_), _rnd(M_))
            if pemode[0] is not None and pemode[0] != md:
                E = P.engs['pe']
                if E['count'] > 0:
                    E['thunks'].append(lambda en, s=E['sem'], v=E['count']: en.wait_ge(s, v))
            pemode[0] = md

        def MM(out, lhsT, rhs, start, stop, r, w):
            pe_mode(lhsT)
            P.op('pe', lambda e: e.matmul(out, lhsT=lhsT, rhs=rhs, start=start, stop=stop), r, w)

        def TR(out, in_, idn, r, w):
            pe_mode(in_)
            P.op('pe', lambda e: e.transpose(out, in_, idn), r, w)

        def DMA(queue, slot, out, in_, r, w):
            P.op(queue, lambda e: e.dma_start(out=out, in_=in_), r, w, dma=slot)

        def MEMSET(eng, ap, val, w):
            P.op(eng, lambda e: e.memset(ap, val), (), w)

        DMA('sync', 'ld_misc', pp[:, :], pp_d[:, :], (), ['pp'])
        DMA('sync', 'ld_misc', lamt[:, :], lam_d[:, :], (), ['tmp0'])
        DMA('sync', 'ld_misc', lw32[:, :], lw_d[:, :], (), ['lw32'])
        DMA('sync', 'ld_misc', fg[:, :], fg_d[:, :], (), ['fg'])
        DMA('sync', 'ld_misc', cst[:, :], cst_d[:, :], (), ['cst'])
        TS('dve', omu[:, :], pp[:, PP_MU:PP_MU + 13], -1.0, 1.0, ALU.mult, ALU.add, ['pp'], ['omu'])
        TS('dve', omka[:, :], pp[:, PP_KA:PP_KA + 4], -1.0, 1.0, ALU.mult, ALU.add, ['pp'], ['omka'])
        TS('dve', g08[:, :], pp[:, PP_SUB:PP_SUB + 1], 1.0 - LAM_INIT, None, ALU.mult, None, ['pp'], ['g08'])
        CP('dve', lwbf[:, :], lw32[:, :], ['lw32'], ['lwbf'])
        CP('dve', identbf[:, :], ident, ['cst'], ['identbf'])
        MEMSET('pool', onesbf[:, :], 1.0, ['onesbf'])
        MEMSET('pool', ones32[:, :], 1.0, ['ones32'])
        for t_, n_ in [(qT, 'qT'), (linbf, 'linbf'), (arT, 'arT'), (MNb, 'MNb'), (MNk, 'MNk'), (bkTF, 'bkTF'), (VTF, 'VTF'),
                       (Wsb, 'Wsb'), (Usb, 'Usb'), (Ytf, 'Ytf'), (Tb[0], 'Tb0'), (Tb[1], 'Tb1'), (bT, 'bT'), (ktT, 'ktT'), (yT, 'yT')] + [(qbuf[i], f'qbuf{i}') for i in range(NQ)]:
            nd = len(t_.shape)
            MEMSET('pool', t_[(slice(None),) * nd], 0.0, [n_])
        TT('dve', lamt[:, 0:64], lamt[:, 0:64], lamt[:, 64:128], ALU.mult, ['tmp0'], ['tmp0'])
        TT('dve', lamt[:, 128:192], lamt[:, 128:192], lamt[:, 192:256], ALU.mult, ['tmp0'], ['tmp0'])
        P.op('dve', lambda e: e.tensor_reduce(out=lamw[:, 0:1], in_=lamt[:, 0:64], axis=AX.X, op=ALU.add), ['tmp0'], ['lamw'])
        P.op('dve', lambda e: e.tensor_reduce(out=lamw[:, 1:2], in_=lamt[:, 128:192], axis=AX.X, op=ALU.add), ['tmp0'], ['lamw'])
        ACT(lamw[:, 2:4], lamw[:, 0:2], AF.Exp, ['lamw'], ['lamw2'])
        TT('dve', lamw[:, 4:5], lamw[:, 3:4], lamw[:, 2:3], ALU.subtract, ['lamw2'], ['lamw3'])
        TS('dve', neglam[:, :], lamw[:, 4:5], -LAM_INIT, None, ALU.add, None, ['lamw3'], ['neglam'])

        stg32 = [kTc[:, :, :].rearrange("p a b -> p (a b)").bitcast(F32), Vc[:, :, :].rearrange("p a b -> p (a b)").bitcast(F32)]
        stgbf = [kTc[:, :, :].rearrange("p a b -> p (a b)"), Vc[:, :, :].rearrange("p a b -> p (a b)")]
        stn = ['kTc', 'Vc']
        BO = 9216
        for c in range(8):
            s_ = c % 2
            src = stg32[s_]
            dst = stgbf[s_]
            DMA('sync', f'ld_s{s_}', src[:, 0:DIN], win_d[c * 128:(c + 1) * 128, :], (), [stn[s_]])
            ng = pp[:, PP_NG + c:PP_NG + c + 1]
            eng = 'dve' if c % 2 == 0 else 'pool'
            TS(eng, dst[:, BO:BO + 2048], src[:, 0:2048], ng, None, ALU.mult, None, [stn[s_], 'pp'], [stn[s_]])
            TS(eng, dst[:, BO + OFF_WDAD:BO + OFF_WDAD + 128], src[:, 3584:3712], ng, None, ALU.mult, None, [stn[s_], 'pp'], [stn[s_]])
            for j in range(3):
                outap = dst[:, BO + OFF_RKV:BO + OFF_RKV + 1536].rearrange("p (a b) -> p a b", b=384)[:, :, j * 128:(j + 1) * 128]
                inap = src[:, 2048 + j * 512:2048 + (j + 1) * 512].rearrange("p (a b) -> p a b", b=128)
                TS(eng, outap, inap, ng, None, ALU.mult, None, [stn[s_], 'pp'], [stn[s_]])
            TS(eng, dst[:, BO + OFF_GB:BO + OFF_GB + 512], src[:, 3712:4224], ng, None, ALU.mult, None, [stn[s_], 'pp'], [stn[s_]])
            DMA('sync', f'st_w{s_}', wbf_d[c * 128:(c + 1) * 128, :], dst[:, BO:BO + DIN], [stn[s_]], [('wbf', c, c + 1)])
        for c in range(8):
            s_ = c % 2
            src = stg32[s_]
            dst = stgbf[s_]
            DMA('sync', f'ld_s{s_}', src[:, 0:D], wout_d[c * 128:(c + 1) * 128, :], (), [stn[s_]])
            CP('dve' if c % 2 == 0 else 'pool', dst[:, BO:BO + D], src[:, 0:D], [stn[s_]], [stn[s_]])
            DMA('sync', f'st_w{s_}', wobf_d[c * 128:(c + 1) * 128, :], dst[:, BO:BO + D], [stn[s_]], [('wobf', c, c + 1)])

        wbf_v = wbf_d.rearrange("(c p) n -> p c n", p=128)
        wobf_v = wobf_d.rearrange("(c p) n -> p c n", p=128)

        NS = 5
        wctr = [0]

        def load_w(view, off, ncols, rname):
            i = wctr[0] % NS
            wctr[0] += 1
            DMA('sync', f'ld_w{i}', wslot[i][:, :, 0:ncols], view[:, :, off:off + ncols], [rname], [f'wslot{i}'])
            return wslot[i], f'wslot{i}'

        tctr = [0]

        def T_():
            i = tctr[0] % NTMP
            tctr[0] += 1
            return tmp[i][:, :], f'tmp{i}'

        qctr = [0]

        def Q_():
            i = qctr[0] % NQ
            qctr[0] += 1
            return qbuf[i], f'qbuf{i}'

        stctr = [0]
        xctr = [0]
        rctr = {'A': 0, 'R': 0}
        BANKS = {'A': [0, 1, 2, 3], 'R': [4, 5, 6]}

        def bank(stream):
            bl = BANKS[stream]
            i = bl[rctr[stream] % len(bl)]
            rctr[stream] += 1
            return B[i], f'B{i}'

        def proj_ftile(ws, wsn, col, stream):
            bk, bn = bank(stream)
            o = bk[:, 0:ST]
            for c in range(8):
                MM(o, ws[:, c, col:col + 128], xnT[:, c, :], c == 0, c == 7, [wsn, 'xnT'], [bn])
            return o, bn

        def shift_evac(o, on, mucol, dst, dstn):
            tm = tmpm[stctr[0] % 2]
            tmn = f'tmpm{stctr[0] % 2}'
            stctr[0] += 1
            ACT(tm[:, 1:ST + 1], o, AF.Copy, [on, 'pp'], [tmn], scale=pp[:, PP_MU + mucol:PP_MU + mucol + 1])
            CP('pool', tm[:, 0:1], carry[:, mucol:mucol + 1], ['carry'], [tmn])
            STT(dst, o, omu[:, mucol:mucol + 1], tm[:, 0:ST], ALU.mult, ALU.add, [on, tmn, 'omu'], [dstn])
            CP('pool', carry[:, mucol:mucol + 1], tm[:, ST:ST + 1], [tmn], ['carry'])

        def bcol(p, lo, n):
            return bT[:, p * ST + lo:p * ST + lo + n]

        def kcol(p, lo, n):
            return ktT[:, p * ST + lo:p * ST + lo + n]

        def ycol(p, lo, n):
            return yT[:, p * ST + lo:p * ST + lo + n]

        def ar(p, e, c, a, n):
            off = (((p * 2 + e) * 4 + c) * 2 + a) * CH
            return arT[:, off:off + n]

        MNbf = MNb[:, :, :].rearrange("p h t -> p (h t)")
        MNkf = MNk[:, :, :].rearrange("p h t -> p (h t)")

        def merge(gens):
            prog = [0] * len(gens)
            alive = [True] * len(gens)
            while any(alive):
                best = None
                for i, (g, n) in enumerate(gens):
                    if alive[i]:
                        f = prog[i] / max(n, 1)
                        if best is None or f < best[0]:
                            best = (f, i)
                i = best[1]
                try:
                    next(gens[i][0])
                    prog[i] += 1
                except StopIteration:
                    alive[i] = False

        sbi = [0]
        def head_gen(b, st):
            row0 = b * S + st * ST
            t0 = st * ST
            for tt in range(2):
                xi = xctr[0] % 2
                xctr[0] += 1
                xb = xbuf[xi]
                xn_ = f'xbuf{xi}'
                DMA('sync', f'ld_x{xi}', xb[:, :], x_d[row0 + tt * 128:row0 + (tt + 1) * 128, :], (), [xn_])
                ACT(junk[:, :], xb[:, :], AF.Square, [xn_], ['junk', 'stat0'], accum=stat[:, 0:1])
                ACT(stat[:, 1:2], stat[:, 0:1], AF.Sqrt, ['stat0'], ['stat1'], scale=1.0 / D, bias=RMS_EPS)
                RECIP(stat[:, 2:3], stat[:, 1:2], ['stat1'], ['stat2'])
                TS('dve', xb[:, :], xb[:, :], stat[:, 2:3], None, ALU.mult, None, [xn_, 'stat2'], [xn_])
                for c in range(8):
                    bk = B[c // 4]
                    TR(bk[:, (c % 4) * 128:(c % 4) * 128 + 128], xb[:, c * 128:(c + 1) * 128], ident, [xn_, 'cst'], [f'B{c // 4}'])
                for hb in range(2):
                    CP('act' if hb == 0 else 'dve', xnT[:, hb * 4:hb * 4 + 4, tt * 128:(tt + 1) * 128],
                       B[hb][:, :].rearrange("p (c t) -> p c t", t=128), [f'B{hb}'], ['xnT'])
                yield
            for half in range(2):
                ws, wsn = load_w(wbf_v, OFF_K + half * 256, 256, 'wbf')
                for f in range(2):
                    h = half * 2 + f
                    o, on = proj_ftile(ws, wsn, f * 128, 'A')
                    CP('act', kTc[:, h, t0:t0 + ST], o, [on], [('kTc', h * 4096 + t0, h * 4096 + t0 + ST)])
                    yield
            for half in range(2):
                ws, wsn = load_w(wbf_v, OFF_VA + half * 256, 256, 'wbf')
                for tt in range(2):
                    bk, bn = bank('A')
                    for c in range(8):
                        MM(bk[:, 0:256], xnT[:, c, tt * 128:(tt + 1) * 128], ws[:, c, 0:256], c == 0, c == 7, [wsn, 'xnT'], [bn])
                    kt = st * 2 + tt
                    CP('dve', Vc[:, kt, half * 256:half * 256 + 256], bk[:, 0:256], [bn], [('Vc', kt * 512, kt * 512 + 512)])
                    yield
            for half in range(2):
                ws, wsn = load_w(wbf_v, OFF_Q + half * 256, 256, 'wbf')
                for f in range(2):
                    h = half * 2 + f
                    o, on = proj_ftile(ws, wsn, f * 128, 'A')
                    CP('act', qT[0:64, h, 0, :], o[0:64, :], [on], ['qT'])
                    CP('dve', qT[64:128, h, 1, :], o[64:128, :], [on], ['qT'])
                    yield
            for half in range(2):
                ws, wsn = load_w(wbf_v, OFF_GA + half * 256, 256, 'wbf')
                for f in range(2):
                    h = half * 2 + f
                    o, on = proj_ftile(ws, wsn, f * 128, 'A')
                    ACT(sga[:, h, :], o, AF.Silu, [on], ['sga'])
                    yield

            yield

        def rwkv_gen(b, st):
            if st == 0:
                MEMSET('pool', S32[:, :], 0.0, ['S32'])
                MEMSET('pool', Sbf[sbi[0]][:, :], 0.0, [f'Sbf{sbi[0]}'])
                MEMSET('pool', carry[:, :], 0.0, ['carry'])
            ws, wsn = load_w(wbf_v, OFF_WDAD, 128, 'wbf')
            o, on = proj_ftile(ws, wsn, 0, 'R')
            shift_evac(o, on, 12, wdad[:, :], 'wdad')
            ACT(linbf[0:64, 0, :], wdad[0:64, :], AF.Tanh, ['wdad'], ['linbf'])
            CP('dve', linbf[64:128, 1, :], wdad[64:128, :], ['wdad'], ['linbf'])
            yield
            def prep_gen(p, tset):
                (sg, sgn), (aT, aTn), (cs, csn), (csp, cspn), (Egi, Egin), (kraw, krawn), (ksq, ksqn), (ka, kan) = tset
                Eg, Egn = sg, sgn
                Egp, Egpn = csp, cspn
                kk, kkn = ksq, ksqn
                rk = rkv[p % 2]
                rkn = f'rkv{p % 2}'
                ws, wsn = load_w(wbf_v, OFF_RKV + p * 384, 256, 'wbf')
                for j in range(2):
                    o, on = proj_ftile(ws, wsn, j * 128, 'R')
                    shift_evac(o, on, j * 4 + p, rk[:, j, :], rkn)
                    yield
                ws, wsn = load_w(wbf_v, OFF_RKV + p * 384 + 256, 128, 'wbf')
                o, on = proj_ftile(ws, wsn, 0, 'R')
                shift_evac(o, on, 2 * 4 + p, rk[:, 2, :], rkn)
                yield
                rs, ks, vs = rk[:, 0, :], rk[:, 1, :], rk[:, 2, :]
                bk, bn = bank('R')
                MM(bk[:, 0:256], lwbf[:, p * 128:(p + 1) * 128], linbf[:, 0, :], True, True, ['lwbf', 'linbf'], [bn])
                MM(bk[:, 256:512], lwbf[:, p * 128:(p + 1) * 128], linbf[:, 1, :], True, True, ['lwbf', 'linbf'], [bn])
                ACT(sg, bk[:, 0:256], AF.Sigmoid, [bn, 'pp'], [sgn], bias=pp[:, PP_W0 + p:PP_W0 + p + 1])
                ACT(aT, bk[:, 256:512], AF.Sigmoid, [bn, 'pp'], [aTn], bias=pp[:, PP_A0 + p:PP_A0 + p + 1])
                TS('dve', kraw, ks, pp[:, PP_KK + p:PP_KK + p + 1], None, ALU.mult, None, [rkn, 'pp'], [krawn])
                TT('pool', ksq, kraw, kraw, ALU.mult, [krawn], [ksqn])
                yield
                P.op('dve', lambda e, cs=cs, sg=sg: e.tensor_tensor_scan(out=cs, data0=scanmask, data1=sg, initial=0.0, op0=ALU.mult, op1=ALU.add), [sgn, 'cst'], [csn])
                bk2, bn2 = bank('R')
                MM(bk2[:, 0:256], blockones, ksq, True, True, ['cst', ksqn], [bn2])
                TS('dve', ksq, bk2[:, 0:256], 1e-24, None, ALU.max, None, [bn2], [ksqn])
                TT('pool', csp, cs, sg, ALU.subtract, [csn, sgn], [cspn])
                yield
                ACT(Eg, cs, AF.Exp, [csn], [Egn], scale=-C0)
                ACT(Egi, cs, AF.Exp, [csn], [Egin], scale=C0)
                ACT(Egp, csp, AF.Exp, [cspn], [Egpn], scale=-C0)
                ACT(ksq, ksq, AF.Sqrt, [ksqn], [ksqn])
                TS('dve', ka, aT, pp[:, PP_KA + p:PP_KA + p + 1], omka[:, p:p + 1], ALU.mult, ALU.add, [aTn, 'pp', 'omka'], [kan])
                TT('dve', ka, ks, ka, ALU.mult, [rkn, kan], [kan])
                yield
                CP('pool', gam[:, p, :], Eg.rearrange("p (c t) -> p c t", t=CH)[:, :, CH - 1], [Egn], ['gam'])
                RECIP(ksq, ksq, [ksqn], [ksqn])
                TT('dve', kk, kraw, ksq, ALU.mult, [krawn, ksqn], [kkn])
                STT(kraw, rs, pp[:, PP_RK + p:PP_RK + p + 1], ka, ALU.mult, ALU.mult, [rkn, kan, 'pp'], [krawn])
                bk3, bn3 = bank('R')
                MM(bk3[:, 0:256], blockones, kraw, True, True, ['cst', krawn], [bn3])
                TT('dve', bonus[:, p, :], bk3[:, 0:256], vs, ALU.mult, [bn3, rkn], ['bonus'])
                yield
                for e_ in range(2):
                    PR_ = slice(64 * e_, 64 * e_ + 64)
                    base = (p * 2 + e_) * 512
                    av = arT[PR_, base:base + 512].rearrange("p (c a t) -> p c a t", c=4, a=2)
                    TT('dve' if e_ == 0 else 'pool', av[:, :, 1, :], rs[PR_, :].rearrange("p (c t) -> p c t", t=CH),
                       Eg[PR_, :].rearrange("p (c t) -> p c t", t=CH), ALU.mult, [rkn, Egn], ['arT'])
                    STT(av[:, :, 0, :], kk[PR_, :].rearrange("p (c t) -> p c t", t=CH), -1.0,
                        Egp[PR_, :].rearrange("p (c t) -> p c t", t=CH), ALU.mult, ALU.mult, [kkn, Egpn], ['arT'])
                yield
                TT('pool', kk, kk, aT, ALU.mult, [kkn, aTn], [kkn])
                TT('dve', bcol(p, 0, ST), kk, Egi, ALU.mult, [kkn, Egin], ['bT'])
                TT('dve', kcol(p, 0, ST), ka, Egi, ALU.mult, [kan, Egin], ['ktT'])
                CP('pool', ycol(p, 0, ST), vs, [rkn], ['yT'])
                yield

            setA = [(tmp[i][:, :], f'tmp{i}') for i in range(8)]
            setB = [(xbuf[i // 4][:, (i % 4) * ST:(i % 4 + 1) * ST], (f'xbuf{i // 4}', (i % 4) * ST, (i % 4 + 1) * ST)) for i in range(8)]
            for pp_ in range(0, 4, 2):
                subs = [prep_gen(pp_, setA), prep_gen(pp_ + 1, setB)]
                live = [True, True]
                while any(live):
                    for si in range(2):
                        if live[si]:
                            try:
                                next(subs[si])
                            except StopIteration:
                                live[si] = False
                    yield
            for half in range(2):
                ws, wsn = load_w(wbf_v, OFF_GB + half * 256, 256, 'wbf')
                for f in range(2):
                    h = half * 2 + f
                    o, on = proj_ftile(ws, wsn, f * 128, 'R')
                    ACT(sgb[:, h, :], o, AF.Silu, [on], ['sgb'])
                    yield
            for c in range(0 if os.environ.get('KNOCHUNK') else 4):
                c0 = c * CH
                cres = ('yT', c0, c0 + CH)
                for p in range(4):
                    TR(PTB[:, p * 128:(p + 1) * 128], bcol(p, c0, 128), identbf[:, :], ['bT', 'identbf'], ['PTB'])
                    TR(PTB[:, 512 + p * 128:512 + (p + 1) * 128], kcol(p, c0, 128), identbf[:, :], ['ktT', 'identbf'], ['PTB'])
                CP('act', bkTF[0:64, :], PTB[0:64, :], ['PTB'], ['bkTF'])
                for p in range(4):
                    TR(B[4][:, p * 128:(p + 1) * 128], ycol(p, c0, 128), ident, [cres, 'cst'], ['B4'])
                CP('dve', VTF[0:64, :], B[4][0:64, :], ['B4'], ['VTF'])
                yield
                mA = maskA.unsqueeze(1).broadcast_to([64, 4, 128])
                for h in range(8):
                    p, e = h // 2, h % 2
                    MM(B[5 + h // 4][:, (h % 4) * 128:(h % 4) * 128 + 128], bcol(p, c0, 128), ar(p, e, c, 0, 128), True, True, ['bT', 'arT'], [f'B{5 + h // 4}'])
                for hf in range(2):
                    TT('dve', MNb[0:64, hf * 4:hf * 4 + 4, :], B[5 + hf][0:64, :].rearrange("p (h t) -> p h t", t=128), mA, ALU.mult, [f'B{5 + hf}', 'cst'], ['MNb'])
                for h in range(8):
                    p, e = h // 2, h % 2
                    MM(B[4][:, h * 64:(h + 1) * 64], ar(p, e, c, 0, 128), bcol(p, c0, 64), True, True, ['bT', 'arT'], ['B4'])
                QT, QTn = Q_()
                mC = maskC.unsqueeze(1).broadcast_to([64, 8, 64])
                TT('dve', QT[0:64, 0:8, :], B[4][0:64, :].rearrange("p (h t) -> p h t", t=64), mC, ALU.mult, ['B4', 'cst'], [QTn])
                yield
                for h in range(8):
                    p, e = h // 2, h % 2
                    MM(B[5 + h // 4][:, (h % 4) * 128:(h % 4) * 128 + 128], kcol(p, c0, 128), ar(p, e, c, 0, 128), True, True, ['ktT', 'arT'], [f'B{5 + h // 4}'])
                for hf in range(2):
                    TT('dve', MNk[0:64, hf * 4:hf * 4 + 4, :], B[5 + hf][0:64, :].rearrange("p (h t) -> p h t", t=128), mA, ALU.mult, [f'B{5 + hf}', 'cst'], ['MNk'])
                ti = 0
                Tc, Tn = Tb[0], 'Tb0'
                TT('pool', Tc[0:64, 0:8, :], MNb[0:64, 0:8, 0:64], ident[0:64, 0:64].unsqueeze(1).broadcast_to([64, 8, 64]), ALU.add, ['MNb', 'cst'], [Tn])
                yield
                Qr = lambda h: MNb[:, h, 0:64]
                Ql = lambda h: MNbf[:, h * 128:h * 128 + 128]
                Qn = 'MNb'
                QTf = QT[:, :, :].rearrange("p h t -> p (h t)")
                QTr = lambda h, QT=QT: QT[:, h, :]
                QTl = lambda h, QTf=QTf: QTf[:, h * 64:h * 64 + 128]
                pend = None
                for lvl in range(1, 7):
                    need_q = lvl <= 4
                    need_qt = lvl <= 5
                    if need_q:
                        for h in range(8):
                            MM(B[4][:, h * 64:(h + 1) * 64], QTl(h), Qr(h), True, True, [QTn, Qn], ['B4'])
                    if need_qt:
                        for h in range(8):
                            MM(B[5][:, h * 64:(h + 1) * 64], Ql(h), QTr(h), True, True, [QTn, Qn], ['B5'])
                    if pend is not None:
                        pl, pn = pend
                        Tf = Tc[:, :, :].rearrange("p h t -> p (h t)")
                        for h in range(8):
                            MM(B[6][:, h * 64:(h + 1) * 64], pl(h), Tc[:, h, :], True, True, [pn, Tn], ['B6'])
                    if need_q:
                        Q2, Q2n = Q_()
                        CP('act', Q2[0:64, 0:8, :], B[4][0:64, :].rearrange("p (h t) -> p h t", t=64), ['B4'], [Q2n])
                    if need_qt:
                        QT2, QT2n = Q_()
                        CP('act', QT2[0:64, 0:8, :], B[5][0:64, :].rearrange("p (h t) -> p h t", t=64), ['B5'], [QT2n])
                    if pend is not None:
                        ti ^= 1
                        Tnew, Tnn = Tb[ti], f'Tb{ti}'
                        TT('dve', Tnew[0:64, 0:8, :], B[6][0:64, :].rearrange("p (h t) -> p h t", t=64), Tc[0:64, 0:8, :], ALU.add, ['B6', Tn], [Tnn])
                        Tc, Tn = Tnew, Tnn
                    if need_qt:
                        QT2f = QT2[:, :, :].rearrange("p h t -> p (h t)")
                        pend = ((lambda h, f_=QT2f: f_[:, h * 64:h * 64 + 128]), QT2n)
                        QTr = lambda h, QT2=QT2: QT2[:, h, :]
                        QTl = pend[0]
                        QTn = QT2n
                    else:
                        pend = None
                    if need_q:
                        Q2f = Q2[:, :, :].rearrange("p h t -> p (h t)")
                        Qr = lambda h, Q2=Q2: Q2[:, h, :]
                        Ql = lambda h, f_=Q2f: f_[:, h * 64:h * 64 + 128]
                        Qn = Q2n
                    yield
                Tf = Tc[:, :, :].rearrange("p h t -> p (h t)")
                So, Son = Sbf[sbi[0]], f'Sbf{sbi[0]}'
                for h in range(8):
                    p, e = h // 2, h % 2
                    MM(B[4][:, h * 64:(h + 1) * 64], ar(p, e, c, 0, 128), So[:, p * 128 + e * 64:p * 128 + e * 64 + 64], True, False, ['arT', Son], ['B4'])
                    MM(B[4][:, h * 64:(h + 1) * 64], MNkf[:, h * 128:h * 128 + 128], VTF[:, h * 64:(h + 1) * 64], False, True, ['MNk', 'VTF'], ['B4'])
                CP('act', Wsb[0:64, :], B[4][0:64, :], ['B4'], ['Wsb'])
                yield
                for h in range(8):
                    MM(B[5][:, h * 64:(h + 1) * 64], Tf[:, h * 64:h * 64 + 128], Wsb[:, h * 64:(h + 1) * 64], True, True, [Tn, 'Wsb'], ['B5'])
                CP('act', Usb[0:64, :], B[5][0:64, :], ['B5'], ['Usb'])
                yield
                for h in range(8):
                    p, e = h // 2, h % 2
                    o = B[6][:, h * 64:(h + 1) * 64]
                    MM(o, ar(p, e, c, 1, 128), So[:, p * 128 + e * 64:p * 128 + e * 64 + 64], True, False, ['arT', Son], ['B6'])
                    MM(o, MNbf[:, h * 128 + 64:h * 128 + 192], Usb[:, h * 64:(h + 1) * 64], False, False, ['MNb', 'Usb'], ['B6'])
                    MM(o, MNkf[:, h * 128 + 64:h * 128 + 192], VTF[:, h * 64:(h + 1) * 64], False, True, ['MNk', 'VTF'], ['B6'])
                CP('act', Ytf[0:64, :], B[6][0:64, :], ['B6'], ['Ytf'])
                for p in range(4):
                    o = B[4][:, p * 128:(p + 1) * 128]
                    MM(o, bkTF[:, p * 128:(p + 1) * 128], Usb[:, p * 128:(p + 1) * 128], True, False, ['bkTF', 'Usb'], ['B4'])
                    MM(o, bkTF[:, 512 + p * 128:512 + (p + 1) * 128], VTF[:, p * 128:(p + 1) * 128], False, True, ['bkTF', 'VTF'], ['B4'])
                TT('dve', tmpS[:, :], B[4][:, :], S32[:, :], ALU.add, ['B4', 'S32'], ['tmpS'])
                gbc = gam[:, :, c:c + 1].broadcast_to([128, 4, 128])
                tS3 = tmpS[:, :].rearrange("p (a b) -> p a b", b=128)
                TT('dve', S32[:, :].rearrange("p (a b) -> p a b", b=128), tS3, gbc, ALU.mult, ['tmpS', 'gam'], ['S32'])
                sbi[0] ^= 1
                TT('pool', Sbf[sbi[0]][:, :].rearrange("p (a b) -> p a b", b=128), tS3, gbc, ALU.mult, ['tmpS', 'gam'], [f'Sbf{sbi[0]}'])
                yield
                for p in range(4):
                    TR(B[5][:, p * 128:(p + 1) * 128], Ytf[:, p * 128:(p + 1) * 128], ident, ['Ytf', 'cst'], ['B5'])
                for p in range(4):
                    CP('dve' if p % 2 == 0 else 'act', ycol(p, c0, CH), B[5][:, p * 128:p * 128 + CH], ['B5'], [cres])
                yield

        def tail_gen(b, st):
            row0 = b * S + st * ST
            def epi_gen(p):
                y = ycol(p, 0, ST)
                (sq, sqn), (m, mn), (msq, msqn) = [(tmp[3 * p + i][:, :], f'tmp{3 * p + i}') for i in range(3)]
                bk, bn = bank('R')
                MM(bk[:, 0:256], blockones, y, True, True, ['cst', 'yT'], [bn])
                TT('pool', sq, y, y, ALU.mult, ['yT'], [sqn])
                MM(bk[:, 256:512], blockones, sq, True, True, ['cst', sqn], [bn])
                TS('dve', m, bk[:, 0:256], 1.0 / 64, None, ALU.mult, None, [bn], [mn])
                TT('pool', msq, m, m, ALU.mult, [mn], [msqn])
                STT(msq, bk[:, 256:512], 1.0 / 64, msq, ALU.mult, ALU.subtract, [bn, msqn], [msqn])
                yield
                ACT(msq, msq, AF.Sqrt, [msqn], [msqn], bias=GN_EPS)
                RECIP(msq, msq, [msqn], [msqn])
                TT('dve', m, y, m, ALU.subtract, ['yT', mn], [mn])
                yield
                TT('dve', m, m, msq, ALU.mult, [mn, msqn], [mn])
                TS('dve', m, m, pp[:, PP_LNG + p:PP_LNG + p + 1], pp[:, PP_LNB + p:PP_LNB + p + 1], ALU.mult, ALU.add, [mn, 'pp'], [mn])
                TT('pool', m, m, bonus[:, p, :], ALU.add, [mn, 'bonus'], [mn])
                TT('dve', mixT[:, 4 + p, :], m, sgb[:, p, :], ALU.mult, [mn, 'sgb'], ['mixT'])
                yield

            subs = [epi_gen(p) for p in range(4)]
            live = [True] * 4
            while any(live):
                for si in range(4):
                    if live[si]:
                        try:
                            next(subs[si])
                        except StopIteration:
                            live[si] = False
                yield
            for tt in range(2):
                DMA('sync', f'ld_h{tt}', hres[tt][:, :], x_d[row0 + tt * 128:row0 + (tt + 1) * 128, :], (), [f'hres{tt}'])
            for n in range(4):
                ws, wsn = load_w(wobf_v, n * 256, 256, 'wobf')
                for tt in range(2):
                    bk, bn = bank('A')
                    for f in range(8):
                        MM(bk[:, 0:256], mixT[:, f, tt * 128:(tt + 1) * 128], ws[:, f, 0:256], f == 0, f == 7, ['mixT', wsn], [bn])
                    TT('dve', hres[tt][:, n * 256:(n + 1) * 256], bk[:, 0:256], hres[tt][:, n * 256:(n + 1) * 256], ALU.add, [bn, f'hres{tt}'], [f'hres{tt}'])
                    yield
            for tt in range(2):
                ACT(junk[:, :], hres[tt][:, :], AF.Square, [f'hres{tt}'], ['junk', 'stat4'], accum=stat[:, 4:5])
                ACT(stat[:, 5:6], stat[:, 4:5], AF.Sqrt, ['stat4'], ['stat5'], scale=1.0 / D, bias=RMS_EPS)
                RECIP(stat[:, 6:7], stat[:, 5:6], ['stat5'], ['stat6'])
                STT(hres[tt][:, :], hres[tt][:, :], stat[:, 6:7], fg[:, :], ALU.mult, ALU.mult, [f'hres{tt}', 'stat6', 'fg'], [f'hres{tt}'])
                DMA('pool', f'st_{tt}', out_d[row0 + tt * 128:row0 + (tt + 1) * 128, :], hres[tt][:, :], [f'hres{tt}'], ['out'])

            yield

        def att_gen(b, st):
            nkt = 2 * st + 2

            items = [(h, kt) for h in range(0 if os.environ.get('KNOATT') else 4) for kt in range(nkt)]

            def emit_scores(i):
                h, kt = items[i]
                j = kt - 2 * st
                col0 = 128 * max(j, 0)
                bk, bn = B[i % 2], f'B{i % 2}'
                kres = ('kTc', h * 4096 + kt * 128, h * 4096 + kt * 128 + 128)
                if col0 == 0:
                    MM(bk[:, :], kTc[:, h, kt * 128:(kt + 1) * 128], qT[:, h, :, :].rearrange("p m q -> p (m q)"), True, True, [kres, 'qT'], [bn])
                else:
                    for m_ in range(2):
                        MM(bk[:, m_ * ST + col0:(m_ + 1) * ST], kTc[:, h, kt * 128:(kt + 1) * 128], qT[:, h, m_, col0:ST], True, True, [kres, 'qT'], [bn])

            if items:
                emit_scores(0)
            for i, (h, kt) in enumerate(items):
                if i + 1 < len(items):
                    emit_scores(i + 1)
                j = kt - 2 * st
                col0 = 128 * max(j, 0)
                bk, bn = B[i % 2], f'B{i % 2}'
                pt = Pt[i % 2]
                ptn = f'Pt{i % 2}'
                sview = bk[:, :].rearrange("p (m q) -> p m q", q=ST)[:, :, col0:ST]
                pview = pt[:, :, col0:ST]
                ACT(pview, sview, AF.Exp, [bn], [ptn], scale=0.125)
                if j >= 0:
                    MEMSET('pool', pt[64:128, :, col0:col0 + 64], 0.0, [ptn])
                vv = Vc[:, kt, h * 128:(h + 1) * 128]
                vres = ('Vc', kt * 512, kt * 512 + 512)
                last = kt == nkt - 1
                if col0 == 0:
                    p2 = pt[:, :, :].rearrange("p m q -> p (m q)")
                    MM(B[2][:, :], vv, p2, kt == 0, last, [vres, ptn], ['B2'])
                    MM(B[3][:, :], onesbf[:, :], p2, kt == 0, last, ['onesbf', ptn], ['B3'])
                else:
                    for m_ in range(2):
                        MM(B[2][:, m_ * ST + col0:(m_ + 1) * ST], vv, pt[:, m_, col0:ST], False, last and m_ == 1, [vres, ptn], ['B2'])
                    for m_ in range(2):
                        MM(B[3][:, m_ * ST + col0:(m_ + 1) * ST], onesbf[:, :], pt[:, m_, col0:ST], False, last and m_ == 1, ['onesbf', ptn], ['B3'])
                if last:
                    a1 = att1[:, :, :].rearrange("p m q -> p (m q)")
                    a2 = att2[:, :, :].rearrange("p m q -> p (m q)")
                    RECIP(a1, B[3][:, :], ['B3'], ['att1'])
                    TT('dve', a2, B[2][:, :], a1, ALU.mult, ['B2', 'att1'], ['att2'])
                    o_, on_ = att1[:, 0, :], 'att1'
                    STT(o_, att2[:, 1, :], neglam[:, 0:1], att2[:, 0, :], ALU.mult, ALU.add, ['att2', 'neglam'], [on_])
                    sq = att1[:, 1, :]
                    TT('pool', sq, o_, o_, ALU.mult, [on_], [on_])
                    sbk, sbn = B[i % 2], f'B{i % 2}'
                    MM(sbk[:, 0:256], ones32[:, :], sq, True, True, ['ones32', on_], [sbn])
                    ACT(sq, sbk[:, 0:256], AF.Sqrt, [sbn], [on_], scale=1.0 / 128, bias=SUBLN_EPS)
                    RECIP(sq, sq, [on_], [on_])
                    STT(o_, o_, g08[:, 0:1], sq, ALU.mult, ALU.mult, [on_, 'g08'], [on_])
                    TT('dve', mixT[:, h, :], o_, sga[:, h, :], ALU.mult, [on_, 'sga'], ['mixT'])
                yield


        items_ = [(b, st) for b in range(NB) for st in range(NST)]
        merge([(head_gen(*items_[0]), 1)])
        for ii, (b, st) in enumerate(items_):
            merge([(rwkv_gen(b, st), 70), (att_gen(b, st), 4 * (2 * st + 2))])
            gl = [(tail_gen(b, st), 15)]
            if ii + 1 < len(items_):
                gl.append((head_gen(*items_[ii + 1]), 23))
            merge(gl)

        P.final_wait('pool', ['st_0', 'st_1'])

        with nc.Block() as block:
            @block.sync
            def _(e):
                for f in P.engs['sync']['thunks']:
                    f(e)

            @block.tensor
            def _(e):
                for f in P.engs['pe']['thunks']:
                    f(e)

            @block.scalar
            def _(e):
                for f in P.engs['act']['thunks']:
                    f(e)

            @block.vector
            def _(e):
                for f in P.engs['dve']['thunks']:
                    f(e)

            @block.gpsimd
            def _(e):
                for f in P.engs['pool']['thunks']:
                    f(e)
    return nc


def make_consts():
    cst = np.zeros((128, 832), np.float32)
    cst[:, 0:128] = np.eye(128, dtype=np.float32)
    cst[0:64, 128:192] = 1.0
    cst[64:128, 192:256] = 1.0
    j = np.arange(64)[:, None]
    t = np.arange(64)[None, :]
    cst[0:64, 256:320] = (j < t)
    cst[0:64, 320:384] = (j <= t)
    cst[0:64, 384:448] = (j > t)
    sm = np.ones(256, np.float32)
    sm[::64] = 0.0
    cst[:, 576:832] = sm[None, :]
    return cst


def prep_params(norm_g, lambda_q1, lambda_k1, lambda_q2, lambda_k2, subln_g, shift_mu, w0, w2, a0, a2,
                k_k, k_a, r_k, ln_x_g, ln_x_b, final_g):
    f = lambda v, n: np.ascontiguousarray(np.asarray(v, np.float32).reshape(n, 128).T)
    pp = np.zeros((128, PP_N), np.float32)
    pp[:, PP_MU:PP_MU + 13] = f(shift_mu[0], 13)
    pp[:, PP_W0:PP_W0 + 4] = f(w0[0], 4)
    pp[:, PP_A0:PP_A0 + 4] = f(a0[0], 4)
    pp[:, PP_KK:PP_KK + 4] = f(k_k[0], 4)
    pp[:, PP_KA:PP_KA + 4] = f(k_a[0], 4)
    pp[:, PP_RK:PP_RK + 4] = f(r_k[0].reshape(-1), 4)
    pp[:, PP_LNG:PP_LNG + 4] = f(ln_x_g[0], 4)
    pp[:, PP_LNB:PP_LNB + 4] = f(ln_x_b[0], 4)
    pp[:, PP_SUB] = np.asarray(subln_g[0], np.float32)
    pp[:, PP_NG:PP_NG + 8] = f(norm_g[0], 8)
    lam4 = np.concatenate([np.broadcast_to(np.asarray(v[0], np.float32)[None, :], (128, 64))
                           for v in (lambda_q1, lambda_k1, lambda_q2, lambda_k2)], axis=1)
    lw = np.concatenate([np.asarray(w2[0], np.float32), np.asarray(a2[0], np.float32)], axis=0)
    fg = np.broadcast_to(np.asarray(final_g, np.float32)[None, :], (128, D))
    return dict(pp=pp, lam4=np.ascontiguousarray(lam4), lw=np.ascontiguousarray(lw), fg=np.ascontiguousarray(fg),
                cst=make_consts())


_CACHE = {}


def run(x, w_in, w_out, small, n_cores):
    x = np.asarray(x, np.float32)
    Bt, S, _ = x.shape
    NB = Bt // n_cores
    key = (NB, S)
    if key not in _CACHE:
        _CACHE[key] = build(NB, S)
    nc = _CACHE[key]
    w_in2 = np.ascontiguousarray(np.asarray(w_in, np.float32)[0])
    w_out2 = np.ascontiguousarray(np.asarray(w_out, np.float32)[0])
    in_maps = []
    for c in range(n_cores):
        m = dict(small)
        m["x"] = np.ascontiguousarray(x[c * NB:(c + 1) * NB].reshape(NB * S, D))
        m["w_in"] = w_in2
        m["w_out"] = w_out2
        in_maps.append(m)
    res = run_bass_kernel_spmd(nc, in_maps, core_ids=list(range(n_cores)))
    outs = [np.asarray(r["out"]).reshape(NB, S, D) for r in res.results]
    return np.concatenate(outs, axis=0).astype(np.float32)


def kernel(x, norm_g, w_in, lambda_q1, lambda_k1, lambda_q2, lambda_k2, subln_g,
           shift_mu, w0, w2, a0, a2, k_k, k_a, r_k, ln_x_g, ln_x_b, w_out, final_g):
    small = prep_params(norm_g, lambda_q1, lambda_k1, lambda_q2, lambda_k2, subln_g, shift_mu, w0, w2, a0, a2,
                        k_k, k_a, r_k, ln_x_g, ln_x_b, final_g)
    return run(x, w_in, w_out, small, 8)
```

```python
import os
import numpy as np
import concourse.bass as bass
import concourse.mybir as mybir
from concourse.bass_utils import run_bass_kernel_spmd

F32 = mybir.dt.float32
BF16 = mybir.dt.bfloat16
AF = mybir.ActivationFunctionType
ALU = mybir.AluOpType
AX = mybir.AxisListType

D = 1024
DIN = 4224
ST = 256
CH = 64
C0 = 0.6065306597126334
RMS_EPS = 1e-6
SUBLN_EPS = 1e-5
GN_EPS = 1e-5 * 64
LAM_INIT = 0.2
BIG = 1 << 40

OFF_Q, OFF_K, OFF_VA, OFF_GA, OFF_WDAD, OFF_RKV, OFF_GB = 0, 512, 1024, 1536, 2048, 2176, 3712

PP_MU, PP_W0, PP_A0, PP_KK, PP_KA, PP_RK, PP_LNG, PP_LNB, PP_SUB, PP_NG, PP_N = 0, 13, 17, 21, 25, 29, 33, 37, 41, 42, 50


class Prog:
    def __init__(self):
        self.engs = {}
        self.acc = {}

    def add(self, name, sem, real=True):
        self.engs[name] = dict(sem=sem, count=0, thunks=[], waited={}, real=real)

    @staticmethod
    def _norm(r):
        if isinstance(r, str):
            return (r, 0, BIG)
        return r

    def op(self, eng, fn, reads=(), writes=(), dma=None):
        self.nops = getattr(self, 'nops', 0) + 1
        if self.nops > int(os.environ.get('KSTOP', '100000000')):
            return
        if os.environ.get('KLOG'):
            import inspect
            fr = inspect.stack()
            print('OP', self.nops, eng, [f.lineno for f in fr[1:4]], flush=True)
        E = self.engs[eng]
        deps = {}
        reads = [self._norm(r) for r in reads]
        writes = [self._norm(r) for r in writes]
        for (n, lo, hi) in reads:
            for (l2, h2, k, e, i) in self.acc.get(n, ()):
                if k == 'w' and l2 < hi and lo < h2:
                    deps[e] = max(deps.get(e, 0), i)
        for (n, lo, hi) in writes:
            for (l2, h2, k, e, i) in self.acc.get(n, ()):
                if l2 < hi and lo < h2:
                    deps[e] = max(deps.get(e, 0), i)
        for e, i in deps.items():
            if e == eng and eng == 'pe':
                continue
            if E['waited'].get(e, 0) >= i:
                continue
            sem = self.engs[e]['sem']
            if self.engs[e].get('unordered'):
                E['waited'][e] = BIG
                E['thunks'].append(lambda en, s=sem, d=self.engs[e]: en.wait_ge(s, d['count']))
            else:
                E['waited'][e] = i
                E['thunks'].append(lambda en, s=sem, v=i: en.wait_ge(s, v))
        ce = dma if dma else eng
        CE = self.engs[ce]
        inc = 16 if dma else 1
        CE['count'] += inc
        idx = CE['count']
        E['thunks'].append(lambda en, f=fn, s=CE['sem'], i=inc: f(en).then_inc(s, i))
        for (n, lo, hi) in writes:
            lst = self.acc.setdefault(n, [])
            lst[:] = [a for a in lst if not (lo <= a[0] and a[1] <= hi)]
            lst.append((lo, hi, 'w', ce, idx))
        for (n, lo, hi) in reads:
            lst = self.acc.setdefault(n, [])
            lst[:] = [a for a in lst if not (a[2] == 'r' and a[3] == ce and lo <= a[0] and a[1] <= hi)]
            lst.append((lo, hi, 'r', ce, idx))

    def final_wait(self, eng, names):
        E = self.engs[eng]
        for n in names:
            c = self.engs[n]['count']
            if c > 0:
                E['thunks'].append(lambda en, s=self.engs[n]['sem'], v=c: en.wait_ge(s, v))


def build(NB, S, dbg=False):
    NT = S // 128
    NST = S // ST
    nc = bass.Bass("TRN2", target_bir_lowering=False)
    dt = nc.dram_tensor
    x_d = dt("x", [NB * S, D], F32, kind="ExternalInput").ap()
    win_d = dt("w_in", [D, DIN], F32, kind="ExternalInput").ap()
    wout_d = dt("w_out", [D, D], F32, kind="ExternalInput").ap()
    pp_d = dt("pp", [128, PP_N], F32, kind="ExternalInput").ap()
    lam_d = dt("lam4", [128, 256], F32, kind="ExternalInput").ap()
    lw_d = dt("lw", [128, 512], F32, kind="ExternalInput").ap()
    fg_d = dt("fg", [128, D], F32, kind="ExternalInput").ap()
    cst_d = dt("cst", [128, 832], F32, kind="ExternalInput").ap()
    out_d = dt("out", [NB * S, D], F32, kind="ExternalOutput").ap()
    wbf_d = dt("wbf", [D, DIN], BF16, kind="ExternalOutput").ap()
    wobf_d = dt("wobf", [D, D], BF16, kind="ExternalOutput").ap()

    sb = nc.alloc_sbuf_tensor
    kTc = sb("kTc", [128, 4, 4096], BF16)
    Vc = sb("Vc", [128, 32, 512], BF16)
    wslot = [sb(f"wslot{i}", [128, 8, 256], BF16) for i in range(5)]
    pp = sb("pp_sb", [128, PP_N], F32)
    omu = sb("omu", [128, 13], F32)
    omka = sb("omka", [128, 4], F32)
    g08 = sb("g08", [128, 1], F32)
    neglam = sb("neglam", [128, 1], F32)
    lamw = sb("lamw", [128, 8], F32)
    lw32 = sb("lw32", [128, 512], F32)
    lwbf = sb("lwbf", [128, 512], BF16)
    fg = sb("fg_sb", [128, D], F32)
    cst = sb("cst_sb", [128, 832], F32)
    ident = cst[:, 0:128]
    blockones = cst[:, 128:256]
    maskA = cst[0:64, 256:384]
    maskC = cst[0:64, 384:448]
    scanmask = cst[:, 576:832]
    identbf = sb("identbf", [128, 128], BF16)
    onesbf = sb("onesbf", [128, 128], BF16)
    ones32 = sb("ones32", [128, 128], F32)
    xbuf = [sb(f"xbuf{i}", [128, D], F32) for i in range(2)]
    hres = [sb(f"hres{i}", [128, D], F32) for i in range(2)]
    stat = sb("stat", [128, 16], F32)
    junk = sb("junk", [128, D], F32)
    xnT = sb("xnT", [128, 8, ST], BF16)
    qT = sb("qT", [128, 4, 2, ST], BF16)
    sga = sb("sga", [128, 4, ST], BF16)
    sgb = sb("sgb", [128, 4, ST], BF16)
    mixT = sb("mixT", [128, 8, ST], BF16)
    rkv = [sb(f"rkv{i}", [128, 3, ST], F32) for i in range(2)]
    wdad = sb("wdad", [128, ST], F32)
    linbf = sb("linbf", [128, 2, ST], BF16)
    tmpm = [sb(f"tmpm{i}", [128, ST + 2], F32) for i in range(2)]
    carry = sb("carry", [128, 16], F32)
    NTMP = 12
    tmp = [sb(f"tmp{i}", [128, ST], F32) for i in range(NTMP)]
    lamt = tmp[0]
    arT = sb("arT", [128, 4 * 2 * 4 * 2 * CH + CH], BF16)
    bT = sb("bT", [128, 4 * ST + CH], BF16)
    ktT = sb("ktT", [128, 4 * ST + CH], BF16)
    bonus = sb("bonus", [128, 4, ST], F32)
    gam = sb("gam", [128, 4, 4], F32)
    yT = sb("yT", [128, 4 * ST + CH], F32)
    MNb = sb("MNb", [128, 9, 128], BF16)
    MNk = sb("MNk", [128, 9, 128], BF16)
    NQ = 6
    qbuf = [sb(f"qbuf{i}", [128, 9, 64], BF16) for i in range(NQ)]
    Tb = [sb(f"Tb{i}", [128, 9, 64], BF16) for i in range(2)]
    bkTF = sb("bkTF", [128, 1024], BF16)
    VTF = sb("VTF", [128, 512], BF16)
    Wsb = sb("Wsb", [128, 512], BF16)
    Usb = sb("Usb", [128, 512], BF16)
    S32 = sb("S32", [128, 512], F32)
    Sbf = [sb(f"Sbf{i}", [128, 512], BF16) for i in range(2)]
    tmpS = sb("tmpS", [128, 512], F32)
    Ytf = sb("Ytf", [128, 512], F32)
    Pt = [sb(f"Pt{i}", [128, 2, ST], BF16) for i in range(2)]
    att1 = sb("att1", [128, 2, ST], F32)
    att2 = sb("att2", [128, 2, ST], F32)

    ps = nc.alloc_psum_tensor
    B = [ps(f"B{i}", [128, 512], F32) for i in range(7)]
    PTB = ps("PTB", [128, 1024], BF16)

    P = Prog()
    import contextlib
    es = contextlib.ExitStack()
    with es:
        def sem(n):
            return es.enter_context(nc.semaphore(n))
        for e in ('pe', 'act', 'dve', 'pool', 'sync'):
            P.add(e, sem("s_" + e))
        slots = ['ld_misc', 'ld_x0', 'ld_x1', 'ld_w0', 'ld_w1', 'ld_w2', 'ld_w3', 'ld_w4', 'ld_h0', 'ld_h1', 'st_0', 'st_1', 'st_w0', 'st_w1', 'ld_s0', 'ld_s1']
        for s_ in slots:
            P.add(s_, sem(s_), real=False)
        P.engs['ld_misc']['unordered'] = True

        def ACT(out, in_, func, r, w, scale=1.0, bias=0.0, accum=None):
            if accum is None:
                P.op('act', lambda e: e.activation(out=out, in_=in_, func=func, scale=scale, bias=bias), r, w)
            else:
                P.op('act', lambda e: e.activation(out=out, in_=in_, func=func, scale=scale, bias=bias, accum_out=accum), r, w)

        def CP(eng, out, in_, r, w):
            if eng == 'act':
                P.op('act', lambda e: e.activation(out=out, in_=in_, func=AF.Copy), r, w)
            else:
                P.op(eng, lambda e: e.tensor_copy(out=out, in_=in_), r, w)

        def TT(eng, out, in0, in1, op, r, w):
            P.op(eng, lambda e: e.tensor_tensor(out=out, in0=in0, in1=in1, op=op), r, w)

        def TS(eng, out, in0, s1, s2, op0, op1, r, w):
            if s2 is None:
                P.op(eng, lambda e: e.tensor_scalar(out=out, in0=in0, scalar1=s1, scalar2=None, op0=op0), r, w)
            else:
                P.op(eng, lambda e: e.tensor_scalar(out=out, in0=in0, scalar1=s1, scalar2=s2, op0=op0, op1=op1), r, w)

        def STT(out, in0, scalar, in1, op0, op1, r, w):
            P.op('dve', lambda e: e.scalar_tensor_tensor(out=out, in0=in0, scalar=scalar, in1=in1, op0=op0, op1=op1), r, w)

        def RECIP(out, in_, r, w):
            P.op('dve', lambda e: e.reciprocal(out=out, in_=in_), r, w)

        pemode = [None]

        def _rnd(v):
            return 32 if v <= 32 else (64 if v <= 64 else 128)

        def pe_mode(lhsT):
            K_ = lhsT.shape[0]
            M_ = 1
            for d_ in lhsT.shape[1:]:
                M_ *= d_
            md = (_rnd(K_), _rnd(M_))
            if pemode[0] is not None and pemode[0] != md:
                E = P.engs['pe']
                if E['count'] > 0:
                    E['thunks'].append(lambda en, s=E['sem'], v=E['count']: en.wait_ge(s, v))
                    P.ndrain = getattr(P, 'ndrain', 0) + 1
                    if os.environ.get('KLOGDRAIN'):
                        print('DRAIN', pemode[0], '->', md, flush=True)
            pemode[0] = md

        def MM(out, lhsT, rhs, start, stop, r, w):
            pe_mode(lhsT)
            P.op('pe', lambda e: e.matmul(out, lhsT=lhsT, rhs=rhs, start=start, stop=stop), r, w)

        def TR(out, in_, idn, r, w):
            pe_mode(in_)
            P.op('pe', lambda e: e.transpose(out, in_, idn), r, w)

        def DMA(queue, slot, out, in_, r, w):
            P.op(queue, lambda e: e.dma_start(out=out, in_=in_), r, w, dma=slot)

        def MEMSET(eng, ap, val, w):
            P.op(eng, lambda e: e.memset(ap, val), (), w)

        DMA('sync', 'ld_misc', pp[:, :], pp_d[:, :], (), ['pp'])
        DMA('sync', 'ld_misc', lamt[:, :], lam_d[:, :], (), ['tmp0'])
        DMA('sync', 'ld_misc', lw32[:, :], lw_d[:, :], (), ['lw32'])
        DMA('sync', 'ld_misc', fg[:, :], fg_d[:, :], (), ['fg'])
        DMA('sync', 'ld_misc', cst[:, :], cst_d[:, :], (), ['cst'])
        TS('dve', omu[:, :], pp[:, PP_MU:PP_MU + 13], -1.0, 1.0, ALU.mult, ALU.add, ['pp'], ['omu'])
        TS('dve', omka[:, :], pp[:, PP_KA:PP_KA + 4], -1.0, 1.0, ALU.mult, ALU.add, ['pp'], ['omka'])
        TS('dve', g08[:, :], pp[:, PP_SUB:PP_SUB + 1], 1.0 - LAM_INIT, None, ALU.mult, None, ['pp'], ['g08'])
        CP('dve', lwbf[:, :], lw32[:, :], ['lw32'], ['lwbf'])
        CP('dve', identbf[:, :], ident, ['cst'], ['identbf'])
        MEMSET('pool', onesbf[:, :], 1.0, ['onesbf'])
        MEMSET('pool', ones32[:, :], 1.0, ['ones32'])
        for t_, n_ in [(qT, 'qT'), (linbf, 'linbf'), (arT, 'arT'), (MNb, 'MNb'), (MNk, 'MNk'), (bkTF, 'bkTF'), (VTF, 'VTF'),
                       (Wsb, 'Wsb'), (Usb, 'Usb'), (Ytf, 'Ytf'), (Tb[0], 'Tb0'), (Tb[1], 'Tb1'), (bT, 'bT'), (ktT, 'ktT'), (yT, 'yT')] + [(qbuf[i], f'qbuf{i}') for i in range(NQ)]:
            nd = len(t_.shape)
            MEMSET('pool', t_[(slice(None),) * nd], 0.0, [n_])
        TT('dve', lamt[:, 0:64], lamt[:, 0:64], lamt[:, 64:128], ALU.mult, ['tmp0'], ['tmp0'])
        TT('dve', lamt[:, 128:192], lamt[:, 128:192], lamt[:, 192:256], ALU.mult, ['tmp0'], ['tmp0'])
        P.op('dve', lambda e: e.tensor_reduce(out=lamw[:, 0:1], in_=lamt[:, 0:64], axis=AX.X, op=ALU.add), ['tmp0'], ['lamw'])
        P.op('dve', lambda e: e.tensor_reduce(out=lamw[:, 1:2], in_=lamt[:, 128:192], axis=AX.X, op=ALU.add), ['tmp0'], ['lamw'])
        ACT(lamw[:, 2:4], lamw[:, 0:2], AF.Exp, ['lamw'], ['lamw2'])
        TT('dve', lamw[:, 4:5], lamw[:, 3:4], lamw[:, 2:3], ALU.subtract, ['lamw2'], ['lamw3'])
        TS('dve', neglam[:, :], lamw[:, 4:5], -LAM_INIT, None, ALU.add, None, ['lamw3'], ['neglam'])

        stg32 = [kTc[:, :, :].rearrange("p a b -> p (a b)").bitcast(F32), Vc[:, :, :].rearrange("p a b -> p (a b)").bitcast(F32)]
        stgbf = [kTc[:, :, :].rearrange("p a b -> p (a b)"), Vc[:, :, :].rearrange("p a b -> p (a b)")]
        stn = ['kTc', 'Vc']
        BO = 9216
        for c in range(8):
            s_ = c % 2
            src = stg32[s_]
            dst = stgbf[s_]
            DMA('sync', f'ld_s{s_}', src[:, 0:DIN], win_d[c * 128:(c + 1) * 128, :], (), [stn[s_]])
            ng = pp[:, PP_NG + c:PP_NG + c + 1]
            eng = 'dve' if c % 2 == 0 else 'pool'
            TS(eng, dst[:, BO:BO + 2048], src[:, 0:2048], ng, None, ALU.mult, None, [stn[s_], 'pp'], [stn[s_]])
            TS(eng, dst[:, BO + OFF_WDAD:BO + OFF_WDAD + 128], src[:, 3584:3712], ng, None, ALU.mult, None, [stn[s_], 'pp'], [stn[s_]])
            for j in range(3):
                outap = dst[:, BO + OFF_RKV:BO + OFF_RKV + 1536].rearrange("p (a b) -> p a b", b=384)[:, :, j * 128:(j + 1) * 128]
                inap = src[:, 2048 + j * 512:2048 + (j + 1) * 512].rearrange("p (a b) -> p a b", b=128)
                TS(eng, outap, inap, ng, None, ALU.mult, None, [stn[s_], 'pp'], [stn[s_]])
            TS(eng, dst[:, BO + OFF_GB:BO + OFF_GB + 512], src[:, 3712:4224], ng, None, ALU.mult, None, [stn[s_], 'pp'], [stn[s_]])
            DMA('sync', f'st_w{s_}', wbf_d[c * 128:(c + 1) * 128, :], dst[:, BO:BO + DIN], [stn[s_]], [('wbf', c, c + 1)])
        for c in range(8):
            s_ = c % 2
            src = stg32[s_]
            dst = stgbf[s_]
            DMA('sync', f'ld_s{s_}', src[:, 0:D], wout_d[c * 128:(c + 1) * 128, :], (), [stn[s_]])
            CP('dve' if c % 2 == 0 else 'pool', dst[:, BO:BO + D], src[:, 0:D], [stn[s_]], [stn[s_]])
            DMA('sync', f'st_w{s_}', wobf_d[c * 128:(c + 1) * 128, :], dst[:, BO:BO + D], [stn[s_]], [('wobf', c, c + 1)])

        wbf_v = wbf_d.rearrange("(c p) n -> p c n", p=128)
        wobf_v = wobf_d.rearrange("(c p) n -> p c n", p=128)

        NS = 5
        wctr = [0]

        def load_w(view, off, ncols, rname):
            i = wctr[0] % NS
            wctr[0] += 1
            DMA('sync', f'ld_w{i}', wslot[i][:, :, 0:ncols], view[:, :, off:off + ncols], [rname], [f'wslot{i}'])
            return wslot[i], f'wslot{i}'

        tctr = [0]

        def T_():
            i = tctr[0] % NTMP
            tctr[0] += 1
            return tmp[i][:, :], f'tmp{i}'

        qctr = [0]

        def Q_():
            i = qctr[0] % NQ
            qctr[0] += 1
            return qbuf[i], f'qbuf{i}'

        stctr = [0]
        xctr = [0]
        rctr = {'A': 0, 'R': 0}
        BANKS = {'A': [0, 1, 2, 3], 'R': [4, 5, 6]}

        def bank(stream):
            bl = BANKS[stream]
            i = bl[rctr[stream] % len(bl)]
            rctr[stream] += 1
            return B[i], f'B{i}'

        def proj_ftile(ws, wsn, col, stream):
            bk, bn = bank(stream)
            o = bk[:, 0:ST]
            for c in range(8):
                MM(o, ws[:, c, col:col + 128], xnT[:, c, :], c == 0, c == 7, [wsn, 'xnT'], [bn])
            return o, bn

        def shift_evac(o, on, mucol, dst, dstn):
            tm = tmpm[stctr[0] % 2]
            tmn = f'tmpm{stctr[0] % 2}'
            stctr[0] += 1
            ACT(tm[:, 1:ST + 1], o, AF.Copy, [on, 'pp'], [tmn], scale=pp[:, PP_MU + mucol:PP_MU + mucol + 1])
            CP('pool', tm[:, 0:1], carry[:, mucol:mucol + 1], ['carry'], [tmn])
            STT(dst, o, omu[:, mucol:mucol + 1], tm[:, 0:ST], ALU.mult, ALU.add, [on, tmn, 'omu'], [dstn])
            CP('pool', carry[:, mucol:mucol + 1], tm[:, ST:ST + 1], [tmn], ['carry'])

        def bcol(p, lo, n):
            return bT[:, p * ST + lo:p * ST + lo + n]

        def kcol(p, lo, n):
            return ktT[:, p * ST + lo:p * ST + lo + n]

        def ycol(p, lo, n):
            return yT[:, p * ST + lo:p * ST + lo + n]

        def ar(p, e, c, a, n):
            off = (((p * 2 + e) * 4 + c) * 2 + a) * CH
            return arT[:, off:off + n]

        MNbf = MNb[:, :, :].rearrange("p h t -> p (h t)")
        MNkf = MNk[:, :, :].rearrange("p h t -> p (h t)")

        def merge(gens):
            prog = [0] * len(gens)
            alive = [True] * len(gens)
            while any(alive):
                best = None
                for i, (g, n) in enumerate(gens):
                    if alive[i]:
                        f = prog[i] / max(n, 1)
                        if best is None or f < best[0]:
                            best = (f, i)
                i = best[1]
                try:
                    next(gens[i][0])
                    prog[i] += 1
                except StopIteration:
                    alive[i] = False

        sbi = [0]
        def head_gen(b, st):
            row0 = b * S + st * ST
            t0 = st * ST
            for tt in range(2):
                xi = xctr[0] % 2
                xctr[0] += 1
                xb = xbuf[xi]
                xn_ = f'xbuf{xi}'
                DMA('sync', f'ld_x{xi}', xb[:, :], x_d[row0 + tt * 128:row0 + (tt + 1) * 128, :], (), [xn_])
                ACT(junk[:, :], xb[:, :], AF.Square, [xn_], ['junk', 'stat0'], accum=stat[:, 0:1])
                ACT(stat[:, 1:2], stat[:, 0:1], AF.Sqrt, ['stat0'], ['stat1'], scale=1.0 / D, bias=RMS_EPS)
                RECIP(stat[:, 2:3], stat[:, 1:2], ['stat1'], ['stat2'])
                TS('dve', xb[:, :], xb[:, :], stat[:, 2:3], None, ALU.mult, None, [xn_, 'stat2'], [xn_])
                for c in range(8):
                    bk = B[c // 4]
                    TR(bk[:, (c % 4) * 128:(c % 4) * 128 + 128], xb[:, c * 128:(c + 1) * 128], ident, [xn_, 'cst'], [f'B{c // 4}'])
                for hb in range(2):
                    CP('act' if hb == 0 else 'dve', xnT[:, hb * 4:hb * 4 + 4, tt * 128:(tt + 1) * 128],
                       B[hb][:, :].rearrange("p (c t) -> p c t", t=128), [f'B{hb}'], ['xnT'])
                yield
            for half in range(2):
                ws, wsn = load_w(wbf_v, OFF_K + half * 256, 256, 'wbf')
                for f in range(2):
                    h = half * 2 + f
                    o, on = proj_ftile(ws, wsn, f * 128, 'A')
                    CP('act', kTc[:, h, t0:t0 + ST], o, [on], [('kTc', h * 4096 + t0, h * 4096 + t0 + ST)])
                    yield
            for half in range(2):
                ws, wsn = load_w(wbf_v, OFF_VA + half * 256, 256, 'wbf')
                for tt in range(2):
                    bk, bn = bank('A')
                    for c in range(8):
                        MM(bk[:, 0:256], xnT[:, c, tt * 128:(tt + 1) * 128], ws[:, c, 0:256], c == 0, c == 7, [wsn, 'xnT'], [bn])
                    kt = st * 2 + tt
                    CP('dve', Vc[:, kt, half * 256:half * 256 + 256], bk[:, 0:256], [bn], [('Vc', kt * 512, kt * 512 + 512)])
                    yield
            for half in range(2):
                ws, wsn = load_w(wbf_v, OFF_Q + half * 256, 256, 'wbf')
                for f in range(2):
                    h = half * 2 + f
                    o, on = proj_ftile(ws, wsn, f * 128, 'A')
                    CP('act', qT[0:64, h, 0, :], o[0:64, :], [on], ['qT'])
                    CP('dve', qT[64:128, h, 1, :], o[64:128, :], [on], ['qT'])
                    yield
            for half in range(2):
                ws, wsn = load_w(wbf_v, OFF_GA + half * 256, 256, 'wbf')
                for f in range(2):
                    h = half * 2 + f
                    o, on = proj_ftile(ws, wsn, f * 128, 'A')
                    ACT(sga[:, h, :], o, AF.Silu, [on], ['sga'])
                    yield

            yield

        def rwkv_gen(b, st):
            if st == 0:
                MEMSET('pool', S32[:, :], 0.0, ['S32'])
                MEMSET('pool', Sbf[sbi[0]][:, :], 0.0, [f'Sbf{sbi[0]}'])
                MEMSET('pool', carry[:, :], 0.0, ['carry'])
            ws, wsn = load_w(wbf_v, OFF_WDAD, 128, 'wbf')
            o, on = proj_ftile(ws, wsn, 0, 'R')
            shift_evac(o, on, 12, wdad[:, :], 'wdad')
            ACT(linbf[0:64, 0, :], wdad[0:64, :], AF.Tanh, ['wdad'], ['linbf'])
            CP('dve', linbf[64:128, 1, :], wdad[64:128, :], ['wdad'], ['linbf'])
            yield
            def prep_gen(p, tset):
                (sg, sgn), (aT, aTn), (cs, csn), (csp, cspn), (Egi, Egin), (kraw, krawn), (ksq, ksqn), (ka, kan) = tset
                Eg, Egn = sg, sgn
                Egp, Egpn = csp, cspn
                kk, kkn = ksq, ksqn
                rk = rkv[p % 2]
                rkn = f'rkv{p % 2}'
                ws, wsn = load_w(wbf_v, OFF_RKV + p * 384, 256, 'wbf')
                for j in range(2):
                    o, on = proj_ftile(ws, wsn, j * 128, 'R')
                    shift_evac(o, on, j * 4 + p, rk[:, j, :], rkn)
                    yield
                ws, wsn = load_w(wbf_v, OFF_RKV + p * 384 + 256, 128, 'wbf')
                o, on = proj_ftile(ws, wsn, 0, 'R')
                shift_evac(o, on, 2 * 4 + p, rk[:, 2, :], rkn)
                yield
                rs, ks, vs = rk[:, 0, :], rk[:, 1, :], rk[:, 2, :]
                bk, bn = bank('R')
                MM(bk[:, 0:256], lwbf[:, p * 128:(p + 1) * 128], linbf[:, 0, :], True, True, ['lwbf', 'linbf'], [bn])
                MM(bk[:, 256:512], lwbf[:, p * 128:(p + 1) * 128], linbf[:, 1, :], True, True, ['lwbf', 'linbf'], [bn])
                ACT(sg, bk[:, 0:256], AF.Sigmoid, [bn, 'pp'], [sgn], bias=pp[:, PP_W0 + p:PP_W0 + p + 1])
                ACT(aT, bk[:, 256:512], AF.Sigmoid, [bn, 'pp'], [aTn], bias=pp[:, PP_A0 + p:PP_A0 + p + 1])
                TS('dve', kraw, ks, pp[:, PP_KK + p:PP_KK + p + 1], None, ALU.mult, None, [rkn, 'pp'], [krawn])
                TT('pool', ksq, kraw, kraw, ALU.mult, [krawn], [ksqn])
                yield
                P.op('dve', lambda e, cs=cs, sg=sg: e.tensor_tensor_scan(out=cs, data0=scanmask, data1=sg, initial=0.0, op0=ALU.mult, op1=ALU.add), [sgn, 'cst'], [csn])
                bk2, bn2 = bank('R')
                MM(bk2[:, 0:256], blockones, ksq, True, True, ['cst', ksqn], [bn2])
                TS('dve', ksq, bk2[:, 0:256], 1e-24, None, ALU.max, None, [bn2], [ksqn])
                TT('pool', csp, cs, sg, ALU.subtract, [csn, sgn], [cspn])
                yield
                ACT(Eg, cs, AF.Exp, [csn], [Egn], scale=-C0)
                ACT(Egi, cs, AF.Exp, [csn], [Egin], scale=C0)
                ACT(Egp, csp, AF.Exp, [cspn], [Egpn], scale=-C0)
                ACT(ksq, ksq, AF.Sqrt, [ksqn], [ksqn])
                TS('dve', ka, aT, pp[:, PP_KA + p:PP_KA + p + 1], omka[:, p:p + 1], ALU.mult, ALU.add, [aTn, 'pp', 'omka'], [kan])
                TT('dve', ka, ks, ka, ALU.mult, [rkn, kan], [kan])
                yield
                CP('pool', gam[:, p, :], Eg.rearrange("p (c t) -> p c t", t=CH)[:, :, CH - 1], [Egn], ['gam'])
                RECIP(ksq, ksq, [ksqn], [ksqn])
                TT('dve', kk, kraw, ksq, ALU.mult, [krawn, ksqn], [kkn])
                STT(kraw, rs, pp[:, PP_RK + p:PP_RK + p + 1], ka, ALU.mult, ALU.mult, [rkn, kan, 'pp'], [krawn])
                bk3, bn3 = bank('R')
                MM(bk3[:, 0:256], blockones, kraw, True, True, ['cst', krawn], [bn3])
                TT('dve', bonus[:, p, :], bk3[:, 0:256], vs, ALU.mult, [bn3, rkn], ['bonus'])
                yield
                for e_ in range(2):
                    PR_ = slice(64 * e_, 64 * e_ + 64)
                    base = (p * 2 + e_) * 512
                    av = arT[PR_, base:base + 512].rearrange("p (c a t) -> p c a t", c=4, a=2)
                    TT('dve' if e_ == 0 else 'pool', av[:, :, 1, :], rs[PR_, :].rearrange("p (c t) -> p c t", t=CH),
                       Eg[PR_, :].rearrange("p (c t) -> p c t", t=CH), ALU.mult, [rkn, Egn], ['arT'])
                    STT(av[:, :, 0, :], kk[PR_, :].rearrange("p (c t) -> p c t", t=CH), -1.0,
                        Egp[PR_, :].rearrange("p (c t) -> p c t", t=CH), ALU.mult, ALU.mult, [kkn, Egpn], ['arT'])
                yield
                TT('pool', kk, kk, aT, ALU.mult, [kkn, aTn], [kkn])
                TT('dve', bcol(p, 0, ST), kk, Egi, ALU.mult, [kkn, Egin], ['bT'])
                TT('dve', kcol(p, 0, ST), ka, Egi, ALU.mult, [kan, Egin], ['ktT'])
                CP('pool', ycol(p, 0, ST), vs, [rkn], ['yT'])
                yield

            setA = [(tmp[i][:, :], f'tmp{i}') for i in range(8)]
            setB = [(xbuf[i // 4][:, (i % 4) * ST:(i % 4 + 1) * ST], (f'xbuf{i // 4}', (i % 4) * ST, (i % 4 + 1) * ST)) for i in range(8)]
            for pp_ in range(0, 4, 2):
                subs = [prep_gen(pp_, setA), prep_gen(pp_ + 1, setB)]
                live = [True, True]
                while any(live):
                    for si in range(2):
                        if live[si]:
                            try:
                                next(subs[si])
                            except StopIteration:
                                live[si] = False
                    yield
            for half in range(2):
                ws, wsn = load_w(wbf_v, OFF_GB + half * 256, 256, 'wbf')
                for f in range(2):
                    h = half * 2 + f
                    o, on = proj_ftile(ws, wsn, f * 128, 'R')
                    ACT(sgb[:, h, :], o, AF.Silu, [on], ['sgb'])
                    yield
            for c in range(0 if os.environ.get('KNOCHUNK') else 4):
                c0 = c * CH
                cres = ('yT', c0, c0 + CH)
                for p in range(4):
                    TR(PTB[:, p * 128:(p + 1) * 128], bcol(p, c0, 128), identbf[:, :], ['bT', 'identbf'], ['PTB'])
                    TR(PTB[:, 512 + p * 128:512 + (p + 1) * 128], kcol(p, c0, 128), identbf[:, :], ['ktT', 'identbf'], ['PTB'])
                CP('act', bkTF[0:64, :], PTB[0:64, :], ['PTB'], ['bkTF'])
                for p in range(4):
                    TR(B[4][:, p * 128:(p + 1) * 128], ycol(p, c0, 128), ident, [cres, 'cst'], ['B4'])
                CP('dve', VTF[0:64, :], B[4][0:64, :], ['B4'], ['VTF'])
                yield
                mA = maskA.unsqueeze(1).broadcast_to([64, 4, 128])
                for h in range(8):
                    p, e = h // 2, h % 2
                    MM(B[5 + h // 4][:, (h % 4) * 128:(h % 4) * 128 + 128], bcol(p, c0, 128), ar(p, e, c, 0, 128), True, True, ['bT', 'arT'], [f'B{5 + h // 4}'])
                for hf in range(2):
                    TT('dve', MNb[0:64, hf * 4:hf * 4 + 4, :], B[5 + hf][0:64, :].rearrange("p (h t) -> p h t", t=128), mA, ALU.mult, [f'B{5 + hf}', 'cst'], ['MNb'])
                for h in range(8):
                    p, e = h // 2, h % 2
                    MM(B[4][:, h * 64:(h + 1) * 64], ar(p, e, c, 0, 128), bcol(p, c0, 64), True, True, ['bT', 'arT'], ['B4'])
                QT, QTn = Q_()
                mC = maskC.unsqueeze(1).broadcast_to([64, 8, 64])
                TT('dve', QT[0:64, 0:8, :], B[4][0:64, :].rearrange("p (h t) -> p h t", t=64), mC, ALU.mult, ['B4', 'cst'], [QTn])
                yield
                for h in range(8):
                    p, e = h // 2, h % 2
                    MM(B[5 + h // 4][:, (h % 4) * 128:(h % 4) * 128 + 128], kcol(p, c0, 128), ar(p, e, c, 0, 128), True, True, ['ktT', 'arT'], [f'B{5 + h // 4}'])
                for hf in range(2):
                    TT('dve', MNk[0:64, hf * 4:hf * 4 + 4, :], B[5 + hf][0:64, :].rearrange("p (h t) -> p h t", t=128), mA, ALU.mult, [f'B{5 + hf}', 'cst'], ['MNk'])
                ti = 0
                Tc, Tn = Tb[0], 'Tb0'
                TT('pool', Tc[0:64, 0:8, :], MNb[0:64, 0:8, 0:64], ident[0:64, 0:64].unsqueeze(1).broadcast_to([64, 8, 64]), ALU.add, ['MNb', 'cst'], [Tn])
                yield
                Qr = lambda h: MNb[:, h, 0:64]
                Ql = lambda h: MNbf[:, h * 128:h * 128 + 128]
                Qn = 'MNb'
                QTf = QT[:, :, :].rearrange("p h t -> p (h t)")
                QTr = lambda h, QT=QT: QT[:, h, :]
                QTl = lambda h, QTf=QTf: QTf[:, h * 64:h * 64 + 128]
                pend = None
                for lvl in range(1, 7):
                    need_q = lvl <= 4
                    need_qt = lvl <= 5
                    if need_q:
                        for h in range(8):
                            MM(B[4][:, h * 64:(h + 1) * 64], QTl(h), Qr(h), True, True, [QTn, Qn], ['B4'])
                    if need_qt:
                        for h in range(8):
                            MM(B[5][:, h * 64:(h + 1) * 64], Ql(h), QTr(h), True, True, [QTn, Qn], ['B5'])
                    if pend is not None:
                        pl, pn = pend
                        Tf = Tc[:, :, :].rearrange("p h t -> p (h t)")
                        for h in range(8):
                            MM(B[6][:, h * 64:(h + 1) * 64], pl(h), Tc[:, h, :], True, True, [pn, Tn], ['B6'])
                    if need_q:
                        Q2, Q2n = Q_()
                        CP('act', Q2[0:64, 0:8, :], B[4][0:64, :].rearrange("p (h t) -> p h t", t=64), ['B4'], [Q2n])
                    if need_qt:
                        QT2, QT2n = Q_()
                        CP('act', QT2[0:64, 0:8, :], B[5][0:64, :].rearrange("p (h t) -> p h t", t=64), ['B5'], [QT2n])
                    if pend is not None:
                        ti ^= 1
                        Tnew, Tnn = Tb[ti], f'Tb{ti}'
                        TT('dve', Tnew[0:64, 0:8, :], B[6][0:64, :].rearrange("p (h t) -> p h t", t=64), Tc[0:64, 0:8, :], ALU.add, ['B6', Tn], [Tnn])
                        Tc, Tn = Tnew, Tnn
                    if need_qt:
                        QT2f = QT2[:, :, :].rearrange("p h t -> p (h t)")
                        pend = ((lambda h, f_=QT2f: f_[:, h * 64:h * 64 + 128]), QT2n)
                        QTr = lambda h, QT2=QT2: QT2[:, h, :]
                        QTl = pend[0]
                        QTn = QT2n
                    else:
                        pend = None
                    if need_q:
                        Q2f = Q2[:, :, :].rearrange("p h t -> p (h t)")
                        Qr = lambda h, Q2=Q2: Q2[:, h, :]
                        Ql = lambda h, f_=Q2f: f_[:, h * 64:h * 64 + 128]
                        Qn = Q2n
                    yield
                Tf = Tc[:, :, :].rearrange("p h t -> p (h t)")
                So, Son = Sbf[sbi[0]], f'Sbf{sbi[0]}'
                for h in range(8):
                    p, e = h // 2, h % 2
                    MM(B[4][:, h * 64:(h + 1) * 64], ar(p, e, c, 0, 128), So[:, p * 128 + e * 64:p * 128 + e * 64 + 64], True, False, ['arT', Son], ['B4'])
                    MM(B[4][:, h * 64:(h + 1) * 64], MNkf[:, h * 128:h * 128 + 128], VTF[:, h * 64:(h + 1) * 64], False, True, ['MNk', 'VTF'], ['B4'])
                CP('act', Wsb[0:64, :], B[4][0:64, :], ['B4'], ['Wsb'])
                yield
                for h in range(8):
                    MM(B[5][:, h * 64:(h + 1) * 64], Tf[:, h * 64:h * 64 + 128], Wsb[:, h * 64:(h + 1) * 64], True, True, [Tn, 'Wsb'], ['B5'])
                CP('act', Usb[0:64, :], B[5][0:64, :], ['B5'], ['Usb'])
                yield
                for h in range(8):
                    p, e = h // 2, h % 2
                    o = B[6][:, h * 64:(h + 1) * 64]
                    MM(o, ar(p, e, c, 1, 128), So[:, p * 128 + e * 64:p * 128 + e * 64 + 64], True, False, ['arT', Son], ['B6'])
                    MM(o, MNbf[:, h * 128 + 64:h * 128 + 192], Usb[:, h * 64:(h + 1) * 64], False, False, ['MNb', 'Usb'], ['B6'])
                    MM(o, MNkf[:, h * 128 + 64:h * 128 + 192], VTF[:, h * 64:(h + 1) * 64], False, True, ['MNk', 'VTF'], ['B6'])
                CP('act', Ytf[0:64, :], B[6][0:64, :], ['B6'], ['Ytf'])
                for p in range(4):
                    o = B[4][:, p * 128:(p + 1) * 128]
                    MM(o, bkTF[:, p * 128:(p + 1) * 128], Usb[:, p * 128:(p + 1) * 128], True, False, ['bkTF', 'Usb'], ['B4'])
                    MM(o, bkTF[:, 512 + p * 128:512 + (p + 1) * 128], VTF[:, p * 128:(p + 1) * 128], False, True, ['bkTF', 'VTF'], ['B4'])
                TT('dve', tmpS[:, :], B[4][:, :], S32[:, :], ALU.add, ['B4', 'S32'], ['tmpS'])
                gbc = gam[:, :, c:c + 1].broadcast_to([128, 4, 128])
                tS3 = tmpS[:, :].rearrange("p (a b) -> p a b", b=128)
                TT('dve', S32[:, :].rearrange("p (a b) -> p a b", b=128), tS3, gbc, ALU.mult, ['tmpS', 'gam'], ['S32'])
                sbi[0] ^= 1
                TT('pool', Sbf[sbi[0]][:, :].rearrange("p (a b) -> p a b", b=128), tS3, gbc, ALU.mult, ['tmpS', 'gam'], [f'Sbf{sbi[0]}'])
                yield
                for p in range(4):
                    TR(B[5][:, p * 128:(p + 1) * 128], Ytf[:, p * 128:(p + 1) * 128], ident, ['Ytf', 'cst'], ['B5'])
                for p in range(4):
                    CP('dve' if p % 2 == 0 else 'act', ycol(p, c0, CH), B[5][:, p * 128:p * 128 + CH], ['B5'], [cres])
                yield

        def tail_gen(b, st):
            row0 = b * S + st * ST
            def epi_gen(p):
                y = ycol(p, 0, ST)
                (sq, sqn), (m, mn), (msq, msqn) = [(tmp[3 * p + i][:, :], f'tmp{3 * p + i}') for i in range(3)]
                bk, bn = bank('R')
                MM(bk[:, 0:256], blockones, y, True, True, ['cst', 'yT'], [bn])
                TT('pool', sq, y, y, ALU.mult, ['yT'], [sqn])
                MM(bk[:, 256:512], blockones, sq, True, True, ['cst', sqn], [bn])
                TS('dve', m, bk[:, 0:256], 1.0 / 64, None, ALU.mult, None, [bn], [mn])
                TT('pool', msq, m, m, ALU.mult, [mn], [msqn])
                STT(msq, bk[:, 256:512], 1.0 / 64, msq, ALU.mult, ALU.subtract, [bn, msqn], [msqn])
                yield
                ACT(msq, msq, AF.Sqrt, [msqn], [msqn], bias=GN_EPS)
                RECIP(msq, msq, [msqn], [msqn])
                TT('dve', m, y, m, ALU.subtract, ['yT', mn], [mn])
                yield
                TT('dve', m, m, msq, ALU.mult, [mn, msqn], [mn])
                TS('dve', m, m, pp[:, PP_LNG + p:PP_LNG + p + 1], pp[:, PP_LNB + p:PP_LNB + p + 1], ALU.mult, ALU.add, [mn, 'pp'], [mn])
                TT('pool', m, m, bonus[:, p, :], ALU.add, [mn, 'bonus'], [mn])
                TT('dve', mixT[:, 4 + p, :], m, sgb[:, p, :], ALU.mult, [mn, 'sgb'], ['mixT'])
                yield

            subs = [epi_gen(p) for p in range(4)]
            live = [True] * 4
            while any(live):
                for si in range(4):
                    if live[si]:
                        try:
                            next(subs[si])
                        except StopIteration:
                            live[si] = False
                yield
            for tt in range(2):
                DMA('sync', f'ld_h{tt}', hres[tt][:, :], x_d[row0 + tt * 128:row0 + (tt + 1) * 128, :], (), [f'hres{tt}'])
            for n in range(4):
                ws, wsn = load_w(wobf_v, n * 256, 256, 'wobf')
                for tt in range(2):
                    bk, bn = bank('A')
                    for f in range(8):
                        MM(bk[:, 0:256], mixT[:, f, tt * 128:(tt + 1) * 128], ws[:, f, 0:256], f == 0, f == 7, ['mixT', wsn], [bn])
                    TT('dve', hres[tt][:, n * 256:(n + 1) * 256], bk[:, 0:256], hres[tt][:, n * 256:(n + 1) * 256], ALU.add, [bn, f'hres{tt}'], [f'hres{tt}'])
                    yield
            for tt in range(2):
                ACT(junk[:, :], hres[tt][:, :], AF.Square, [f'hres{tt}'], ['junk', 'stat4'], accum=stat[:, 4:5])
                ACT(stat[:, 5:6], stat[:, 4:5], AF.Sqrt, ['stat4'], ['stat5'], scale=1.0 / D, bias=RMS_EPS)
                RECIP(stat[:, 6:7], stat[:, 5:6], ['stat5'], ['stat6'])
                STT(hres[tt][:, :], hres[tt][:, :], stat[:, 6:7], fg[:, :], ALU.mult, ALU.mult, [f'hres{tt}', 'stat6', 'fg'], [f'hres{tt}'])
                DMA('pool', f'st_{tt}', out_d[row0 + tt * 128:row0 + (tt + 1) * 128, :], hres[tt][:, :], [f'hres{tt}'], ['out'])

            yield

        def att_gen(b, st):
            nkt = 2 * st + 2

            items = [(h, kt) for h in range(0 if os.environ.get('KNOATT') else 4) for kt in range(nkt)]

            def emit_scores(i):
                h, kt = items[i]
                j = kt - 2 * st
                col0 = 128 * max(j, 0)
                bk, bn = B[i % 2], f'B{i % 2}'
                kres = ('kTc', h * 4096 + kt * 128, h * 4096 + kt * 128 + 128)
                if col0 == 0:
                    MM(bk[:, :], kTc[:, h, kt * 128:(kt + 1) * 128], qT[:, h, :, :].rearrange("p m q -> p (m q)"), True, True, [kres, 'qT'], [bn])
                else:
                    for m_ in range(2):
                        MM(bk[:, m_ * ST + col0:(m_ + 1) * ST], kTc[:, h, kt * 128:(kt + 1) * 128], qT[:, h, m_, col0:ST], True, True, [kres, 'qT'], [bn])

            def emit_exp(i):
                h, kt = items[i]
                j = kt - 2 * st
                col0 = 128 * max(j, 0)
                bk, bn = B[i % 2], f'B{i % 2}'
                pt = Pt[i % 2]
                ptn = f'Pt{i % 2}'
                sview = bk[:, :].rearrange("p (m q) -> p m q", q=ST)[:, :, col0:ST]
                pview = pt[:, :, col0:ST]
                ACT(pview, sview, AF.Exp, [bn], [ptn], scale=0.125)
                if j >= 0:
                    MEMSET('pool', pt[64:128, :, col0:col0 + 64], 0.0, [ptn])

            def emit_pv(i):
                h, kt = items[i]
                j = kt - 2 * st
                col0 = 128 * max(j, 0)
                pt = Pt[i % 2]
                ptn = f'Pt{i % 2}'
                vv = Vc[:, kt, h * 128:(h + 1) * 128]
                vres = ('Vc', kt * 512, kt * 512 + 512)
                last = kt == nkt - 1
                if col0 == 0:
                    p2 = pt[:, :, :].rearrange("p m q -> p (m q)")
                    MM(B[2][:, :], vv, p2, kt == 0, last, [vres, ptn], ['B2'])
                    MM(B[3][:, :], onesbf[:, :], p2, kt == 0, last, ['onesbf', ptn], ['B3'])
                else:
                    for m_ in range(2):
                        MM(B[2][:, m_ * ST + col0:(m_ + 1) * ST], vv, pt[:, m_, col0:ST], False, last and m_ == 1, [vres, ptn], ['B2'])
                    for m_ in range(2):
                        MM(B[3][:, m_ * ST + col0:(m_ + 1) * ST], onesbf[:, :], pt[:, m_, col0:ST], False, last and m_ == 1, ['onesbf', ptn], ['B3'])
                if last:
                    a1 = att1[:, :, :].rearrange("p m q -> p (m q)")
                    a2 = att2[:, :, :].rearrange("p m q -> p (m q)")
                    RECIP(a1, B[3][:, :], ['B3'], ['att1'])
                    TT('dve', a2, B[2][:, :], a1, ALU.mult, ['B2', 'att1'], ['att2'])
                    o_, on_ = att1[:, 0, :], 'att1'
                    STT(o_, att2[:, 1, :], neglam[:, 0:1], att2[:, 0, :], ALU.mult, ALU.add, ['att2', 'neglam'], [on_])
                    sq = att1[:, 1, :]
                    TT('pool', sq, o_, o_, ALU.mult, [on_], [on_])
                    sbk, sbn = B[(i + 1) % 2], f'B{(i + 1) % 2}'
                    MM(sbk[:, 0:256], ones32[:, :], sq, True, True, ['ones32', on_], [sbn])
                    ACT(sq, sbk[:, 0:256], AF.Sqrt, [sbn], [on_], scale=1.0 / 128, bias=SUBLN_EPS)
                    RECIP(sq, sq, [on_], [on_])
                    STT(o_, o_, g08[:, 0:1], sq, ALU.mult, ALU.mult, [on_, 'g08'], [on_])
                    TT('dve', mixT[:, h, :], o_, sga[:, h, :], ALU.mult, [on_, 'sga'], ['mixT'])

            n_it = len(items)
            for s_ in range(n_it + 2 if n_it else 0):
                if s_ < n_it:
                    emit_scores(s_)
                if 0 <= s_ - 1 < n_it:
                    emit_exp(s_ - 1)
                if 0 <= s_ - 2 < n_it:
                    emit_pv(s_ - 2)
                yield

        items_ = [(b, st) for b in range(NB) for st in range(NST)]
        merge([(head_gen(*items_[0]), 1)])
        for ii, (b, st) in enumerate(items_):
            merge([(rwkv_gen(b, st), int(os.environ.get("KREST", "70"))), (att_gen(b, st), 4 * (2 * st + 2))])
            gl = [(tail_gen(b, st), 15)]
            if ii + 1 < len(items_):
                gl.append((head_gen(*items_[ii + 1]), 23))
            merge(gl)

        P.final_wait('pool', ['st_0', 'st_1'])

        with nc.Block() as block:
            @block.sync
            def _(e):
                for f in P.engs['sync']['thunks']:
                    f(e)

            @block.tensor
            def _(e):
                for f in P.engs['pe']['thunks']:
                    f(e)

            @block.scalar
            def _(e):
                for f in P.engs['act']['thunks']:
                    f(e)

            @block.vector
            def _(e):
                for f in P.engs['dve']['thunks']:
                    f(e)

            @block.gpsimd
            def _(e):
                for f in P.engs['pool']['thunks']:
                    f(e)
    return nc


def make_consts():
    cst = np.zeros((128, 832), np.float32)
    cst[:, 0:128] = np.eye(128, dtype=np.float32)
    cst[0:64, 128:192] = 1.0
    cst[64:128, 192:256] = 1.0
    j = np.arange(64)[:, None]
    t = np.arange(64)[None, :]
    cst[0:64, 256:320] = (j < t)
    cst[0:64, 320:384] = (j <= t)
    cst[0:64, 384:448] = (j > t)
    sm = np.ones(256, np.float32)
    sm[::64] = 0.0
    cst[:, 576:832] = sm[None, :]
    return cst


def prep_params(norm_g, lambda_q1, lambda_k1, lambda_q2, lambda_k2, subln_g, shift_mu, w0, w2, a0, a2,
                k_k, k_a, r_k, ln_x_g, ln_x_b, final_g):
    f = lambda v, n: np.ascontiguousarray(np.asarray(v, np.float32).reshape(n, 128).T)
    pp = np.zeros((128, PP_N), np.float32)
    pp[:, PP_MU:PP_MU + 13] = f(shift_mu[0], 13)
    pp[:, PP_W0:PP_W0 + 4] = f(w0[0], 4)
    pp[:, PP_A0:PP_A0 + 4] = f(a0[0], 4)
    pp[:, PP_KK:PP_KK + 4] = f(k_k[0], 4)
    pp[:, PP_KA:PP_KA + 4] = f(k_a[0], 4)
    pp[:, PP_RK:PP_RK + 4] = f(r_k[0].reshape(-1), 4)
    pp[:, PP_LNG:PP_LNG + 4] = f(ln_x_g[0], 4)
    pp[:, PP_LNB:PP_LNB + 4] = f(ln_x_b[0], 4)
    pp[:, PP_SUB] = np.asarray(subln_g[0], np.float32)
    pp[:, PP_NG:PP_NG + 8] = f(norm_g[0], 8)
    lam4 = np.concatenate([np.broadcast_to(np.asarray(v[0], np.float32)[None, :], (128, 64))
                           for v in (lambda_q1, lambda_k1, lambda_q2, lambda_k2)], axis=1)
    lw = np.concatenate([np.asarray(w2[0], np.float32), np.asarray(a2[0], np.float32)], axis=0)
    fg = np.broadcast_to(np.asarray(final_g, np.float32)[None, :], (128, D))
    return dict(pp=pp, lam4=np.ascontiguousarray(lam4), lw=np.ascontiguousarray(lw), fg=np.ascontiguousarray(fg),
                cst=make_consts())


_CACHE = {}


def run(x, w_in, w_out, small, n_cores):
    x = np.asarray(x, np.float32)
    Bt, S, _ = x.shape
    NB = Bt // n_cores
    key = (NB, S)
    if key not in _CACHE:
        _CACHE[key] = build(NB, S)
    nc = _CACHE[key]
    w_in2 = np.ascontiguousarray(np.asarray(w_in, np.float32)[0])
    w_out2 = np.ascontiguousarray(np.asarray(w_out, np.float32)[0])
    in_maps = []
    for c in range(n_cores):
        m = dict(small)
        m["x"] = np.ascontiguousarray(x[c * NB:(c + 1) * NB].reshape(NB * S, D))
        m["w_in"] = w_in2
        m["w_out"] = w_out2
        in_maps.append(m)
    res = run_bass_kernel_spmd(nc, in_maps, core_ids=list(range(n_cores)))
    outs = [np.asarray(r["out"]).reshape(NB, S, D) for r in res.results]
    return np.concatenate(outs, axis=0).astype(np.float32)


def kernel(x, norm_g, w_in, lambda_q1, lambda_k1, lambda_q2, lambda_k2, subln_g,
           shift_mu, w0, w2, a0, a2, k_k, k_a, r_k, ln_x_g, ln_x_b, w_out, final_g):
    small = prep_params(norm_g, lambda_q1, lambda_k1, lambda_q2, lambda_k2, subln_g, shift_mu, w0, w2, a0, a2,
                        k_k, k_a, r_k, ln_x_g, ln_x_b, final_g)
    return run(x, w_in, w_out, small, 8)
```

```python
import os
import numpy as np
import concourse.bass as bass
import concourse.mybir as mybir
from concourse.bass_utils import run_bass_kernel_spmd

F32 = mybir.dt.float32
BF16 = mybir.dt.bfloat16
AF = mybir.ActivationFunctionType
ALU = mybir.AluOpType
AX = mybir.AxisListType

D = 1024
DIN = 4224
ST = 256
CH = 64
C0 = 0.6065306597126334
RMS_EPS = 1e-6
SUBLN_EPS = 1e-5
GN_EPS = 1e-5 * 64
LAM_INIT = 0.2
BIG = 1 << 40

OFF_Q, OFF_K, OFF_VA, OFF_GA, OFF_WDAD, OFF_RKV, OFF_GB = 0, 512, 1024, 1536, 2048, 2176, 3712

PP_MU, PP_W0, PP_A0, PP_KK, PP_KA, PP_RK, PP_LNG, PP_LNB, PP_SUB, PP_NG, PP_N = 0, 13, 17, 21, 25, 29, 33, 37, 41, 42, 50


class Prog:
    def __init__(self):
        self.engs = {}
        self.acc = {}

    def add(self, name, sem, real=True):
        self.engs[name] = dict(sem=sem, count=0, thunks=[], waited={}, real=real)

    @staticmethod
    def _norm(r):
        if isinstance(r, str):
            return (r, 0, BIG)
        return r

    def op(self, eng, fn, reads=(), writes=(), dma=None):
        self.nops = getattr(self, 'nops', 0) + 1
        if self.nops > int(os.environ.get('KSTOP', '100000000')):
            return
        if os.environ.get('KLOG'):
            import inspect
            fr = inspect.stack()
            print('OP', self.nops, eng, [f.lineno for f in fr[1:4]], flush=True)
        E = self.engs[eng]
        deps = {}
        reads = [self._norm(r) for r in reads]
        writes = [self._norm(r) for r in writes]
        for (n, lo, hi) in reads:
            for (l2, h2, k, e, i) in self.acc.get(n, ()):
                if k == 'w' and l2 < hi and lo < h2:
                    deps[e] = max(deps.get(e, 0), i)
        for (n, lo, hi) in writes:
            for (l2, h2, k, e, i) in self.acc.get(n, ()):
                if l2 < hi and lo < h2:
                    deps[e] = max(deps.get(e, 0), i)
        for e, i in deps.items():
            if e == eng and eng == 'pe':
                continue
            if E['waited'].get(e, 0) >= i:
                continue
            sem = self.engs[e]['sem']
            if self.engs[e].get('unordered'):
                E['waited'][e] = BIG
                E['thunks'].append(lambda en, s=sem, d=self.engs[e]: en.wait_ge(s, d['count']))
            else:
                E['waited'][e] = i
                E['thunks'].append(lambda en, s=sem, v=i: en.wait_ge(s, v))
        ce = dma if dma else eng
        CE = self.engs[ce]
        inc = 16 if dma else 1
        CE['count'] += inc
        idx = CE['count']
        E['thunks'].append(lambda en, f=fn, s=CE['sem'], i=inc: f(en).then_inc(s, i))
        for (n, lo, hi) in writes:
            lst = self.acc.setdefault(n, [])
            lst[:] = [a for a in lst if not (lo <= a[0] and a[1] <= hi)]
            lst.append((lo, hi, 'w', ce, idx))
        for (n, lo, hi) in reads:
            lst = self.acc.setdefault(n, [])
            lst[:] = [a for a in lst if not (a[2] == 'r' and a[3] == ce and lo <= a[0] and a[1] <= hi)]
            lst.append((lo, hi, 'r', ce, idx))

    def final_wait(self, eng, names):
        E = self.engs[eng]
        for n in names:
            c = self.engs[n]['count']
            if c > 0:
                E['thunks'].append(lambda en, s=self.engs[n]['sem'], v=c: en.wait_ge(s, v))


def build(NB, S, dbg=False):
    NT = S // 128
    NST = S // ST
    nc = bass.Bass("TRN2", target_bir_lowering=False)
    dt = nc.dram_tensor
    x_d = dt("x", [NB * S, D], F32, kind="ExternalInput").ap()
    win_d = dt("w_in", [D, DIN], F32, kind="ExternalInput").ap()
    wout_d = dt("w_out", [D, D], F32, kind="ExternalInput").ap()
    pp_d = dt("pp", [128, PP_N], F32, kind="ExternalInput").ap()
    lam_d = dt("lam4", [128, 256], F32, kind="ExternalInput").ap()
    lw_d = dt("lw", [128, 512], F32, kind="ExternalInput").ap()
    fg_d = dt("fg", [128, D], F32, kind="ExternalInput").ap()
    cst_d = dt("cst", [128, 832], F32, kind="ExternalInput").ap()
    out_d = dt("out", [NB * S, D], F32, kind="ExternalOutput").ap()
    wbf_d = dt("wbf", [D, DIN], BF16, kind="ExternalOutput").ap()
    wobf_d = dt("wobf", [D, D], BF16, kind="ExternalOutput").ap()

    sb = nc.alloc_sbuf_tensor
    kTc = sb("kTc", [128, 4, 4096], BF16)
    Vc = sb("Vc", [128, 32, 512], BF16)
    wslot = [sb(f"wslot{i}", [128, 8, 256], BF16) for i in range(5)]
    pp = sb("pp_sb", [128, PP_N], F32)
    omu = sb("omu", [128, 13], F32)
    omka = sb("omka", [128, 4], F32)
    g08 = sb("g08", [128, 1], F32)
    neglam = sb("neglam", [128, 1], F32)
    lamw = sb("lamw", [128, 8], F32)
    lw32 = sb("lw32", [128, 512], F32)
    lwbf = sb("lwbf", [128, 512], BF16)
    fg = sb("fg_sb", [128, D], F32)
    cst = sb("cst_sb", [128, 832], F32)
    ident = cst[:, 0:128]
    blockones = cst[:, 128:256]
    maskA = cst[0:64, 256:384]
    maskC = cst[0:64, 384:448]
    scanmask = cst[:, 576:832]
    identbf = sb("identbf", [128, 128], BF16)
    onesbf = sb("onesbf", [128, 128], BF16)
    ones32 = sb("ones32", [128, 128], F32)
    xbuf = [sb(f"xbuf{i}", [128, D], F32) for i in range(2)]
    hres = [sb(f"hres{i}", [128, D], F32) for i in range(2)]
    stat = sb("stat", [128, 16], F32)
    junk = sb("junk", [128, D], F32)
    xnT = sb("xnT", [128, 8, ST], BF16)
    qT = sb("qT", [128, 4, 2, ST], BF16)
    sga = sb("sga", [128, 4, ST], BF16)
    sgb = sb("sgb", [128, 4, ST], BF16)
    mixT = sb("mixT", [128, 8, ST], BF16)
    rkv = [sb(f"rkv{i}", [128, 3, ST], F32) for i in range(2)]
    wdad = sb("wdad", [128, ST], F32)
    linbf = sb("linbf", [128, 2, ST], BF16)
    tmpm = [sb(f"tmpm{i}", [128, ST + 2], F32) for i in range(2)]
    carry = sb("carry", [128, 16], F32)
    NTMP = 12
    tmp = [sb(f"tmp{i}", [128, ST], F32) for i in range(NTMP)]
    lamt = tmp[0]
    arT = sb("arT", [128, 4 * 2 * 4 * 2 * CH + CH], BF16)
    bT = sb("bT", [128, 4 * ST + CH], BF16)
    ktT = sb("ktT", [128, 4 * ST + CH], BF16)
    bonus = sb("bonus", [128, 4, ST], F32)
    gam = sb("gam", [128, 4, 4], F32)
    yT = sb("yT", [128, 4 * ST + CH], F32)
    MNb = sb("MNb", [128, 9, 128], BF16)
    MNk = sb("MNk", [128, 9, 128], BF16)
    NQ = 6
    qbuf = [sb(f"qbuf{i}", [128, 9, 64], BF16) for i in range(NQ)]
    Tb = [sb(f"Tb{i}", [128, 9, 64], BF16) for i in range(2)]
    bkTF = sb("bkTF", [128, 1024], BF16)
    VTF = sb("VTF", [128, 512], BF16)
    Wsb = sb("Wsb", [128, 512], BF16)
    Usb = sb("Usb", [128, 512], BF16)
    S32 = sb("S32", [128, 512], F32)
    Sbf = [sb(f"Sbf{i}", [128, 512], BF16) for i in range(2)]
    tmpS = sb("tmpS", [128, 512], F32)
    Ytf = sb("Ytf", [128, 512], F32)
    Pt = [sb(f"Pt{i}", [128, 2, ST], BF16) for i in range(2)]
    att1 = sb("att1", [128, 2, ST], F32)
    att2 = sb("att2", [128, 2, ST], F32)

    ps = nc.alloc_psum_tensor
    B = [ps(f"B{i}", [128, 512], F32) for i in range(7)]
    PTB = ps("PTB", [128, 1024], BF16)

    P = Prog()
    import contextlib
    es = contextlib.ExitStack()
    with es:
        def sem(n):
            return es.enter_context(nc.semaphore(n))
        for e in ('pe', 'act', 'dve', 'pool', 'sync'):
            P.add(e, sem("s_" + e))
        slots = ['ld_misc', 'ld_x0', 'ld_x1', 'ld_w0', 'ld_w1', 'ld_w2', 'ld_w3', 'ld_w4', 'ld_h0', 'ld_h1', 'st_0', 'st_1', 'st_w0', 'st_w1', 'ld_s0', 'ld_s1']
        for s_ in slots:
            P.add(s_, sem(s_), real=False)
        P.engs['ld_misc']['unordered'] = True

        def ACT(out, in_, func, r, w, scale=1.0, bias=0.0, accum=None):
            if accum is None:
                P.op('act', lambda e: e.activation(out=out, in_=in_, func=func, scale=scale, bias=bias), r, w)
            else:
                P.op('act', lambda e: e.activation(out=out, in_=in_, func=func, scale=scale, bias=bias, accum_out=accum), r, w)

        def CP(eng, out, in_, r, w):
            if eng == 'act':
                P.op('act', lambda e: e.activation(out=out, in_=in_, func=AF.Copy), r, w)
            else:
                P.op(eng, lambda e: e.tensor_copy(out=out, in_=in_), r, w)

        def TT(eng, out, in0, in1, op, r, w):
            P.op(eng, lambda e: e.tensor_tensor(out=out, in0=in0, in1=in1, op=op), r, w)

        def TS(eng, out, in0, s1, s2, op0, op1, r, w):
            if s2 is None:
                P.op(eng, lambda e: e.tensor_scalar(out=out, in0=in0, scalar1=s1, scalar2=None, op0=op0), r, w)
            else:
                P.op(eng, lambda e: e.tensor_scalar(out=out, in0=in0, scalar1=s1, scalar2=s2, op0=op0, op1=op1), r, w)

        def STT(out, in0, scalar, in1, op0, op1, r, w):
            P.op('dve', lambda e: e.scalar_tensor_tensor(out=out, in0=in0, scalar=scalar, in1=in1, op0=op0, op1=op1), r, w)

        def RECIP(out, in_, r, w):
            P.op('dve', lambda e: e.reciprocal(out=out, in_=in_), r, w)

        pemode = [None]

        def _rnd(v):
            return 32 if v <= 32 else (64 if v <= 64 else 128)

        def pe_mode(lhsT):
            K_ = lhsT.shape[0]
            M_ = 1
            for d_ in lhsT.shape[1:]:
                M_ *= d_
            md = (_rnd(K_), _rnd(M_))
            if pemode[0] is not None and pemode[0] != md:
                E = P.engs['pe']
                if E['count'] > 0:
                    E['thunks'].append(lambda en, s=E['sem'], v=E['count']: en.wait_ge(s, v))
                    P.ndrain = getattr(P, 'ndrain', 0) + 1
                    if os.environ.get('KLOGDRAIN'):
                        print('DRAIN', pemode[0], '->', md, flush=True)
            pemode[0] = md

        def MM(out, lhsT, rhs, start, stop, r, w):
            pe_mode(lhsT)
            P.op('pe', lambda e: e.matmul(out, lhsT=lhsT, rhs=rhs, start=start, stop=stop), r, w)

        def TR(out, in_, idn, r, w):
            pe_mode(in_)
            P.op('pe', lambda e: e.transpose(out, in_, idn), r, w)

        def DMA(queue, slot, out, in_, r, w):
            P.op(queue, lambda e: e.dma_start(out=out, in_=in_), r, w, dma=slot)

        def MEMSET(eng, ap, val, w):
            P.op(eng, lambda e: e.memset(ap, val), (), w)

        DMA('sync', 'ld_misc', pp[:, :], pp_d[:, :], (), ['pp'])
        DMA('sync', 'ld_misc', lamt[:, :], lam_d[:, :], (), ['tmp0'])
        DMA('sync', 'ld_misc', lw32[:, :], lw_d[:, :], (), ['lw32'])
        DMA('sync', 'ld_misc', fg[:, :], fg_d[:, :], (), ['fg'])
        DMA('sync', 'ld_misc', cst[:, :], cst_d[:, :], (), ['cst'])
        TS('dve', omu[:, :], pp[:, PP_MU:PP_MU + 13], -1.0, 1.0, ALU.mult, ALU.add, ['pp'], ['omu'])
        TS('dve', omka[:, :], pp[:, PP_KA:PP_KA + 4], -1.0, 1.0, ALU.mult, ALU.add, ['pp'], ['omka'])
        TS('dve', g08[:, :], pp[:, PP_SUB:PP_SUB + 1], 1.0 - LAM_INIT, None, ALU.mult, None, ['pp'], ['g08'])
        CP('dve', lwbf[:, :], lw32[:, :], ['lw32'], ['lwbf'])
        CP('dve', identbf[:, :], ident, ['cst'], ['identbf'])
        MEMSET('pool', onesbf[:, :], 1.0, ['onesbf'])
        MEMSET('pool', ones32[:, :], 1.0, ['ones32'])
        for t_, n_ in [(qT, 'qT'), (linbf, 'linbf'), (arT, 'arT'), (MNb, 'MNb'), (MNk, 'MNk'), (bkTF, 'bkTF'), (VTF, 'VTF'),
                       (Wsb, 'Wsb'), (Usb, 'Usb'), (Ytf, 'Ytf'), (Tb[0], 'Tb0'), (Tb[1], 'Tb1'), (bT, 'bT'), (ktT, 'ktT'), (yT, 'yT')] + [(qbuf[i], f'qbuf{i}') for i in range(NQ)]:
            nd = len(t_.shape)
            MEMSET('pool', t_[(slice(None),) * nd], 0.0, [n_])
        TT('dve', lamt[:, 0:64], lamt[:, 0:64], lamt[:, 64:128], ALU.mult, ['tmp0'], ['tmp0'])
        TT('dve', lamt[:, 128:192], lamt[:, 128:192], lamt[:, 192:256], ALU.mult, ['tmp0'], ['tmp0'])
        P.op('dve', lambda e: e.tensor_reduce(out=lamw[:, 0:1], in_=lamt[:, 0:64], axis=AX.X, op=ALU.add), ['tmp0'], ['lamw'])
        P.op('dve', lambda e: e.tensor_reduce(out=lamw[:, 1:2], in_=lamt[:, 128:192], axis=AX.X, op=ALU.add), ['tmp0'], ['lamw'])
        ACT(lamw[:, 2:4], lamw[:, 0:2], AF.Exp, ['lamw'], ['lamw2'])
        TT('dve', lamw[:, 4:5], lamw[:, 3:4], lamw[:, 2:3], ALU.subtract, ['lamw2'], ['lamw3'])
        TS('dve', neglam[:, :], lamw[:, 4:5], -LAM_INIT, None, ALU.add, None, ['lamw3'], ['neglam'])

        stg32 = [kTc[:, :, :].rearrange("p a b -> p (a b)").bitcast(F32), Vc[:, :, :].rearrange("p a b -> p (a b)").bitcast(F32)]
        stgbf = [kTc[:, :, :].rearrange("p a b -> p (a b)"), Vc[:, :, :].rearrange("p a b -> p (a b)")]
        stn = ['kTc', 'Vc']
        BO = 9216
        def TSX(e_, o_, i_, sc_, r_, w_):
            if e_ == 'act':
                ACT(o_, i_, AF.Copy, r_, w_, scale=sc_)
            else:
                TS(e_, o_, i_, sc_, None, ALU.mult, None, r_, w_)

        for c in range(8):
            s_ = c % 2
            src = stg32[s_]
            dst = stgbf[s_]
            DMA('sync', f'ld_s{s_}', src[:, 0:DIN], win_d[c * 128:(c + 1) * 128, :], (), [stn[s_]])
            ng = pp[:, PP_NG + c:PP_NG + c + 1]
            eng = 'dve' if c % 2 == 0 else 'act'
            TSX(eng, dst[:, BO:BO + 2048], src[:, 0:2048], ng, [stn[s_], 'pp'], [stn[s_]])
            TSX(eng, dst[:, BO + OFF_WDAD:BO + OFF_WDAD + 128], src[:, 3584:3712], ng, [stn[s_], 'pp'], [stn[s_]])
            for j in range(3):
                outap = dst[:, BO + OFF_RKV:BO + OFF_RKV + 1536].rearrange("p (a b) -> p a b", b=384)[:, :, j * 128:(j + 1) * 128]
                inap = src[:, 2048 + j * 512:2048 + (j + 1) * 512].rearrange("p (a b) -> p a b", b=128)
                TSX(eng, outap, inap, ng, [stn[s_], 'pp'], [stn[s_]])
            TSX(eng, dst[:, BO + OFF_GB:BO + OFF_GB + 512], src[:, 3712:4224], ng, [stn[s_], 'pp'], [stn[s_]])
            DMA('sync', f'st_w{s_}', wbf_d[c * 128:(c + 1) * 128, :], dst[:, BO:BO + DIN], [stn[s_]], [('wbf', c, c + 1)])
        for c in range(8):
            s_ = c % 2
            src = stg32[s_]
            dst = stgbf[s_]
            DMA('sync', f'ld_s{s_}', src[:, 0:D], wout_d[c * 128:(c + 1) * 128, :], (), [stn[s_]])
            CP('dve' if c % 2 == 0 else 'act', dst[:, BO:BO + D], src[:, 0:D], [stn[s_]], [stn[s_]])
            DMA('sync', f'st_w{s_}', wobf_d[c * 128:(c + 1) * 128, :], dst[:, BO:BO + D], [stn[s_]], [('wobf', c, c + 1)])

        wbf_v = wbf_d.rearrange("(c p) n -> p c n", p=128)
        wobf_v = wobf_d.rearrange("(c p) n -> p c n", p=128)

        NS = 5
        wctr = [0]

        def load_w(view, off, ncols, rname):
            i = wctr[0] % NS
            wctr[0] += 1
            DMA('sync', f'ld_w{i}', wslot[i][:, :, 0:ncols], view[:, :, off:off + ncols], [rname], [f'wslot{i}'])
            return wslot[i], f'wslot{i}'

        tctr = [0]

        def T_():
            i = tctr[0] % NTMP
            tctr[0] += 1
            return tmp[i][:, :], f'tmp{i}'

        qctr = [0]

        def Q_():
            i = qctr[0] % NQ
            qctr[0] += 1
            return qbuf[i], f'qbuf{i}'

        stctr = [0]
        xctr = [0]
        rctr = {'A': 0, 'R': 0}
        BANKS = {'A': [0, 1, 2, 3], 'R': [4, 5, 6]}

        def bank(stream):
            bl = BANKS[stream]
            i = bl[rctr[stream] % len(bl)]
            rctr[stream] += 1
            return B[i], f'B{i}'

        def proj_ftile(ws, wsn, col, stream):
            bk, bn = bank(stream)
            o = bk[:, 0:ST]
            for c in range(8):
                MM(o, ws[:, c, col:col + 128], xnT[:, c, :], c == 0, c == 7, [wsn, 'xnT'], [bn])
            return o, bn

        def shift_evac(o, on, mucol, dst, dstn):
            tm = tmpm[stctr[0] % 2]
            tmn = f'tmpm{stctr[0] % 2}'
            stctr[0] += 1
            ACT(tm[:, 1:ST + 1], o, AF.Copy, [on, 'pp'], [tmn], scale=pp[:, PP_MU + mucol:PP_MU + mucol + 1])
            CP('pool', tm[:, 0:1], carry[:, mucol:mucol + 1], ['carry'], [tmn])
            STT(dst, o, omu[:, mucol:mucol + 1], tm[:, 0:ST], ALU.mult, ALU.add, [on, tmn, 'omu'], [dstn])
            CP('pool', carry[:, mucol:mucol + 1], tm[:, ST:ST + 1], [tmn], ['carry'])

        def bcol(p, lo, n):
            return bT[:, p * ST + lo:p * ST + lo + n]

        def kcol(p, lo, n):
            return ktT[:, p * ST + lo:p * ST + lo + n]

        def ycol(p, lo, n):
            return yT[:, p * ST + lo:p * ST + lo + n]

        def ar(p, e, c, a, n):
            off = (((p * 2 + e) * 4 + c) * 2 + a) * CH
            return arT[:, off:off + n]

        MNbf = MNb[:, :, :].rearrange("p h t -> p (h t)")
        MNkf = MNk[:, :, :].rearrange("p h t -> p (h t)")

        def merge(gens):
            prog = [0] * len(gens)
            alive = [True] * len(gens)
            while any(alive):
                best = None
                for i, (g, n) in enumerate(gens):
                    if alive[i]:
                        f = prog[i] / max(n, 1)
                        if best is None or f < best[0]:
                            best = (f, i)
                i = best[1]
                try:
                    next(gens[i][0])
                    prog[i] += 1
                except StopIteration:
                    alive[i] = False

        sbi = [0]
        def head_gen(b, st):
            row0 = b * S + st * ST
            t0 = st * ST
            for tt in range(2):
                xi = xctr[0] % 2
                xctr[0] += 1
                xb = xbuf[xi]
                xn_ = f'xbuf{xi}'
                DMA('sync', f'ld_x{xi}', xb[:, :], x_d[row0 + tt * 128:row0 + (tt + 1) * 128, :], (), [xn_])
                ACT(junk[:, :], xb[:, :], AF.Square, [xn_], ['junk', 'stat0'], accum=stat[:, 0:1])
                ACT(stat[:, 1:2], stat[:, 0:1], AF.Sqrt, ['stat0'], ['stat1'], scale=1.0 / D, bias=RMS_EPS)
                RECIP(stat[:, 2:3], stat[:, 1:2], ['stat1'], ['stat2'])
                TS('dve', xb[:, :], xb[:, :], stat[:, 2:3], None, ALU.mult, None, [xn_, 'stat2'], [xn_])
                for c in range(8):
                    bk = B[c // 4]
                    TR(bk[:, (c % 4) * 128:(c % 4) * 128 + 128], xb[:, c * 128:(c + 1) * 128], ident, [xn_, 'cst'], [f'B{c // 4}'])
                for hb in range(2):
                    CP('act' if hb == 0 else 'dve', xnT[:, hb * 4:hb * 4 + 4, tt * 128:(tt + 1) * 128],
                       B[hb][:, :].rearrange("p (c t) -> p c t", t=128), [f'B{hb}'], ['xnT'])
                yield
            for half in range(2):
                ws, wsn = load_w(wbf_v, OFF_K + half * 256, 256, 'wbf')
                for f in range(2):
                    h = half * 2 + f
                    o, on = proj_ftile(ws, wsn, f * 128, 'A')
                    CP('act', kTc[:, h, t0:t0 + ST], o, [on], [('kTc', h * 4096 + t0, h * 4096 + t0 + ST)])
                    yield
            for half in range(2):
                ws, wsn = load_w(wbf_v, OFF_VA + half * 256, 256, 'wbf')
                for tt in range(2):
                    bk, bn = bank('A')
                    for c in range(8):
                        MM(bk[:, 0:256], xnT[:, c, tt * 128:(tt + 1) * 128], ws[:, c, 0:256], c == 0, c == 7, [wsn, 'xnT'], [bn])
                    kt = st * 2 + tt
                    CP('dve', Vc[:, kt, half * 256:half * 256 + 256], bk[:, 0:256], [bn], [('Vc', kt * 512, kt * 512 + 512)])
                    yield
            for half in range(2):
                ws, wsn = load_w(wbf_v, OFF_Q + half * 256, 256, 'wbf')
                for f in range(2):
                    h = half * 2 + f
                    o, on = proj_ftile(ws, wsn, f * 128, 'A')
                    CP('act', qT[0:64, h, 0, :], o[0:64, :], [on], ['qT'])
                    CP('dve', qT[64:128, h, 1, :], o[64:128, :], [on], ['qT'])
                    yield
            for half in range(2):
                ws, wsn = load_w(wbf_v, OFF_GA + half * 256, 256, 'wbf')
                for f in range(2):
                    h = half * 2 + f
                    o, on = proj_ftile(ws, wsn, f * 128, 'A')
                    ACT(sga[:, h, :], o, AF.Silu, [on], ['sga'])
                    yield

            yield

        def rwkv_gen(b, st):
            if st == 0:
                MEMSET('pool', S32[:, :], 0.0, ['S32'])
                MEMSET('pool', Sbf[sbi[0]][:, :], 0.0, [f'Sbf{sbi[0]}'])
                MEMSET('pool', carry[:, :], 0.0, ['carry'])
            ws, wsn = load_w(wbf_v, OFF_WDAD, 128, 'wbf')
            o, on = proj_ftile(ws, wsn, 0, 'R')
            shift_evac(o, on, 12, wdad[:, :], 'wdad')
            ACT(linbf[0:64, 0, :], wdad[0:64, :], AF.Tanh, ['wdad'], ['linbf'])
            CP('dve', linbf[64:128, 1, :], wdad[64:128, :], ['wdad'], ['linbf'])
            yield
            def prep_gen(p, tset):
                (sg, sgn), (aT, aTn), (cs, csn), (csp, cspn), (Egi, Egin), (kraw, krawn), (ksq, ksqn), (ka, kan) = tset
                Eg, Egn = sg, sgn
                Egp, Egpn = csp, cspn
                kk, kkn = ksq, ksqn
                rk = rkv[p % 2]
                rkn = f'rkv{p % 2}'
                ws, wsn = load_w(wbf_v, OFF_RKV + p * 384, 256, 'wbf')
                for j in range(2):
                    o, on = proj_ftile(ws, wsn, j * 128, 'R')
                    shift_evac(o, on, j * 4 + p, rk[:, j, :], rkn)
                    yield
                ws, wsn = load_w(wbf_v, OFF_RKV + p * 384 + 256, 128, 'wbf')
                o, on = proj_ftile(ws, wsn, 0, 'R')
                shift_evac(o, on, 2 * 4 + p, rk[:, 2, :], rkn)
                yield
                rs, ks, vs = rk[:, 0, :], rk[:, 1, :], rk[:, 2, :]
                bk, bn = bank('R')
                MM(bk[:, 0:256], lwbf[:, p * 128:(p + 1) * 128], linbf[:, 0, :], True, True, ['lwbf', 'linbf'], [bn])
                MM(bk[:, 256:512], lwbf[:, p * 128:(p + 1) * 128], linbf[:, 1, :], True, True, ['lwbf', 'linbf'], [bn])
                ACT(sg, bk[:, 0:256], AF.Sigmoid, [bn, 'pp'], [sgn], bias=pp[:, PP_W0 + p:PP_W0 + p + 1])
                ACT(aT, bk[:, 256:512], AF.Sigmoid, [bn, 'pp'], [aTn], bias=pp[:, PP_A0 + p:PP_A0 + p + 1])
                TS('dve', kraw, ks, pp[:, PP_KK + p:PP_KK + p + 1], None, ALU.mult, None, [rkn, 'pp'], [krawn])
                TT('pool', ksq, kraw, kraw, ALU.mult, [krawn], [ksqn])
                yield
                P.op('dve', lambda e, cs=cs, sg=sg: e.tensor_tensor_scan(out=cs, data0=scanmask, data1=sg, initial=0.0, op0=ALU.mult, op1=ALU.add), [sgn, 'cst'], [csn])
                bk2, bn2 = bank('R')
                MM(bk2[:, 0:256], blockones, ksq, True, True, ['cst', ksqn], [bn2])
                TS('dve', ksq, bk2[:, 0:256], 1e-24, None, ALU.max, None, [bn2], [ksqn])
                TT('pool', csp, cs, sg, ALU.subtract, [csn, sgn], [cspn])
                yield
                ACT(Eg, cs, AF.Exp, [csn], [Egn], scale=-C0)
                ACT(Egi, cs, AF.Exp, [csn], [Egin], scale=C0)
                ACT(Egp, csp, AF.Exp, [cspn], [Egpn], scale=-C0)
                ACT(ksq, ksq, AF.Sqrt, [ksqn], [ksqn])
                TS('dve', ka, aT, pp[:, PP_KA + p:PP_KA + p + 1], omka[:, p:p + 1], ALU.mult, ALU.add, [aTn, 'pp', 'omka'], [kan])
                TT('dve', ka, ks, ka, ALU.mult, [rkn, kan], [kan])
                yield
                CP('pool', gam[:, p, :], Eg.rearrange("p (c t) -> p c t", t=CH)[:, :, CH - 1], [Egn], ['gam'])
                RECIP(ksq, ksq, [ksqn], [ksqn])
                TT('dve', kk, kraw, ksq, ALU.mult, [krawn, ksqn], [kkn])
                STT(kraw, rs, pp[:, PP_RK + p:PP_RK + p + 1], ka, ALU.mult, ALU.mult, [rkn, kan, 'pp'], [krawn])
                bk3, bn3 = bank('R')
                MM(bk3[:, 0:256], blockones, kraw, True, True, ['cst', krawn], [bn3])
                TT('dve', bonus[:, p, :], bk3[:, 0:256], vs, ALU.mult, [bn3, rkn], ['bonus'])
                yield
                for e_ in range(2):
                    PR_ = slice(64 * e_, 64 * e_ + 64)
                    base = (p * 2 + e_) * 512
                    av = arT[PR_, base:base + 512].rearrange("p (c a t) -> p c a t", c=4, a=2)
                    TT('dve' if e_ == 0 else 'pool', av[:, :, 1, :], rs[PR_, :].rearrange("p (c t) -> p c t", t=CH),
                       Eg[PR_, :].rearrange("p (c t) -> p c t", t=CH), ALU.mult, [rkn, Egn], ['arT'])
                    STT(av[:, :, 0, :], kk[PR_, :].rearrange("p (c t) -> p c t", t=CH), -1.0,
                        Egp[PR_, :].rearrange("p (c t) -> p c t", t=CH), ALU.mult, ALU.mult, [kkn, Egpn], ['arT'])
                yield
                TT('pool', kk, kk, aT, ALU.mult, [kkn, aTn], [kkn])
                TT('dve', bcol(p, 0, ST), kk, Egi, ALU.mult, [kkn, Egin], ['bT'])
                TT('dve', kcol(p, 0, ST), ka, Egi, ALU.mult, [kan, Egin], ['ktT'])
                CP('pool', ycol(p, 0, ST), vs, [rkn], ['yT'])
                yield

            setA = [(tmp[i][:, :], f'tmp{i}') for i in range(8)]
            setB = [(xbuf[i // 4][:, (i % 4) * ST:(i % 4 + 1) * ST], (f'xbuf{i // 4}', (i % 4) * ST, (i % 4 + 1) * ST)) for i in range(8)]
            for pp_ in range(0, 4, 2):
                subs = [prep_gen(pp_, setA), prep_gen(pp_ + 1, setB)]
                live = [True, True]
                while any(live):
                    for si in range(2):
                        if live[si]:
                            try:
                                next(subs[si])
                            except StopIteration:
                                live[si] = False
                    yield
            for half in range(2):
                ws, wsn = load_w(wbf_v, OFF_GB + half * 256, 256, 'wbf')
                for f in range(2):
                    h = half * 2 + f
                    o, on = proj_ftile(ws, wsn, f * 128, 'R')
                    ACT(sgb[:, h, :], o, AF.Silu, [on], ['sgb'])
                    yield
            for c in range(0 if os.environ.get('KNOCHUNK') else 4):
                c0 = c * CH
                cres = ('yT', c0, c0 + CH)
                for p in range(4):
                    TR(PTB[:, p * 128:(p + 1) * 128], bcol(p, c0, 128), identbf[:, :], ['bT', 'identbf'], ['PTB'])
                    TR(PTB[:, 512 + p * 128:512 + (p + 1) * 128], kcol(p, c0, 128), identbf[:, :], ['ktT', 'identbf'], ['PTB'])
                CP('act', bkTF[0:64, :], PTB[0:64, :], ['PTB'], ['bkTF'])
                for p in range(4):
                    TR(B[4][:, p * 128:(p + 1) * 128], ycol(p, c0, 128), ident, [cres, 'cst'], ['B4'])
                CP('dve', VTF[0:64, :], B[4][0:64, :], ['B4'], ['VTF'])
                yield
                mA = maskA.unsqueeze(1).broadcast_to([64, 4, 128])
                for h in range(8):
                    p, e = h // 2, h % 2
                    MM(B[5 + h // 4][:, (h % 4) * 128:(h % 4) * 128 + 128], bcol(p, c0, 128), ar(p, e, c, 0, 128), True, True, ['bT', 'arT'], [f'B{5 + h // 4}'])
                for hf in range(2):
                    TT('dve', MNb[0:64, hf * 4:hf * 4 + 4, :], B[5 + hf][0:64, :].rearrange("p (h t) -> p h t", t=128), mA, ALU.mult, [f'B{5 + hf}', 'cst'], ['MNb'])
                for h in range(8):
                    p, e = h // 2, h % 2
                    MM(B[4][:, h * 64:(h + 1) * 64], ar(p, e, c, 0, 128), bcol(p, c0, 64), True, True, ['bT', 'arT'], ['B4'])
                QT, QTn = Q_()
                mC = maskC.unsqueeze(1).broadcast_to([64, 8, 64])
                TT('dve', QT[0:64, 0:8, :], B[4][0:64, :].rearrange("p (h t) -> p h t", t=64), mC, ALU.mult, ['B4', 'cst'], [QTn])
                yield
                for h in range(8):
                    p, e = h // 2, h % 2
                    MM(B[5 + h // 4][:, (h % 4) * 128:(h % 4) * 128 + 128], kcol(p, c0, 128), ar(p, e, c, 0, 128), True, True, ['ktT', 'arT'], [f'B{5 + h // 4}'])
                for hf in range(2):
                    TT('dve', MNk[0:64, hf * 4:hf * 4 + 4, :], B[5 + hf][0:64, :].rearrange("p (h t) -> p h t", t=128), mA, ALU.mult, [f'B{5 + hf}', 'cst'], ['MNk'])
                ti = 0
                Tc, Tn = Tb[0], 'Tb0'
                TT('pool', Tc[0:64, 0:8, :], MNb[0:64, 0:8, 0:64], ident[0:64, 0:64].unsqueeze(1).broadcast_to([64, 8, 64]), ALU.add, ['MNb', 'cst'], [Tn])
                yield
                Qr = lambda h: MNb[:, h, 0:64]
                Ql = lambda h: MNbf[:, h * 128:h * 128 + 128]
                Qn = 'MNb'
                QTf = QT[:, :, :].rearrange("p h t -> p (h t)")
                QTr = lambda h, QT=QT: QT[:, h, :]
                QTl = lambda h, QTf=QTf: QTf[:, h * 64:h * 64 + 128]
                pend = None
                for lvl in range(1, 7):
                    need_q = lvl <= 4
                    need_qt = lvl <= 5
                    if need_q:
                        for h in range(8):
                            MM(B[4][:, h * 64:(h + 1) * 64], QTl(h), Qr(h), True, True, [QTn, Qn], ['B4'])
                    if need_qt:
                        for h in range(8):
                            MM(B[5][:, h * 64:(h + 1) * 64], Ql(h), QTr(h), True, True, [QTn, Qn], ['B5'])
                    if pend is not None:
                        pl, pn = pend
                        Tf = Tc[:, :, :].rearrange("p h t -> p (h t)")
                        for h in range(8):
                            MM(B[6][:, h * 64:(h + 1) * 64], pl(h), Tc[:, h, :], True, True, [pn, Tn], ['B6'])
                    if need_q:
                        Q2, Q2n = Q_()
                        CP('act', Q2[0:64, 0:8, :], B[4][0:64, :].rearrange("p (h t) -> p h t", t=64), ['B4'], [Q2n])
                    if need_qt:
                        QT2, QT2n = Q_()
                        CP('act', QT2[0:64, 0:8, :], B[5][0:64, :].rearrange("p (h t) -> p h t", t=64), ['B5'], [QT2n])
                    if pend is not None:
                        ti ^= 1
                        Tnew, Tnn = Tb[ti], f'Tb{ti}'
                        TT('dve', Tnew[0:64, 0:8, :], B[6][0:64, :].rearrange("p (h t) -> p h t", t=64), Tc[0:64, 0:8, :], ALU.add, ['B6', Tn], [Tnn])
                        Tc, Tn = Tnew, Tnn
                    if need_qt:
                        QT2f = QT2[:, :, :].rearrange("p h t -> p (h t)")
                        pend = ((lambda h, f_=QT2f: f_[:, h * 64:h * 64 + 128]), QT2n)
                        QTr = lambda h, QT2=QT2: QT2[:, h, :]
                        QTl = pend[0]
                        QTn = QT2n
                    else:
                        pend = None
                    if need_q:
                        Q2f = Q2[:, :, :].rearrange("p h t -> p (h t)")
                        Qr = lambda h, Q2=Q2: Q2[:, h, :]
                        Ql = lambda h, f_=Q2f: f_[:, h * 64:h * 64 + 128]
                        Qn = Q2n
                    yield
                Tf = Tc[:, :, :].rearrange("p h t -> p (h t)")
                So, Son = Sbf[sbi[0]], f'Sbf{sbi[0]}'
                for h in range(8):
                    p, e = h // 2, h % 2
                    MM(B[4][:, h * 64:(h + 1) * 64], ar(p, e, c, 0, 128), So[:, p * 128 + e * 64:p * 128 + e * 64 + 64], True, False, ['arT', Son], ['B4'])
                    MM(B[4][:, h * 64:(h + 1) * 64], MNkf[:, h * 128:h * 128 + 128], VTF[:, h * 64:(h + 1) * 64], False, True, ['MNk', 'VTF'], ['B4'])
                CP('act', Wsb[0:64, :], B[4][0:64, :], ['B4'], ['Wsb'])
                yield
                for h in range(8):
                    MM(B[5][:, h * 64:(h + 1) * 64], Tf[:, h * 64:h * 64 + 128], Wsb[:, h * 64:(h + 1) * 64], True, True, [Tn, 'Wsb'], ['B5'])
                CP('act', Usb[0:64, :], B[5][0:64, :], ['B5'], ['Usb'])
                yield
                for h in range(8):
                    p, e = h // 2, h % 2
                    o = B[6][:, h * 64:(h + 1) * 64]
                    MM(o, ar(p, e, c, 1, 128), So[:, p * 128 + e * 64:p * 128 + e * 64 + 64], True, False, ['arT', Son], ['B6'])
                    MM(o, MNbf[:, h * 128 + 64:h * 128 + 192], Usb[:, h * 64:(h + 1) * 64], False, False, ['MNb', 'Usb'], ['B6'])
                    MM(o, MNkf[:, h * 128 + 64:h * 128 + 192], VTF[:, h * 64:(h + 1) * 64], False, True, ['MNk', 'VTF'], ['B6'])
                CP('act', Ytf[0:64, :], B[6][0:64, :], ['B6'], ['Ytf'])
                for p in range(4):
                    o = B[4][:, p * 128:(p + 1) * 128]
                    MM(o, bkTF[:, p * 128:(p + 1) * 128], Usb[:, p * 128:(p + 1) * 128], True, False, ['bkTF', 'Usb'], ['B4'])
                    MM(o, bkTF[:, 512 + p * 128:512 + (p + 1) * 128], VTF[:, p * 128:(p + 1) * 128], False, True, ['bkTF', 'VTF'], ['B4'])
                TT('dve', tmpS[:, :], B[4][:, :], S32[:, :], ALU.add, ['B4', 'S32'], ['tmpS'])
                gbc = gam[:, :, c:c + 1].broadcast_to([128, 4, 128])
                tS3 = tmpS[:, :].rearrange("p (a b) -> p a b", b=128)
                TT('dve', S32[:, :].rearrange("p (a b) -> p a b", b=128), tS3, gbc, ALU.mult, ['tmpS', 'gam'], ['S32'])
                sbi[0] ^= 1
                TT('pool', Sbf[sbi[0]][:, :].rearrange("p (a b) -> p a b", b=128), tS3, gbc, ALU.mult, ['tmpS', 'gam'], [f'Sbf{sbi[0]}'])
                yield
                for p in range(4):
                    TR(B[5][:, p * 128:(p + 1) * 128], Ytf[:, p * 128:(p + 1) * 128], ident, ['Ytf', 'cst'], ['B5'])
                for p in range(4):
                    CP('dve' if p % 2 == 0 else 'act', ycol(p, c0, CH), B[5][:, p * 128:p * 128 + CH], ['B5'], [cres])
                yield

        def tail_gen(b, st):
            row0 = b * S + st * ST
            def epi_gen(p):
                y = ycol(p, 0, ST)
                (sq, sqn), (m, mn), (msq, msqn) = [(tmp[3 * p + i][:, :], f'tmp{3 * p + i}') for i in range(3)]
                bk, bn = bank('R')
                MM(bk[:, 0:256], blockones, y, True, True, ['cst', 'yT'], [bn])
                TT('pool', sq, y, y, ALU.mult, ['yT'], [sqn])
                MM(bk[:, 256:512], blockones, sq, True, True, ['cst', sqn], [bn])
                TS('dve', m, bk[:, 0:256], 1.0 / 64, None, ALU.mult, None, [bn], [mn])
                TT('pool', msq, m, m, ALU.mult, [mn], [msqn])
                STT(msq, bk[:, 256:512], 1.0 / 64, msq, ALU.mult, ALU.subtract, [bn, msqn], [msqn])
                yield
                ACT(msq, msq, AF.Sqrt, [msqn], [msqn], bias=GN_EPS)
                RECIP(msq, msq, [msqn], [msqn])
                TT('dve', m, y, m, ALU.subtract, ['yT', mn], [mn])
                yield
                TT('dve', m, m, msq, ALU.mult, [mn, msqn], [mn])
                TS('dve', m, m, pp[:, PP_LNG + p:PP_LNG + p + 1], pp[:, PP_LNB + p:PP_LNB + p + 1], ALU.mult, ALU.add, [mn, 'pp'], [mn])
                TT('pool', m, m, bonus[:, p, :], ALU.add, [mn, 'bonus'], [mn])
                TT('dve', mixT[:, 4 + p, :], m, sgb[:, p, :], ALU.mult, [mn, 'sgb'], ['mixT'])
                yield

            subs = [epi_gen(p) for p in range(4)]
            live = [True] * 4
            while any(live):
                for si in range(4):
                    if live[si]:
                        try:
                            next(subs[si])
                        except StopIteration:
                            live[si] = False
                yield
            for tt in range(2):
                DMA('sync', f'ld_h{tt}', hres[tt][:, :], x_d[row0 + tt * 128:row0 + (tt + 1) * 128, :], (), [f'hres{tt}'])
            for n in range(4):
                ws, wsn = load_w(wobf_v, n * 256, 256, 'wobf')
                for tt in range(2):
                    bk, bn = bank('A')
                    for f in range(8):
                        MM(bk[:, 0:256], mixT[:, f, tt * 128:(tt + 1) * 128], ws[:, f, 0:256], f == 0, f == 7, ['mixT', wsn], [bn])
                    TT('dve', hres[tt][:, n * 256:(n + 1) * 256], bk[:, 0:256], hres[tt][:, n * 256:(n + 1) * 256], ALU.add, [bn, f'hres{tt}'], [f'hres{tt}'])
                    yield
            for tt in range(2):
                ACT(junk[:, :], hres[tt][:, :], AF.Square, [f'hres{tt}'], ['junk', 'stat4'], accum=stat[:, 4:5])
                ACT(stat[:, 5:6], stat[:, 4:5], AF.Sqrt, ['stat4'], ['stat5'], scale=1.0 / D, bias=RMS_EPS)
                RECIP(stat[:, 6:7], stat[:, 5:6], ['stat5'], ['stat6'])
                STT(hres[tt][:, :], hres[tt][:, :], stat[:, 6:7], fg[:, :], ALU.mult, ALU.mult, [f'hres{tt}', 'stat6', 'fg'], [f'hres{tt}'])
                DMA('pool', f'st_{tt}', out_d[row0 + tt * 128:row0 + (tt + 1) * 128, :], hres[tt][:, :], [f'hres{tt}'], ['out'])

            yield

        def att_gen(b, st):
            nkt = 2 * st + 2

            items = [(h, kt) for h in range(0 if os.environ.get('KNOATT') else 4) for kt in range(nkt)]

            def emit_scores(i):
                h, kt = items[i]
                j = kt - 2 * st
                col0 = 128 * max(j, 0)
                bk, bn = B[i % 2], f'B{i % 2}'
                kres = ('kTc', h * 4096 + kt * 128, h * 4096 + kt * 128 + 128)
                if col0 == 0:
                    MM(bk[:, :], kTc[:, h, kt * 128:(kt + 1) * 128], qT[:, h, :, :].rearrange("p m q -> p (m q)"), True, True, [kres, 'qT'], [bn])
                else:
                    for m_ in range(2):
                        MM(bk[:, m_ * ST + col0:(m_ + 1) * ST], kTc[:, h, kt * 128:(kt + 1) * 128], qT[:, h, m_, col0:ST], True, True, [kres, 'qT'], [bn])

            def emit_exp(i):
                h, kt = items[i]
                j = kt - 2 * st
                col0 = 128 * max(j, 0)
                bk, bn = B[i % 2], f'B{i % 2}'
                pt = Pt[i % 2]
                ptn = f'Pt{i % 2}'
                sview = bk[:, :].rearrange("p (m q) -> p m q", q=ST)[:, :, col0:ST]
                pview = pt[:, :, col0:ST]
                ACT(pview, sview, AF.Exp, [bn], [ptn], scale=0.125)
                if j >= 0:
                    MEMSET('pool', pt[64:128, :, col0:col0 + 64], 0.0, [ptn])

            def emit_pv(i):
                h, kt = items[i]
                j = kt - 2 * st
                col0 = 128 * max(j, 0)
                pt = Pt[i % 2]
                ptn = f'Pt{i % 2}'
                vv = Vc[:, kt, h * 128:(h + 1) * 128]
                vres = ('Vc', kt * 512, kt * 512 + 512)
                last = kt == nkt - 1
                if col0 == 0:
                    p2 = pt[:, :, :].rearrange("p m q -> p (m q)")
                    MM(B[2][:, :], vv, p2, kt == 0, last, [vres, ptn], ['B2'])
                    MM(B[3][:, :], onesbf[:, :], p2, kt == 0, last, ['onesbf', ptn], ['B3'])
                else:
                    for m_ in range(2):
                        MM(B[2][:, m_ * ST + col0:(m_ + 1) * ST], vv, pt[:, m_, col0:ST], False, last and m_ == 1, [vres, ptn], ['B2'])
                    for m_ in range(2):
                        MM(B[3][:, m_ * ST + col0:(m_ + 1) * ST], onesbf[:, :], pt[:, m_, col0:ST], False, last and m_ == 1, ['onesbf', ptn], ['B3'])
                if last:
                    a1 = att1[:, :, :].rearrange("p m q -> p (m q)")
                    a2 = att2[:, :, :].rearrange("p m q -> p (m q)")
                    RECIP(a1, B[3][:, :], ['B3'], ['att1'])
                    TT('dve', a2, B[2][:, :], a1, ALU.mult, ['B2', 'att1'], ['att2'])
                    o_, on_ = att1[:, 0, :], 'att1'
                    STT(o_, att2[:, 1, :], neglam[:, 0:1], att2[:, 0, :], ALU.mult, ALU.add, ['att2', 'neglam'], [on_])
                    sq = att1[:, 1, :]
                    TT('pool', sq, o_, o_, ALU.mult, [on_], [on_])
                    sbk, sbn = B[(i + 1) % 2], f'B{(i + 1) % 2}'
                    MM(sbk[:, 0:256], ones32[:, :], sq, True, True, ['ones32', on_], [sbn])
                    ACT(sq, sbk[:, 0:256], AF.Sqrt, [sbn], [on_], scale=1.0 / 128, bias=SUBLN_EPS)
                    RECIP(sq, sq, [on_], [on_])
                    STT(o_, o_, g08[:, 0:1], sq, ALU.mult, ALU.mult, [on_, 'g08'], [on_])
                    TT('dve', mixT[:, h, :], o_, sga[:, h, :], ALU.mult, [on_, 'sga'], ['mixT'])

            n_it = len(items)
            for s_ in range(n_it + 2 if n_it else 0):
                if s_ < n_it:
                    emit_scores(s_)
                if 0 <= s_ - 1 < n_it:
                    emit_exp(s_ - 1)
                if 0 <= s_ - 2 < n_it:
                    emit_pv(s_ - 2)
                yield

        items_ = [(b, st) for b in range(NB) for st in range(NST)]
        merge([(head_gen(*items_[0]), 1)])
        for ii, (b, st) in enumerate(items_):
            merge([(rwkv_gen(b, st), int(os.environ.get("KREST", "70"))), (att_gen(b, st), 4 * (2 * st + 2))])
            gl = [(tail_gen(b, st), 15)]
            if ii + 1 < len(items_):
                gl.append((head_gen(*items_[ii + 1]), 23))
            merge(gl)

        P.final_wait('pool', ['st_0', 'st_1'])

        with nc.Block() as block:
            @block.sync
            def _(e):
                for f in P.engs['sync']['thunks']:
                    f(e)

            @block.tensor
            def _(e):
                for f in P.engs['pe']['thunks']:
                    f(e)

            @block.scalar
            def _(e):
                for f in P.engs['act']['thunks']:
                    f(e)

            @block.vector
            def _(e):
                for f in P.engs['dve']['thunks']:
                    f(e)

            @block.gpsimd
            def _(e):
                for f in P.engs['pool']['thunks']:
                    f(e)
    return nc


def make_consts():
    cst = np.zeros((128, 832), np.float32)
    cst[:, 0:128] = np.eye(128, dtype=np.float32)
    cst[0:64, 128:192] = 1.0
    cst[64:128, 192:256] = 1.0
    j = np.arange(64)[:, None]
    t = np.arange(64)[None, :]
    cst[0:64, 256:320] = (j < t)
    cst[0:64, 320:384] = (j <= t)
    cst[0:64, 384:448] = (j > t)
    sm = np.ones(256, np.float32)
    sm[::64] = 0.0
    cst[:, 576:832] = sm[None, :]
    return cst


def prep_params(norm_g, lambda_q1, lambda_k1, lambda_q2, lambda_k2, subln_g, shift_mu, w0, w2, a0, a2,
                k_k, k_a, r_k, ln_x_g, ln_x_b, final_g):
    f = lambda v, n: np.ascontiguousarray(np.asarray(v, np.float32).reshape(n, 128).T)
    pp = np.zeros((128, PP_N), np.float32)
    pp[:, PP_MU:PP_MU + 13] = f(shift_mu[0], 13)
    pp[:, PP_W0:PP_W0 + 4] = f(w0[0], 4)
    pp[:, PP_A0:PP_A0 + 4] = f(a0[0], 4)
    pp[:, PP_KK:PP_KK + 4] = f(k_k[0], 4)
    pp[:, PP_KA:PP_KA + 4] = f(k_a[0], 4)
    pp[:, PP_RK:PP_RK + 4] = f(r_k[0].reshape(-1), 4)
    pp[:, PP_LNG:PP_LNG + 4] = f(ln_x_g[0], 4)
    pp[:, PP_LNB:PP_LNB + 4] = f(ln_x_b[0], 4)
    pp[:, PP_SUB] = np.asarray(subln_g[0], np.float32)
    pp[:, PP_NG:PP_NG + 8] = f(norm_g[0], 8)
    lam4 = np.concatenate([np.broadcast_to(np.asarray(v[0], np.float32)[None, :], (128, 64))
                           for v in (lambda_q1, lambda_k1, lambda_q2, lambda_k2)], axis=1)
    lw = np.concatenate([np.asarray(w2[0], np.float32), np.asarray(a2[0], np.float32)], axis=0)
    fg = np.broadcast_to(np.asarray(final_g, np.float32)[None, :], (128, D))
    return dict(pp=pp, lam4=np.ascontiguousarray(lam4), lw=np.ascontiguousarray(lw), fg=np.ascontiguousarray(fg),
                cst=make_consts())


_CACHE = {}


def run(x, w_in, w_out, small, n_cores):
    x = np.asarray(x, np.float32)
    Bt, S, _ = x.shape
    NB = Bt // n_cores
    key = (NB, S)
    if key not in _CACHE:
        _CACHE[key] = build(NB, S)
    nc = _CACHE[key]
    w_in2 = np.ascontiguousarray(np.asarray(w_in, np.float32)[0])
    w_out2 = np.ascontiguousarray(np.asarray(w_out, np.float32)[0])
    in_maps = []
    for c in range(n_cores):
        m = dict(small)
        m["x"] = np.ascontiguousarray(x[c * NB:(c + 1) * NB].reshape(NB * S, D))
        m["w_in"] = w_in2
        m["w_out"] = w_out2
        in_maps.append(m)
    res = run_bass_kernel_spmd(nc, in_maps, core_ids=list(range(n_cores)))
    outs = [np.asarray(r["out"]).reshape(NB, S, D) for r in res.results]
    return np.concatenate(outs, axis=0).astype(np.float32)


def kernel(x, norm_g, w_in, lambda_q1, lambda_k1, lambda_q2, lambda_k2, subln_g,
           shift_mu, w0, w2, a0, a2, k_k, k_a, r_k, ln_x_g, ln_x_b, w_out, final_g):
    small = prep_params(norm_g, lambda_q1, lambda_k1, lambda_q2, lambda_k2, subln_g, shift_mu, w0, w2, a0, a2,
                        k_k, k_a, r_k, ln_x_g, ln_x_b, final_g)
    return run(x, w_in, w_out, small, 8)
```
